# Optimizing a Trainium2 kernel written in Bass

```python
import jax, jax.numpy as jnp
from jax import lax
import numpy as np

D_MODEL = 2048
BATCH = 1
SEQ = 8192
DEPTH = 2
DEC_BATCH = 16
DEC_SEQ = 16
PAST_LEN = 1024

CHUNK = 64
N_MIXERS = 2
N_ATTN_LAYERS = (DEPTH + 1) // 2
N_CONV_LAYERS = DEPTH // 2
HEAD_DIM = 64
N_HEADS = D_MODEL // HEAD_DIM
N_KV_HEADS = 8
GROUP = N_HEADS // N_KV_HEADS
ATTN_WIDTH = N_HEADS * HEAD_DIM
KV_WIDTH = N_KV_HEADS * HEAD_DIM
WINDOW = 128
WIN_CHUNKS = WINDOW // CHUNK
ROT_DIM = HEAD_DIM // 4
ROPE_THETA = 500000.0
CONV_WIDTH = 31
CONV_CH = D_MODEL
LN_EPS = 1e-5
DEEPNORM_ALPHA = (2.0 * DEPTH) ** 0.25
DEEPNORM_BETA = (8.0 * DEPTH) ** -0.25

kernel_name = 'hybrid_swa_sink_conformer_conv_stream_step'


def layer_norm(x, g, b):
    xf = x.astype(jnp.float32)
    mu = jnp.mean(xf, axis=-1, keepdims=True)
    var = jnp.mean(jnp.square(xf - mu), axis=-1, keepdims=True)
    y = (xf - mu) * lax.rsqrt(var + LN_EPS) * g.astype(jnp.float32) + b.astype(jnp.float32)
    return y.astype(x.dtype)


def partial_rope(x, pos):
    half = ROT_DIM // 2
    inv = ROPE_THETA ** (-jnp.arange(half, dtype=jnp.float32) * (2.0 / ROT_DIM))
    ang = pos.astype(jnp.float32)[:, None] * inv[None, :]
    cos = jnp.cos(ang)[:, None, :]
    sin = jnp.sin(ang)[:, None, :]
    xf = x.astype(jnp.float32)
    x1 = xf[..., :half]
    x2 = xf[..., half:ROT_DIM]
    out = jnp.concatenate([x1 * cos - x2 * sin, x2 * cos + x1 * sin, xf[..., ROT_DIM:]], axis=-1)
    return out.astype(x.dtype)


def sink_probs(s, sink):
    sk = sink.astype(jnp.float32).reshape(N_KV_HEADS, GROUP, 1)
    m = jnp.maximum(jnp.max(s, axis=-1), sk)
    p = jnp.exp(s - m[..., None])
    denom = jnp.sum(p, axis=-1, keepdims=True) + jnp.exp(sk - m)[..., None]
    return p / denom


def attn_project(x, w_in, pos):
    B, T = x.shape[0], x.shape[1]
    h = x @ w_in
    q, k, v, g = jnp.split(h, [ATTN_WIDTH, ATTN_WIDTH + KV_WIDTH, ATTN_WIDTH + 2 * KV_WIDTH], axis=-1)
    q = partial_rope(q.reshape(B, T, N_HEADS, HEAD_DIM), pos)
    k = partial_rope(k.reshape(B, T, N_KV_HEADS, HEAD_DIM), pos)
    v = v.reshape(B, T, N_KV_HEADS, HEAD_DIM)
    return q, k, v, g


def attn_prompt(x, w_in, sink, w_out, wc):
    B, S = x.shape[0], x.shape[1]
    nc = S // CHUNK
    pos = jnp.arange(S, dtype=jnp.float32)
    q, k, v, g = attn_project(x, w_in, pos)
    scale = HEAD_DIM ** -0.5
    qb = q.reshape(B, nc, CHUNK, N_KV_HEADS, GROUP, HEAD_DIM)
    pad = ((0, 0), (WIN_CHUNKS * CHUNK, 0), (0, 0), (0, 0))
    kp = jnp.pad(k, pad).reshape(B, nc + WIN_CHUNKS, CHUNK, N_KV_HEADS, HEAD_DIM)
    vp = jnp.pad(v, pad).reshape(B, nc + WIN_CHUNKS, CHUNK, N_KV_HEADS, HEAD_DIM)
    kb = jnp.concatenate([kp[:, j:j + nc] for j in range(WIN_CHUNKS + 1)], axis=2)
    vb = jnp.concatenate([vp[:, j:j + nc] for j in range(WIN_CHUNKS + 1)], axis=2)
    key_chunk = (jnp.arange(nc)[:, None] - WIN_CHUNKS
                 + jnp.repeat(jnp.arange(WIN_CHUNKS + 1), CHUNK)[None, :])
    valid = key_chunk >= 0
    s = jnp.einsum('bcqkgd,bclkd->bckgql', qb, kb, preferred_element_type=jnp.float32) * scale
    s = jnp.where(valid[None, :, None, None, None, :], s, -jnp.inf)
    p = sink_probs(s, sink)
    o = jnp.einsum('bckgql,bclkd->bcqkgd', p.astype(v.dtype), vb).reshape(B, S, ATTN_WIDTH)
    y = (o * jax.nn.silu(g)) @ w_out
    return y, k[:, S - wc:], v[:, S - wc:]


def attn_sample(x, w_in, sink, w_out, cache_k, cache_v):
    B, T = x.shape[0], x.shape[1]
    wc = cache_k.shape[1]
    pos = PAST_LEN + jnp.arange(T, dtype=jnp.float32)
    q, k, v, g = attn_project(x, w_in, pos)
    scale = HEAD_DIM ** -0.5
    kk = jnp.concatenate([cache_k.astype(k.dtype), k], axis=1)
    vv = jnp.concatenate([cache_v.astype(v.dtype), v], axis=1)
    qh = q.reshape(B, T, N_KV_HEADS, GROUP, HEAD_DIM)
    s = jnp.einsum('btkgd,blkd->bkgtl', qh, kk, preferred_element_type=jnp.float32) * scale
    p = sink_probs(s, sink)
    o = jnp.einsum('bkgtl,blkd->btkgd', p.astype(vv.dtype), vv).reshape(B, T, ATTN_WIDTH)
    y = (o * jax.nn.silu(g)) @ w_out
    L = kk.shape[1]
    return y, kk[:, L - wc:], vv[:, L - wc:]


def conv_project(x, w_in):
    h = x @ w_in
    a, b, g = jnp.split(h, 3, axis=-1)
    return a * jax.nn.sigmoid(b), g


def conv_tail(u_padded, g, w_dw, b_dw, ln_g, ln_b, w_out):
    c = lax.conv_general_dilated(u_padded, w_dw[:, None, :].astype(u_padded.dtype),
                                 window_strides=(1,), padding='VALID',
                                 dimension_numbers=('NWC', 'WIO', 'NWC'),
                                 feature_group_count=CONV_CH)
    c = layer_norm(c + b_dw, ln_g, ln_b)
    return (jax.nn.silu(c) * jax.nn.silu(g)) @ w_out


def conv_prompt(x, w_in, w_dw, b_dw, ln_g, ln_b, w_out):
    u, g = conv_project(x, w_in)
    up = jnp.pad(u, ((0, 0), (CONV_WIDTH - 1, 0), (0, 0)))
    y = conv_tail(up, g, w_dw, b_dw, ln_g, ln_b, w_out)
    return y, up[:, up.shape[1] - (CONV_WIDTH - 1):]


def conv_sample(x, state, w_in, w_dw, b_dw, ln_g, ln_b, w_out):
    u, g = conv_project(x, w_in)
    up = jnp.concatenate([state.astype(u.dtype), u], axis=1)
    y = conv_tail(up, g, w_dw, b_dw, ln_g, ln_b, w_out)
    return y, up[:, up.shape[1] - (CONV_WIDTH - 1):]


def setup_inputs(seed: int = 0) -> dict:
    key = jax.random.key(seed)
    ks = jax.random.split(key, 20)
    f32 = jnp.float32
    wc = min(WINDOW, PAST_LEN)
    nrm = lambda k, shape, s: jax.random.normal(k, shape, f32) * s
    return {
        'x_prompt': nrm(ks[0], (BATCH, SEQ, D_MODEL), 1.0),
        'x_sample': nrm(ks[1], (DEC_BATCH, DEC_SEQ, D_MODEL), 1.0),
        'cache_k': nrm(ks[2], (N_ATTN_LAYERS, DEC_BATCH, wc, N_KV_HEADS, HEAD_DIM), 1.0),
        'cache_v': nrm(ks[3], (N_ATTN_LAYERS, DEC_BATCH, wc, N_KV_HEADS, HEAD_DIM), 1.0),
        'state_conv': nrm(ks[4], (N_CONV_LAYERS, DEC_BATCH, CONV_WIDTH - 1, CONV_CH), 0.5),
        'attn_w_in': nrm(ks[5], (N_ATTN_LAYERS, D_MODEL, 2 * ATTN_WIDTH + 2 * KV_WIDTH), D_MODEL ** -0.5),
        'attn_sink': nrm(ks[6], (N_ATTN_LAYERS, N_HEADS), 0.5),
        'attn_w_out': nrm(ks[7], (N_ATTN_LAYERS, ATTN_WIDTH, D_MODEL), DEEPNORM_BETA * ATTN_WIDTH ** -0.5),
        'conv_w_in': nrm(ks[8], (N_CONV_LAYERS, D_MODEL, 3 * CONV_CH), D_MODEL ** -0.5),
        'conv_w_dw': nrm(ks[9], (N_CONV_LAYERS, CONV_WIDTH, CONV_CH), CONV_WIDTH ** -0.5),
        'conv_b_dw': nrm(ks[10], (N_CONV_LAYERS, CONV_CH), 0.02),
        'conv_ln_g': 1.0 + nrm(ks[11], (N_CONV_LAYERS, CONV_CH), 0.02),
        'conv_ln_b': nrm(ks[12], (N_CONV_LAYERS, CONV_CH), 0.02),
        'conv_w_out': nrm(ks[13], (N_CONV_LAYERS, CONV_CH, D_MODEL), DEEPNORM_BETA * CONV_CH ** -0.5),
        'post_ln_g': 1.0 + nrm(ks[14], (DEPTH, D_MODEL), 0.02),
        'post_ln_b': nrm(ks[15], (DEPTH, D_MODEL), 0.02),
    }


def reference(x_prompt, x_sample, cache_k, cache_v, state_conv,
              attn_w_in, attn_sink, attn_w_out,
              conv_w_in, conv_w_dw, conv_b_dw, conv_ln_g, conv_ln_b, conv_w_out,
              post_ln_g, post_ln_b):
    wc = cache_k.shape[2]
    xp, xs = x_prompt, x_sample
    kp_list, vp_list, ks_list, vs_list, cp_list, cs_list = [], [], [], [], [], []
    for i in range(DEPTH):
        j = i // N_MIXERS
        if i % N_MIXERS == 0:
            yp, kp, vp = attn_prompt(xp, attn_w_in[j], attn_sink[j], attn_w_out[j], wc)
            ys, kn, vn = attn_sample(xs, attn_w_in[j], attn_sink[j], attn_w_out[j], cache_k[j], cache_v[j])
            kp_list.append(kp); vp_list.append(vp); ks_list.append(kn); vs_list.append(vn)
        else:
            yp, cp = conv_prompt(xp, conv_w_in[j], conv_w_dw[j], conv_b_dw[j], conv_ln_g[j], conv_ln_b[j], conv_w_out[j])
            ys, cn = conv_sample(xs, state_conv[j], conv_w_in[j], conv_w_dw[j], conv_b_dw[j], conv_ln_g[j], conv_ln_b[j], conv_w_out[j])
            cp_list.append(cp); cs_list.append(cn)
        xp = layer_norm(DEEPNORM_ALPHA * xp + yp, post_ln_g[i], post_ln_b[i])
        xs = layer_norm(DEEPNORM_ALPHA * xs + ys, post_ln_g[i], post_ln_b[i])
    new_k_prompt = jnp.stack(kp_list)
    new_v_prompt = jnp.stack(vp_list)
    new_k_sample = jnp.stack(ks_list)
    new_v_sample = jnp.stack(vs_list)
    new_conv_prompt = jnp.stack(cp_list)
    new_conv_sample = jnp.stack(cs_list)
    return (xp, xs, new_k_prompt, new_v_prompt, new_k_sample, new_v_sample, new_conv_prompt, new_conv_sample)
```

```python
import numpy as np
import concourse.bass as bass
import concourse.mybir as mybir
from concourse.bass_utils import run_bass_kernel_spmd

F32 = mybir.dt.float32
BF = mybir.dt.bfloat16
AF = mybir.ActivationFunctionType
OP = mybir.AluOpType

NCORES = 8
ALPHA = float((2.0 * 2) ** 0.25)
EPS = 1e-5
NEG = -30000.0
NSLOT = 1280
L1W = 1180
DEBUG_PHASES = 99
OP_LIMIT = 10 ** 9


class Tk:
    __slots__ = ("w", "r", "excl")

    def __init__(self, excl=False):
        self.w = None
        self.r = []
        self.excl = excl


class Sched:
    def __init__(self, nc):
        self.nc = nc
        self.eng = {}
        for name, h in (("pe", nc.tensor), ("act", nc.scalar), ("dve", nc.vector),
                        ("pool", nc.gpsimd), ("sp", nc.sync)):
            self.eng[name] = {"h": h, "sem": nc.alloc_semaphore(name="s_" + name), "cnt": 0,
                              "waited": {}, "dsems": [], "dnext": 0}
        for name, n in (("sp", 8), ("pool", 6), ("act", 4)):
            e = self.eng[name]
            for i in range(n):
                e["dsems"].append([nc.alloc_semaphore(name="d_%s%d" % (name, i)), 0])
        self.out_tokens = []
        self.nops = 0
        self.limit = OP_LIMIT

    def _wait(self, e, tok):
        if tok is None:
            return
        sem, val = tok
        key = id(sem)
        if e["waited"].get(key, 0) >= val:
            return
        e["h"].wait_ge(sem, val)
        e["waited"][key] = val

    def _deps(self, e, R, W):
        need = {}

        def add(tok):
            if tok is None:
                return
            k = id(tok[0])
            if k not in need or need[k][1] < tok[1]:
                need[k] = tok
        for t in R:
            add(t.w)
        for t in W:
            add(t.w)
            for rt in t.r:
                add(rt)
            if len(t.r) > 8:
                mx = {}
                for rt in t.r:
                    k = id(rt[0])
                    if k not in mx or mx[k][1] < rt[1]:
                        mx[k] = rt
                t.r = list(mx.values())
        for tok in need.values():
            self._wait(e, tok)

    def _commit(self, tok, R, W):
        for t in W:
            t.w = tok
            t.r = []
        for t in R:
            t.r.append(tok)

    def op(self, en, fn, R=(), W=()):
        self.nops += 1
        if self.nops > self.limit:
            return None
        e = self.eng[en]
        W = list(W) + [t for t in R if t.excl]
        R = [t for t in R if not t.excl]
        self._deps(e, R, W)
        ins = fn(e["h"])
        e["cnt"] += 1
        ins.then_inc(e["sem"], 1)
        tok = (e["sem"], e["cnt"])
        self._commit(tok, R, W)
        return tok

    def dma(self, en, out, in_, R=(), W=(), is_output=False):
        self.nops += 1
        if self.nops > self.limit:
            return None
        e = self.eng[en]
        self._deps(e, R, W)
        slot = e["dsems"][e["dnext"] % len(e["dsems"])]
        e["dnext"] += 1
        if slot[1] > 0:
            self._wait(e, (slot[0], slot[1]))
        e["h"].dma_start(out=out, in_=in_).then_inc(slot[0], 16)
        slot[1] += 16
        tok = (slot[0], slot[1])
        self._commit(tok, R, W)
        if is_output:
            self.out_tokens.append(tok)
        return tok

    def barrier(self):
        toks = []
        for e in self.eng.values():
            if e["cnt"] > 0:
                toks.append((e["sem"], e["cnt"]))
            for s in e["dsems"]:
                if s[1] > 0:
                    toks.append((s[0], s[1]))
        for e in self.eng.values():
            for t in toks:
                self._wait(e, t)

    def finish(self):
        e = self.eng["sp"]
        for tok in self.out_tokens:
            self._wait(e, tok)
        for name in ("pe", "act", "dve", "pool"):
            o = self.eng[name]
            if o["cnt"] > 0:
                self._wait(e, (o["sem"], o["cnt"]))


def build_nc():
    nc = bass.Bass("TRN2", target_bir_lowering=False)
    S = Sched(nc)

    def din(name, shape):
        return nc.dram_tensor(name, list(shape), F32, kind="ExternalInput").ap()

    def dout(name, shape):
        return nc.dram_tensor(name, list(shape), F32, kind="ExternalOutput").ap()

    xcat = din("xcat", [NSLOT, 2048])
    wina = din("wina", [2048, 5120])
    waout = din("waout", [2048, 2048])
    cwin = din("cwin", [2048, 6144])
    cwout = din("cwout", [2048, 2048])
    cosd = din("cosd", [128, 80])
    sind = din("sind", [128, 80])
    kmAd = din("kmA", [128, 24])
    kmBd = din("kmB", [128, 24])
    hvd = din("hv", [128, 1])
    sinkd = din("sinkl", [128, 16])
    ckd = din("ck", [2, 128, 512])
    cvd = din("cv", [2, 128, 512])
    std = din("st", [2, 30, 2048])
    wdwd = din("wdw", [128, 16 * 31])
    bdwd = din("bdw", [128, 16])
    clgd = din("clg", [128, 16])
    clbd = din("clb", [128, 16])
    plgd = din("plg", [2, 128, 2048])
    plbd = din("plb", [2, 128, 2048])
    identd = din("ident", [128, 128])

    y_p = dout("y_p", [1024, 2048])
    y_s = dout("y_s", [2, 16, 2048])
    nk_p = dout("nk_p", [128, 512])
    nv_p = dout("nv_p", [128, 512])
    nk_s = dout("nk_s", [2, 128, 512])
    nv_s = dout("nv_s", [2, 128, 512])
    nc_p = dout("nc_p", [30, 2048])
    nc_s = dout("nc_s", [2, 30, 2048])
    x1s = nc.dram_tensor("x1s", [1152, 2048], F32, kind="Internal").ap()

    from contextlib import ExitStack
    es_all = ExitStack()

    def sb(name, shape, dt, stack=None):
        return (stack or es_all).enter_context(nc.sbuf_tensor(name, list(shape), dt))

    def ps(name, shape, dt, stack):
        return stack.enter_context(nc.psum_tensor(name, list(shape), dt))

    with es_all:
        B1 = sb("B1", [128, 16, NSLOT], BF)
        B2 = sb("B2", [128, 16, 1184], BF)
        WB = sb("WB", [128, 16, 1536], BF)
        identb = sb("identb", [128, 128], BF)
        ident32 = sb("ident32", [128, 128], F32)
        onesb = sb("onesb", [128, 64], BF)
        ones32 = sb("ones32", [128, 128], F32)
        cosT = sb("cosT", [128, 10, 8], F32)
        sinT = sb("sinT", [128, 10, 8], F32)
        kmA = sb("kmAs", [128, 24], F32)
        kmB = sb("kmBs", [128, 24], F32)
        hv = sb("hvs", [128, 1], F32)
        esk = sb("esk", [128, 16], F32)
        epst = sb("epst", [128, 1], F32)
        wdw = sb("wdws", [128, 16, 31], F32)
        bdw = sb("bdws", [128, 16], F32)
        clg = sb("clgs", [128, 16], F32)
        clb = sb("clbs", [128, 16], F32)

        tC = Tk()
        ctk = []

        def cload(dst, srcap):
            tk = Tk()
            ctk.append(tk)
            S.dma("sp", dst, srcap, W=[tk])
        cload(ident32[:], identd)
        cload(cosT[:].rearrange("p a b -> p (a b)"), cosd)
        cload(sinT[:].rearrange("p a b -> p (a b)"), sind)
        cload(kmA[:], kmAd)
        cload(kmB[:], kmBd)
        cload(hv[:], hvd)
        cload(esk[:], sinkd)
        cload(wdw[:].rearrange("p a b -> p (a b)"), wdwd)
        cload(bdw[:], bdwd)
        cload(clg[:], clgd)
        cload(clb[:], clbd)
        S.op("dve", lambda v: v.tensor_copy(out=identb[:], in_=ident32[:]), R=ctk, W=[tC])
        S.op("dve", lambda v: v.memset(onesb[:], 1.0), W=[tC])
        S.op("dve", lambda v: v.memset(ones32[:], 1.0), W=[tC])
        S.op("dve", lambda v: v.memset(epst[:], EPS), W=[tC])
        S.op("act", lambda a: a.activation(out=esk[:], in_=esk[:], func=AF.Exp), R=[tC], W=[tC])

        tB1 = [Tk() for _ in range(10)]
        tB2 = [Tk() for _ in range(10)]
        tWB = [Tk() for _ in range(4)]
        tKT = [Tk() for _ in range(10)]
        tV = [Tk() for _ in range(10)]
        tSG = Tk()

        def wslot_ap(WB4, s):
            if s < 3:
                return WB[:, :, s * 512:(s + 1) * 512]
            return WB4[:]

        def load_wblock(slot, src_cols_ap):
            S.dma("pool", slot_ap_cur[slot], src_cols_ap.rearrange("(k p) n -> p k n", p=128), W=[tWB[slot]])

        with ExitStack() as esAB:
            KT = sb("KT", [128, 4, NSLOT], BF, esAB)
            V = sb("V", [128, 10, 512], BF, esAB)
            SG = sb("SG", [128, 16, 1152], BF, esAB)
            with ExitStack() as esA:
                slot_ap_cur = [wslot_ap(None, s) for s in range(3)]
                t32 = [sb("t32_%d" % i, [128, 512], F32, esA) for i in range(2)]
                t16 = [sb("t16_%d" % i, [128, 512], BF, esA) for i in range(6)]
                tt32 = [Tk(), Tk()]
                tt16 = [Tk() for _ in range(6)]
                rtmp = sb("rtmp", [128, 4, 64], F32, esA)
                trt = Tk()
                psA = [ps("psA%d" % i, [128, 512], F32, esA) for i in range(3)]
                tpsA = [Tk(True), Tk(True), Tk(True)]
                psQ = [ps("psQ%d" % i, [128, 512], BF, esA) for i in range(3)]
                tpsQ = [Tk(True), Tk(True), Tk(True)]
                esX = ExitStack()
                xb16 = [sb("xb16_%d" % i, [128, 2048], BF, esX) for i in range(2)]
                txb = [Tk(), Tk()]
                psT1 = ps("psT0", [128, 2048], BF, esX)
                psT = [psT1, psT1]
                tpsT1 = Tk(True)
                tpsT = [tpsT1, tpsT1]

                wb_order = list(range(10))
                def wb_src(wb):
                    return wina[:, wb * 512:(wb + 1) * 512]
                xorder = [1, 2, 3, 4, 5, 6, 7, 8, 9, 0]
                sgflat = SG[:].rearrange("p a b -> p (a b)")
                txs = [Tk() for _ in range(10)]

                def xsrc_ap(n_):
                    if n_ < 9:
                        return sgflat[:, n_ * 2048:(n_ + 1) * 2048]
                    return xb16[0][:]
                for n_, i in enumerate(xorder):
                    S.dma("pool", xsrc_ap(n_), xcat[i * 128:(i + 1) * 128, :], W=[txs[n_]])
                    if n_ == 1:
                        load_wblock(0, wb_src(0))
                    if n_ == 6:
                        load_wblock(1, wb_src(1))
                    if n_ == 9:
                        load_wblock(2, wb_src(2))

                def emit_X(n_):
                    i = xorder[n_]
                    xa = xsrc_ap(n_)

                    def tr_x(t):
                        for kk in range(16):
                            ins = t.transpose(out=psT1[:, kk * 128:(kk + 1) * 128],
                                              in_=xa[:, kk * 128:(kk + 1) * 128], identity=identb[:])
                        return ins
                    S.op("pe", tr_x, R=[txs[n_], tC], W=[tpsT1])
                    if n_ % 2 == 0:
                        S.op("act", lambda a: a.activation(
                            out=B1[:, :, i * 128:(i + 1) * 128],
                            in_=psT1[:].rearrange("p (k n) -> p k n", k=16), func=AF.Copy),
                            R=[tpsT1], W=[tB1[i]])
                    else:
                        S.op("dve", lambda v: v.tensor_copy(
                            out=B1[:, :, i * 128:(i + 1) * 128],
                            in_=psT1[:].rearrange("p (k n) -> p k n", k=16)),
                            R=[tpsT1], W=[tB1[i]])

                emit_X(0)
                emit_X(1)
                xnext = [2]
                grp = 0
                pending = []
                qcnt = [0]

                def flush_q(batch):
                    def trq(t):
                        for n_, (i, wb, q16) in enumerate(batch):
                            for c in range(4):
                                ins = t.transpose(out=psQ[n_][:, c * 128:(c + 1) * 128],
                                                  in_=t16[q16][:, c * 128:(c + 1) * 128], identity=identb[:])
                        return ins
                    S.op("pe", trq, R=[tt16[q16] for (_, _, q16) in batch] + [tC],
                         W=[tpsQ[n_] for n_ in range(len(batch))])
                    for n_, (i, wb, q16) in enumerate(batch):
                        if wb < 4:
                            S.op("dve", lambda v, n_=n_, i=i, wb=wb: v.tensor_copy(
                                out=B2[:, wb * 4:(wb + 1) * 4, (i - 1) * 128:i * 128],
                                in_=psQ[n_][:].rearrange("p (c n) -> p c n", c=4)),
                                R=[tpsQ[n_]], W=[tB2[i]])
                        else:
                            S.op("dve", lambda v, n_=n_, i=i: v.tensor_copy(
                                out=KT[:, :, i * 128:(i + 1) * 128],
                                in_=psQ[n_][:].rearrange("p (c n) -> p c n", c=4)),
                                R=[tpsQ[n_]], W=[tKT[i]])

                for wb in range(10):
                    slot = wb % 3
                    Wt = slot_ap_cur[slot]
                    if wb < 6:
                        tiles = list(range(1, 10)) if wb < 4 else list(range(10))
                        for i in tiles:
                            if xnext[0] < 10:
                                emit_X(xnext[0])
                                xnext[0] += 1
                            pa = grp % 3
                            grp += 1

                            def mm(t, i=i, pa=pa, Wt=Wt):
                                for kk in range(16):
                                    ins = t.matmul(psA[pa][:], lhsT=B1[:, kk, i * 128:(i + 1) * 128],
                                                   rhs=Wt[:, kk, :], start=(kk == 0), stop=(kk == 15))
                                return ins
                            S.op("pe", mm, R=[tB1[i], tWB[slot]], W=[tpsA[pa]])
                            if wb == 5:
                                S.op("act", lambda a, i=i, pa=pa: a.activation(out=V[:, i, :], in_=psA[pa][:], func=AF.Copy),
                                     R=[tpsA[pa]], W=[tV[i]])
                                if i >= 8:
                                    b3 = i % 2
                                    S.op("dve", lambda v, pa=pa, b3=b3: v.tensor_copy(out=t32[b3][:], in_=psA[pa][:]),
                                         R=[tpsA[pa]], W=[tt32[b3]])
                                    if i == 8:
                                        S.dma("sp", nv_p[0:64, :], t32[b3][64:128, :], R=[tt32[b3]], is_output=True)
                                    else:
                                        S.dma("sp", nv_p[64:128, :], t32[b3][0:64, :], R=[tt32[b3]], is_output=True)
                                        for b in range(2):
                                            S.dma("sp", nv_s[b, 112:128, :], t32[b3][64 + 32 * b:80 + 32 * b, :],
                                                  R=[tt32[b3]], is_output=True)
                                continue
                            b3 = grp % 2
                            S.op("act", lambda a, pa=pa, b3=b3: a.activation(out=t32[b3][:], in_=psA[pa][:], func=AF.Copy),
                                 R=[tpsA[pa]], W=[tt32[b3]])
                            xv = t32[b3][:].rearrange("p (h d) -> p h d", d=64)
                            x1 = xv[:, :, 0:8]
                            x2 = xv[:, :, 8:16]
                            cb = cosT[:, i, :].unsqueeze(1).broadcast_to([128, 8, 8])
                            sbb = sinT[:, i, :].unsqueeze(1).broadcast_to([128, 8, 8])
                            rv = [rtmp[:, j, :].rearrange("p (h d) -> p h d", d=8) for j in range(4)]

                            def rope1(v, x1=x1, x2=x2, cb=cb, sbb=sbb, rv=rv):
                                v.tensor_tensor(out=rv[0], in0=x1, in1=cb, op=OP.mult)
                                v.tensor_tensor(out=rv[1], in0=x2, in1=sbb, op=OP.mult)
                                v.tensor_tensor(out=rv[2], in0=x2, in1=cb, op=OP.mult)
                                return v.tensor_tensor(out=rv[3], in0=x1, in1=sbb, op=OP.mult)
                            S.op("dve", rope1, R=[tt32[b3], tC], W=[trt])

                            def rope2(v, x1=x1, x2=x2, rv=rv):
                                v.tensor_tensor(out=x1, in0=rv[0], in1=rv[1], op=OP.subtract)
                                return v.tensor_tensor(out=x2, in0=rv[2], in1=rv[3], op=OP.add)
                            S.op("dve", rope2, R=[trt], W=[tt32[b3]])
                            q16 = qcnt[0] % 6
                            qcnt[0] += 1
                            S.op("pool", lambda g, b3=b3, q16=q16: g.tensor_copy(out=t16[q16][:], in_=t32[b3][:]),
                                 R=[tt32[b3]], W=[tt16[q16]])
                            if wb == 4 and i >= 8:
                                if i == 8:
                                    S.dma("sp", nk_p[0:64, :], t32[b3][64:128, :], R=[tt32[b3]], is_output=True)
                                else:
                                    S.dma("sp", nk_p[64:128, :], t32[b3][0:64, :], R=[tt32[b3]], is_output=True)
                                    for b in range(2):
                                        S.dma("sp", nk_s[b, 112:128, :], t32[b3][64 + 32 * b:80 + 32 * b, :],
                                              R=[tt32[b3]], is_output=True)

                            pending.append((i, wb, q16))
                            if len(pending) == 5:
                                flush_q(pending[0:3])
                                del pending[0:3]
                        while pending:
                            flush_q(pending[0:3])
                            del pending[0:3]
                    else:
                        if wb == 6:
                            for tk in txs:
                                tSG.r.extend(tk.r)
                                if tk.w is not None:
                                    tSG.r.append(tk.w)
                        for c in range(4):
                            gt = (wb - 6) * 4 + c
                            for (c0, n) in ((128, 512), (640, 512), (1152, 128)):
                                pa = grp % 3
                                grp += 1
                                rt = [tB1[j] for j in range(c0 // 128, (c0 + n) // 128)]

                                def mmg(t, c=c, c0=c0, n=n, pa=pa, Wt=Wt):
                                    for kk in range(16):
                                        ins = t.matmul(psA[pa][:, 0:n], lhsT=Wt[:, kk, c * 128:(c + 1) * 128],
                                                       rhs=B1[:, kk, c0:c0 + n], start=(kk == 0), stop=(kk == 15))
                                    return ins
                                S.op("pe", mmg, R=rt + [tWB[slot]], W=[tpsA[pa]])
                                S.op("act", lambda a, gt=gt, c0=c0, n=n, pa=pa: a.activation(
                                    out=SG[:, gt, c0 - 128:c0 - 128 + n], in_=psA[pa][:, 0:n], func=AF.Silu),
                                    R=[tpsA[pa]], W=[tSG])
                    if wb + 3 < 10:
                        load_wblock(slot, wb_src(wb + 3))
                esX.close()
            S.barrier()

            if DEBUG_PHASES >= 2:
                with ExitStack() as esB:
                    P1 = [sb("P1_%d" % i, [128, 512], BF, esB) for i in range(2)]
                    P2 = [sb("P2_%d" % i, [128, 512], BF, esB) for i in range(2)]
                    tP1 = [Tk(), Tk()]
                    tP2 = [Tk(), Tk()]
                    Dsb2 = [sb("Dsb%d" % i, [128, 256], F32, esB) for i in range(2)]
                    rD2 = [sb("rD%d" % i, [128, 256], F32, esB) for i in range(2)]
                    og2 = [sb("og%d" % i, [128, 256], F32, esB) for i in range(2)]
                    tDsb2, trD2, tog2 = [Tk(), Tk()], [Tk(), Tk()], [Tk(), Tk()]
                    ck16 = [sb("ck16_%d" % i, [128, 512], BF, esB) for i in range(2)]
                    tck = [Tk(), Tk()]
                    KTc = sb("KTc", [128, 4, 2, 128], BF, esB)
                    Vc = sb("Vc", [128, 2, 512], BF, esB)
                    tKTc = [Tk(), Tk()]
                    tVc = [Tk(), Tk()]
                    pS = [ps("pS_%d" % i, [128, 1024], F32, esB) for i in range(2)]
                    pOD = [ps("pOD_%d" % i, [128, 512], F32, esB) for i in range(4)]
                    tS = [Tk(True), Tk(True)]
                    tOD = [Tk(True) for _ in range(4)]

                    for s in range(3):
                        S.dma("pool", WB[:, :, s * 512:(s + 1) * 512],
                              waout[:, s * 512:(s + 1) * 512].rearrange("(k p) n -> p k n", p=128), W=[tWB[s]])

                    for b in range(2):
                        S.dma("pool", ck16[b][:], ckd[b], W=[tck[b]])
                        S.dma("pool", Vc[:, b, :], cvd[b], W=[tVc[b]])
                        S.dma("sp", nk_s[b, 0:112, :], ckd[b, 16:128, :], is_output=True)
                        S.dma("sp", nv_s[b, 0:112, :], cvd[b, 16:128, :], is_output=True)

                    def prep_caches():
                        for b in range(2):
                            pSb = pS[0][:, 0:256].bitcast(BF)

                            def trc(t, pSb=pSb, b=b):
                                for c in range(4):
                                    ins = t.transpose(out=pSb[:, c * 128:(c + 1) * 128],
                                                      in_=ck16[b][:, c * 128:(c + 1) * 128], identity=identb[:])
                                return ins
                            S.op("pe", trc, R=[tck[b], tC], W=[tS[0]])
                            S.op("dve", lambda v, b=b, pSb=pSb: v.tensor_copy(
                                out=KTc[:, :, b, :], in_=pSb.rearrange("p (c n) -> p c n", c=4)),
                                R=[tS[0]], W=[tKTc[b]])

                    items = []
                    chunks = [("p", j) for j in range(2, 19)] + [("s", 0), ("s", 1)]
                    for (kind, j) in chunks:
                        if kind == "p":
                            nq = 64
                            qc0 = 64 * j - 128
                            qtile = j // 2
                            if j % 2 == 0:
                                i1, i2 = j // 2 - 1, j // 2
                            else:
                                i1, i2 = (j - 1) // 2, (j - 3) // 2
                            K1 = lambda jj, hs, i1=i1: KT[hs, jj, i1 * 128:(i1 + 1) * 128]
                            V1 = lambda cs, i1=i1: V[:, i1, cs]
                            K2 = lambda jj, hs, i2=i2: KT[hs, jj, i2 * 128:(i2 + 1) * 128]
                            V2 = lambda cs, i2=i2: V[:, i2, cs]
                            rdeps = [tKT[i1], tV[i1], tKT[i2], tV[i2], tB2[qtile], tC]
                            bcol = j
                        else:
                            b = j
                            nq = 16
                            qc0 = 1024 + 64 + 32 * b
                            qtile = 9
                            K1 = lambda jj, hs, b=b: KTc[hs, jj, b, :]
                            V1 = lambda cs, b=b: Vc[:, b, cs]
                            K2 = lambda jj, hs: KT[hs, jj, 1152:1280]
                            V2 = lambda cs: V[:, 9, cs]
                            rdeps = [tKTc[b], tVc[b], tKT[9], tV[9], tB2[9], tC]
                            bcol = 19 + b
                        pb = 0
                        for jj in range(4):
                            items.append(dict(nq=nq, qc0=qc0, qtile=qtile, pb=pb, K1=K1, V1=V1, K2=K2, V2=V2,
                                              rdeps=rdeps, bcol=bcol, jj=jj))

                    def emit_S(k):
                        I = items[k]
                        bi = k % 2
                        nq, n4, pb, jj = I["nq"], 4 * I["nq"], I["pb"], I["jj"]

                        def mmS(t):
                            for blk, Kf in ((0, I["K1"]), (1, I["K2"])):
                                for h in range(2):
                                    hs = slice(h * 64, (h + 1) * 64)
                                    q = B2[hs, jj * 4:(jj + 1) * 4, I["qc0"]:I["qc0"] + nq]
                                    c0 = 512 * h + 256 * blk
                                    ins = t.matmul(pS[bi][:, c0:c0 + n4].rearrange("p (g q) -> p g q", g=4),
                                                   lhsT=Kf(jj, hs), rhs=q, start=True, stop=True)
                            return ins
                        S.op("pe", mmS, R=I["rdeps"], W=[tS[bi]])
                        pv = pS[bi][:].rearrange("p (h c) -> p h c", h=2)
                        S.op("act", lambda a: a.activation(
                            out=P1[bi][:, 0:2 * n4].rearrange("p (h c) -> p h c", h=2), in_=pv[:, :, 0:n4], func=AF.Exp,
                            bias=kmA[:, I["bcol"]:I["bcol"] + 1], scale=0.125), R=[tS[bi], tC], W=[tP1[bi]])
                        S.op("act", lambda a: a.activation(
                            out=P2[bi][:, 0:2 * n4].rearrange("p (h c) -> p h c", h=2),
                            in_=pv[:, :, 256:256 + n4], func=AF.Exp,
                            bias=kmB[:, I["bcol"]:I["bcol"] + 1], scale=0.125), R=[tS[bi], tC], W=[tP2[bi]])

                    def emit_PV(k):
                        I = items[k]
                        bi = k % 2
                        b4 = k % 4
                        n4, jj = 4 * I["nq"], I["jj"]

                        def mmPV(t):
                            for h in range(2):
                                hs = slice(h * 64, (h + 1) * 64)
                                cs = slice(jj * 128 + h * 64, jj * 128 + (h + 1) * 64)
                                p1 = P1[bi][:, h * n4:(h + 1) * n4]
                                p2 = P2[bi][:, h * n4:(h + 1) * n4]
                                t.matmul(pOD[b4][hs, 0:n4], lhsT=I["V1"](cs), rhs=p1, start=True, stop=False)
                                t.matmul(pOD[b4][hs, 0:n4], lhsT=I["V2"](cs), rhs=p2, start=False, stop=True)
                                t.matmul(pOD[b4][hs, 256:256 + n4], lhsT=onesb[:, 0:64], rhs=p1, start=True, stop=False)
                                ins = t.matmul(pOD[b4][hs, 256:256 + n4], lhsT=onesb[:, 0:64], rhs=p2,
                                               start=False, stop=True)
                            return ins
                        S.op("pe", mmPV, R=I["rdeps"] + [tP1[bi], tP2[bi]], W=[tOD[b4]])

                    def emit_N(k):
                        I = items[k]
                        bi = k % 2
                        b4 = k % 4
                        nq, n4, jj, qc0 = I["nq"], 4 * I["nq"], I["jj"], I["qc0"]
                        Dsb, rD, og = Dsb2[bi], rD2[bi], og2[bi]
                        tDsb, trD, tog = tDsb2[bi], trD2[bi], tog2[bi]
                        esb = esk[:, jj * 4:(jj + 1) * 4].unsqueeze(2).broadcast_to([128, 4, nq])
                        S.op("dve", lambda v: v.tensor_tensor(
                            out=Dsb[:, 0:n4].rearrange("p (g q) -> p g q", g=4),
                            in0=pOD[b4][:, 256:256 + n4].rearrange("p (g q) -> p g q", g=4), in1=esb, op=OP.add),
                            R=[tOD[b4], tC], W=[tDsb])
                        if False:
                            S.op("dve", lambda v: v.reciprocal(out=rD[:, 0:n4], in_=Dsb[:, 0:n4]), R=[tDsb], W=[trD])
                        else:
                            S.op("act", lambda a: a.activation(out=rD[:, 0:n4], in_=Dsb[:, 0:n4], func=AF.Ln),
                                 R=[tDsb], W=[trD])
                            S.op("act", lambda a: a.activation(out=rD[:, 0:n4], in_=rD[:, 0:n4], func=AF.Exp,
                                                               scale=-1.0), W=[trD])
                        S.op("dve", lambda v: v.tensor_tensor(
                            out=og[:, 0:n4], in0=pOD[b4][:, 0:n4], in1=rD[:, 0:n4], op=OP.mult),
                            R=[tOD[b4], trD], W=[tog])
                        S.op("pool", lambda g: g.tensor_tensor(
                            out=B1[:, jj * 4:(jj + 1) * 4, 128 + qc0:128 + qc0 + nq],
                            in0=og[:, 0:n4].rearrange("p (g q) -> p g q", g=4),
                            in1=SG[:, jj * 4:(jj + 1) * 4, qc0:qc0 + nq], op=OP.mult),
                            R=[tog, tSG], W=[tB1[I["qtile"]]])

                    emit_S(0)
                    for k in range(len(items)):
                        if k == 40:
                            prep_caches()
                        if k + 1 < len(items):
                            emit_S(k + 1)
                        emit_PV(k)
                        if k >= 1:
                            emit_N(k - 1)
                    emit_N(len(items) - 1)
        S.barrier()

        tX1T = Tk()
        if DEBUG_PHASES >= 3:
            with ExitStack() as esC:
                W4 = sb("W4", [128, 16, 512], BF, esC)
                S.dma("pool", W4[:], waout[:, 1536:2048].rearrange("(k p) n -> p k n", p=128), W=[tWB[3]])
                Wc = [WB[:, :, 0:512], WB[:, :, 512:1024], WB[:, :, 1024:1536], W4[:]]
                lnG = sb("lnG", [128, 2048], F32, esC)
                lnB = sb("lnB", [128, 2048], F32, esC)
                tLN = Tk()
                S.dma("sp", lnG[:], plgd[0], W=[tLN])
                S.dma("sp", lnB[:], plbd[0], W=[tLN])
                x32 = [sb("x32_%d" % i, [128, 2048], F32, esC) for i in range(2)]
                z32 = [sb("z32_%d" % i, [128, 2048], F32, esC) for i in range(2)]
                x1b = [sb("x1b%d" % i, [128, 2048], BF, esC) for i in range(2)]
                tx32 = [Tk(), Tk()]
                tz32 = [Tk(), Tk()]
                tx1b = [Tk(), Tk()]
                stt = sb("stt", [128, 4, 6], F32, esC)
                mv = sb("mv", [128, 2], F32, esC)
                sd = sb("sd", [128, 1], F32, esC)
                rstd = sb("rstd", [128, 1], F32, esC)
                tst = Tk()
                pC = [ps("pC%d" % i, [128, 512], F32, esC) for i in range(4)]
                tpC = [Tk(True) for _ in range(4)]
                pT = ps("pTc", [128, 2048], BF, esC)
                tpT = Tk(True)
                def xloadC(i, dst, tk):
                    S.dma("sp", dst[:], xcat[i * 128:(i + 1) * 128, :], W=[tk])

                def postC(i, bi, when):
                    if when == "early":
                        S.dma("sp", x1s[(i - 1) * 128:i * 128, :], z32[bi][:], R=[tz32[bi]], W=[tX1S[i]])
                        S.op("act", lambda a: a.activation(out=x1b[i % 2][:], in_=z32[bi][:], func=AF.Copy),
                             R=[tz32[bi]], W=[tx1b[i % 2]])
                        return

                    def trx(t):
                        for kk in range(16):
                            ins = t.transpose(out=pT[:, kk * 128:(kk + 1) * 128],
                                              in_=x1b[i % 2][:, kk * 128:(kk + 1) * 128], identity=identb[:])
                        return ins
                    S.op("pe", trx, R=[tx1b[i % 2], tC], W=[tpT])
                    pv = pT[:].rearrange("p (k n) -> p k n", k=16)
                    if i < 9:
                        S.op("act", lambda a: a.activation(out=B2[:, :, (i - 1) * 128:i * 128], in_=pv, func=AF.Copy),
                             R=[tpT], W=[tX1T])
                    else:
                        def ev9(a):
                            a.activation(out=B2[:, :, 1024:1088], in_=pv[:, :, 0:64], func=AF.Copy)
                            a.activation(out=B2[:, :, 1118:1134], in_=pv[:, :, 64:80], func=AF.Copy)
                            return a.activation(out=B2[:, :, 1164:1180], in_=pv[:, :, 96:112], func=AF.Copy)
                        S.op("act", ev9, R=[tpT], W=[tX1T])
                for tk in tB2:
                    tX1T.r.extend(tk.r)
                    if tk.w is not None:
                        tX1T.r.append(tk.w)
                S.op("dve", lambda v: v.memset(B2[:, :, 1088:1184], 0.0), W=[tX1T])
                layer_tail(nc, S, tiles=[(i, i * 128, 128) for i in range(1, 10)], actT=B1,
                           tact=lambda i: [tB1[i]], Wc=Wc, tW=tWB, xload=xloadC, x32=x32, tx32=tx32,
                           z32=z32, tz32=tz32, pC=pC, tpC=tpC, lnG=lnG, lnB=lnB, tLN=tLN, stt=stt, mv=mv,
                           sd=sd, rstd=rstd, tst=tst, epst=epst, tC=tC, post=postC)
        if DEBUG_PHASES >= 4:
            phase_D(nc, S, sb, ps, locals())
        S.finish()
    print('total ops', S.nops)
    return nc


def layer_tail(nc, S, tiles, actT, tact, Wc, tW, xload, x32, tx32, z32, tz32, pC, tpC, lnG, lnB, tLN,
               stt, mv, sd, rstd, tst, epst, tC, post):
    pending = None
    for n_, (i, c0, m) in enumerate(tiles):
        bi = n_ % 2
        xload(i, x32[bi], tx32[bi])
        for cg in range(4):
            def mm(t, cg=cg, c0=c0, m=m):
                for kk in range(16):
                    ins = t.matmul(pC[cg][0:m, :], lhsT=actT[:, kk, c0:c0 + m], rhs=Wc[cg][:, kk, :],
                                   start=(kk == 0), stop=(kk == 15))
                return ins
            S.op("pe", mm, R=tact(i) + [tW[cg]], W=[tpC[cg]])
            S.op("dve", lambda v, cg=cg, bi=bi, m=m: v.scalar_tensor_tensor(
                out=z32[bi][0:m, cg * 512:(cg + 1) * 512], in0=x32[bi][0:m, cg * 512:(cg + 1) * 512],
                scalar=ALPHA, in1=pC[cg][0:m, :], op0=OP.mult, op1=OP.add),
                R=[tpC[cg], tx32[bi]], W=[tz32[bi]])
        if pending is not None:
            pending()
            pending = None

        def stats(v, bi=bi, m=m):
            for cg in range(4):
                ins = v.bn_stats(out=stt[0:m, cg, :], in_=z32[bi][0:m, cg * 512:(cg + 1) * 512])
            return ins
        S.op("dve", stats, R=[tz32[bi]], W=[tst])
        S.op("dve", lambda v, m=m: v.bn_aggr(out=mv[0:m, :], in_=stt[0:m, :, :].rearrange("p a b -> p (a b)")),
             W=[tst])
        S.op("act", lambda a, m=m: a.activation(out=sd[0:m, :], in_=mv[0:m, 1:2], func=AF.Sqrt,
                                                bias=epst[0:m, :], scale=1.0), R=[tC], W=[tst])
        S.op("dve", lambda v, m=m: v.reciprocal(out=rstd[0:m, :], in_=sd[0:m, :]), W=[tst])
        S.op("dve", lambda v, bi=bi, m=m: v.tensor_scalar(
            out=z32[bi][0:m, :], in0=z32[bi][0:m, :], scalar1=mv[0:m, 0:1], scalar2=rstd[0:m, 0:1],
            op0=OP.subtract, op1=OP.mult), R=[tst], W=[tz32[bi]])
        S.op("pool", lambda g, bi=bi, m=m: g.tensor_tensor(out=z32[bi][0:m, :], in0=z32[bi][0:m, :],
                                                          in1=lnG[0:m, :], op=OP.mult), R=[tLN], W=[tz32[bi]])
        S.op("pool", lambda g, bi=bi, m=m: g.tensor_tensor(out=z32[bi][0:m, :], in0=z32[bi][0:m, :],
                                                          in1=lnB[0:m, :], op=OP.add), R=[tLN], W=[tz32[bi]])
        pending = (lambda i=i, bi=bi: (post(i, bi, "early"), post(i, bi, "late")))
    if pending is not None:
        pending()


tX1S = [Tk() for _ in range(11)]


def phase_D(nc, S, sb, ps, L):
    from contextlib import ExitStack
    B1, B2, WB = L["B1"], L["B2"], L["WB"]
    tB1, tWB, tX1T, tC = L["tB1"], L["tWB"], L["tX1T"], L["tC"]
    identb, ident32, ones32 = L["identb"], L["ident32"], L["ones32"]
    wdw, bdw, clg, clb, hv, epst = L["wdw"], L["bdw"], L["clg"], L["clb"], L["hv"], L["epst"]
    cwin, cwout, std, nc_p, nc_s, y_p, y_s, x1s = (L["cwin"], L["cwout"], L["std"], L["nc_p"], L["nc_s"],
                                                   L["y_p"], L["y_s"], L["x1s"])
    plgd, plbd = L["plgd"], L["plbd"]
    tSGc = [Tk() for _ in range(16)]
    tU = [Tk() for _ in range(16)]
    tCc = [Tk() for _ in range(16)]
    tWs = [Tk() for _ in range(4)]
    tUT = Tk()
    blocks = ((34, 512), (546, 512), (1058, 122))
    oblocks = ((64, 512), (576, 512), (1088, 92))
    NOUT = 1116

    def wsl(s):
        return WB[:, :, s * 384:(s + 1) * 384]

    def load_cw(ct):
        s = ct % 4
        S.dma("pool", wsl(s)[:, :, 0:256], cwin[:, ct * 384:ct * 384 + 256].rearrange("(k p) n -> p k n", p=128),
              W=[tWs[s]])

    for s in range(4):
        tWs[s].w = None
    for s in range(3):
        for ws in tWs:
            ws.r.extend(tWB[s].r)
            if tWB[s].w is not None:
                ws.r.append(tWB[s].w)
    for ct in range(4):
        load_cw(ct)
    S.barrier()
    with ExitStack() as esD:
        utail = sb("utail", [128, 16, 62], F32, esD)
        esU = ExitStack()
        U16 = sb("U16", [128, 16, L1W], BF, esU)
        with ExitStack() as esD1:
            for ct in range(16):
                for tk in tB1:
                    tSGc[ct].r.extend(tk.r)
                    if tk.w is not None:
                        tSGc[ct].r.append(tk.w)
            stT = sb("stT", [128, 16, 2, 30], F32, esD1)
            st32 = [sb("st32_%d" % i, [30, 2048], F32, esD1) for i in range(2)]
            tst32, tstT = [Tk(), Tk()], Tk()
            for b in range(2):
                S.dma("sp", st32[b][:], std[b], W=[tst32[b]])
                S.dma("sp", nc_s[b, 0:14, :], std[b, 16:30, :], is_output=True)
            sig32 = [sb("sig32_%d" % i, [128, 512], F32, esD1) for i in range(2)]
            tsig = [Tk(), Tk()]
            pa = [ps("pa%d" % i, [128, 512], F32, esD1) for i in range(2)]
            pb_ = [ps("pb%d" % i, [128, 512], F32, esD1) for i in range(2)]
            tpa, tpb = [Tk(True), Tk(True)], [Tk(True), Tk(True)]
            pst = ps("pst", [128, 512], F32, esD1)
            tpst = Tk(True)
            def prep_state():
                for b in range(2):
                    def trs(t, b=b):
                        for ct in range(16):
                            ins = t.transpose(out=pst[:, ct * 30:(ct + 1) * 30],
                                              in_=st32[b][:, ct * 128:(ct + 1) * 128], identity=ident32[0:30, 0:30])
                        return ins
                    S.op("pe", trs, R=[tst32[b], tC], W=[tpst])
                    S.op("dve", lambda v, b=b: v.tensor_copy(out=stT[:, :, b, :],
                                                             in_=pst[:, 0:480].rearrange("p (c n) -> p c n", c=16)),
                         R=[tpst], W=[tstT])
            it = 0
            for ct in range(16):
                s = ct % 4
                Wt = wsl(s)
                for (c0, n) in blocks:
                    bi = it % 2
                    it += 1
                    for part, (pp, tp) in enumerate(((pa, tpa), (pb_, tpb))):
                        def mm(t, part=part, pp=pp, bi=bi, c0=c0, n=n, Wt=Wt):
                            for kk in range(16):
                                ins = t.matmul(pp[bi][:, 0:n], lhsT=Wt[:, kk, part * 128:(part + 1) * 128],
                                               rhs=B2[:, kk, c0:c0 + n], start=(kk == 0), stop=(kk == 15))
                            return ins
                        S.op("pe", mm, R=[tX1T, tWs[s]], W=[tp[bi]])
                    S.op("act", lambda a, bi=bi, n=n: a.activation(out=sig32[bi][:, 0:n], in_=pb_[bi][:, 0:n],
                                                                   func=AF.Sigmoid), R=[tpb[bi]], W=[tsig[bi]])
                    S.op("dve", lambda v, bi=bi, n=n, c0=c0, ct=ct: v.tensor_tensor(
                        out=U16[:, ct, c0:c0 + n], in0=pa[bi][:, 0:n], in1=sig32[bi][:, 0:n], op=OP.mult),
                        R=[tpa[bi], tsig[bi]], W=[tU[ct]])
                    if c0 == 1058:
                        def tails(v, bi=bi, ct=ct):
                            v.tensor_tensor(out=utail[:, ct, 0:30], in0=pa[bi][:, 0:30], in1=sig32[bi][:, 0:30], op=OP.mult)
                            v.tensor_tensor(out=utail[:, ct, 30:46], in0=pa[bi][:, 60:76], in1=sig32[bi][:, 60:76], op=OP.mult)
                            return v.tensor_tensor(out=utail[:, ct, 46:62], in0=pa[bi][:, 106:122],
                                                   in1=sig32[bi][:, 106:122], op=OP.mult)
                        S.op("dve", tails, R=[tpa[bi], tsig[bi]], W=[tUT])
                if ct == 0:
                    prep_state()
                def fix(g, ct=ct):
                    g.tensor_copy(out=U16[:, ct, 1088:1118], in_=stT[:, ct, 0, :])
                    g.tensor_copy(out=U16[:, ct, 1134:1164], in_=stT[:, ct, 1, :])
                    return g.tensor_scalar(out=U16[:, ct, 34:64], in0=U16[:, ct, 34:64], scalar1=hv[:, 0:1],
                                           scalar2=1.0, op0=OP.mult, op1=OP.mult)
                S.op("pool", fix, R=[tstT, tC], W=[tU[ct]])
                if ct + 4 < 16:
                    load_cw(ct + 4)
        S.barrier()
        with ExitStack() as esD2:
            tWo = [Tk() for _ in range(4)]
            for s in range(3):
                for ws in tWs:
                    tWo[s].r.extend(ws.r)
                S.dma("pool", WB[:, :, s * 512:(s + 1) * 512],
                      cwout[:, s * 512:(s + 1) * 512].rearrange("(k p) n -> p k n", p=128), W=[tWo[s]])
            S1 = sb("S1", [128, NOUT], F32, esD2)
            S2 = sb("S2", [128, NOUT], F32, esD2)
            tS1, tS2 = Tk(), Tk()
            esD2i = ExitStack()
            diag = [sb("diag%d" % i, [128, 31, 128], BF, esD2i) for i in range(2)]
            tdg = [Tk(), Tk()]
            sq = [sb("sq%d" % i, [128, 512], F32, esD2i) for i in range(2)]
            tsq = [Tk(), Tk()]
            S.op("dve", lambda v: v.memset(S1[:], 0.0), W=[tS1])
            S.op("dve", lambda v: v.memset(S2[:], 0.0), W=[tS2])
            with ExitStack() as esD2p:
                pc = [ps("pc%d" % i, [128, 512], F32, esD2p) for i in range(6)]
                tpc = [Tk(True) for _ in range(6)]
                it = 0
                for ct in range(16):
                    d = ct % 2

                    def mkdiag(g, ct=ct, d=d):
                        for tap in range(31):
                            ins = g.tensor_scalar(out=diag[d][:, tap, :], in0=identb[:],
                                                  scalar1=wdw[:, ct, tap:tap + 1], scalar2=1.0, op0=OP.mult,
                                                  op1=OP.mult)
                        return ins
                    S.op("dve" if ct == 0 else "pool", mkdiag, R=[tC], W=[tdg[d]])
                    for (o0, n) in reversed(oblocks):
                        bi = it % 6
                        sqi = it % 2
                        it += 1

                        def mmc(t, ct=ct, d=d, o0=o0, n=n, bi=bi):
                            for tap in range(31):
                                ins = t.matmul(pc[bi][:, 0:n], lhsT=diag[d][:, tap, :],
                                               rhs=U16[:, ct, o0 + tap - 30:o0 + tap - 30 + n],
                                               start=(tap == 0), stop=(tap == 30))
                            return ins
                        S.op("pe", mmc, R=[tdg[d], tU[ct]], W=[tpc[bi]])
                        S.op("act", lambda a, ct=ct, o0=o0, n=n, bi=bi: a.activation(
                            out=U16[:, ct, o0:o0 + n], in_=pc[bi][:, 0:n], func=AF.Identity,
                            bias=bdw[:, ct:ct + 1], scale=1.0), R=[tpc[bi], tC], W=[tCc[ct]])
                        S.op("act", lambda a, ct=ct, n=n, bi=bi, sqi=sqi: a.activation(
                            out=sq[sqi][:, 0:n], in_=pc[bi][:, 0:n], func=AF.Square,
                            bias=bdw[:, ct:ct + 1], scale=1.0), R=[tpc[bi], tC], W=[tsq[sqi]])
                        S.op("dve", lambda v, ct=ct, o0=o0, n=n, bi=bi: v.scalar_tensor_tensor(
                            out=S1[:, o0 - 64:o0 - 64 + n], in0=pc[bi][:, 0:n], scalar=bdw[:, ct:ct + 1],
                            in1=S1[:, o0 - 64:o0 - 64 + n], op0=OP.add, op1=OP.add), R=[tpc[bi], tC], W=[tS1])
                        S.op("dve", lambda v, o0=o0, n=n, sqi=sqi: v.tensor_tensor(
                            out=S2[:, o0 - 64:o0 - 64 + n], in0=S2[:, o0 - 64:o0 - 64 + n], in1=sq[sqi][:, 0:n],
                            op=OP.add), R=[tsq[sqi]], W=[tS2])
            esD2i.close()
            S.barrier()
            with ExitStack() as esD3:
                tt = [sb("ttm%d" % i, [128, NOUT], F32, esD3) for i in range(2)]
                ttt = [Tk(), Tk()]
                pm = [ps("pm%d" % i, [128, 512], F32, esD3) for i in range(6)]
                tpm = Tk(True)
                GW = [sb("GW%d" % i, [128, 16, 128], BF, esD3) for i in range(2)]
                tGW = [Tk(), Tk()]
                pg = [ps("pg%d" % i, [128, 512], F32, esD3) for i in range(2)]
                tpg = [Tk(True), Tk(True)]

                def load_gw(ct):
                    S.dma("pool", GW[ct % 2][:],
                          cwin[:, ct * 384 + 256:ct * 384 + 384].rearrange("(k p) n -> p k n", p=128), W=[tGW[ct % 2]])

                gcnt = [0]

                def gate_proj(ct):
                    for (c0, n) in blocks:
                        bi = gcnt[0] % 2
                        gcnt[0] += 1

                        def mmg(t, bi=bi, c0=c0, n=n, ct=ct):
                            for kk in range(16):
                                ins = t.matmul(pg[bi][:, 0:n], lhsT=GW[ct % 2][:, kk, :], rhs=B2[:, kk, c0:c0 + n],
                                               start=(kk == 0), stop=(kk == 15))
                            return ins
                        S.op("pe", mmg, R=[tX1T, tGW[ct % 2]], W=[tpg[bi]])
                        S.op("act", lambda a, bi=bi, n=n, c0=c0, ct=ct: a.activation(
                            out=B1[:, ct, c0:c0 + n], in_=pg[bi][:, 0:n], func=AF.Silu),
                            R=[tpg[bi]], W=[tSGc[ct]])
                    if ct + 2 < 16:
                        load_gw(ct + 2)
                load_gw(0)
                load_gw(1)
                for ct in range(4):
                    gate_proj(ct)

                def mms(t):
                    for k_, (o0, n) in enumerate(oblocks):
                        t.matmul(pm[k_][:, 0:n], lhsT=ones32[:], rhs=S1[:, o0 - 64:o0 - 64 + n], start=True, stop=True)
                        ins = t.matmul(pm[3 + k_][:, 0:n], lhsT=ones32[:], rhs=S2[:, o0 - 64:o0 - 64 + n],
                                       start=True, stop=True)
                    return ins
                S.op("pe", mms, R=[tS1, tS2, tC], W=[tpm])

                def st1(v):
                    for k_, (o0, n) in enumerate(oblocks):
                        sl = slice(o0 - 64, o0 - 64 + n)
                        v.tensor_scalar(out=S1[:, sl], in0=pm[k_][:, 0:n], scalar1=1.0 / 2048, scalar2=None, op0=OP.mult)
                        ins = v.tensor_scalar(out=S2[:, sl], in0=pm[3 + k_][:, 0:n], scalar1=1.0 / 2048, scalar2=None,
                                              op0=OP.mult)
                    return ins
                S.op("dve", st1, R=[tpm], W=[tS1, tS2])
                S.op("dve", lambda v: v.tensor_tensor(out=tt[0][:], in0=S1[:], in1=S1[:], op=OP.mult),
                     R=[tS1], W=[ttt[0]])
                S.op("dve", lambda v: v.tensor_tensor(out=S2[:], in0=S2[:], in1=tt[0][:], op=OP.subtract),
                     R=[ttt[0]], W=[tS2])
                S.op("act", lambda a: a.activation(out=S2[:], in_=S2[:], func=AF.Ln, bias=epst[:, 0:1], scale=1.0),
                     R=[tC], W=[tS2])
                S.op("act", lambda a: a.activation(out=tt[0][:], in_=S2[:], func=AF.Exp, scale=-0.5),
                     R=[tS2], W=[ttt[0]])
                S.op("dve", lambda v: v.tensor_tensor(out=S2[:], in0=S1[:], in1=tt[0][:], op=OP.mult),
                     R=[tS1, ttt[0]], W=[tS2])
                tpr = Tk()

                def wr_ps(v):
                    for k_, (o0, n) in enumerate(oblocks):
                        sl = slice(o0 - 64, o0 - 64 + n)
                        v.tensor_copy(out=pm[k_][:, 0:n], in_=tt[0][:, sl])
                        ins = v.tensor_copy(out=pm[3 + k_][:, 0:n], in_=S2[:, sl])
                    return ins
                S.op("dve", wr_ps, R=[ttt[0], tS2], W=[tpm, tpr])
                for ct in range(16):
                    bi = ct % 2
                    if ct + 4 < 16:
                        gate_proj(ct + 4)

                    def a1(v, ct=ct, bi=bi):
                        for k_, (o0, n) in enumerate(oblocks):
                            sl = slice(o0 - 64, o0 - 64 + n)
                            ins = v.tensor_tensor(out=tt[bi][:, sl], in0=U16[:, ct, o0:o0 + n], in1=pm[k_][:, 0:n],
                                                  op=OP.mult)
                        return ins
                    S.op("dve", a1, R=[tCc[ct], tpr], W=[ttt[bi]])

                    def a2(v, bi=bi):
                        for k_, (o0, n) in enumerate(oblocks):
                            sl = slice(o0 - 64, o0 - 64 + n)
                            ins = v.tensor_tensor(out=tt[bi][:, sl], in0=tt[bi][:, sl], in1=pm[3 + k_][:, 0:n],
                                                  op=OP.subtract)
                        return ins
                    S.op("dve", a2, R=[tpr], W=[ttt[bi]])
                    S.op("act", lambda a, ct=ct, bi=bi: a.activation(
                        out=tt[bi][:], in_=tt[bi][:], func=AF.Silu, bias=clb[:, ct:ct + 1], scale=clg[:, ct:ct + 1]),
                        R=[tC], W=[ttt[bi]])
                    S.op("pool", lambda g, ct=ct, bi=bi: g.tensor_tensor(
                        out=B1[:, ct, 64:64 + NOUT], in0=tt[bi][:], in1=B1[:, ct, 64:64 + NOUT], op=OP.mult),
                        R=[ttt[bi]], W=[tSGc[ct]])
        esU.close()
        S.barrier()
        with ExitStack() as esE:
            W4 = sb("W4e", [128, 16, 512], BF, esE)
            S.dma("pool", W4[:], cwout[:, 1536:2048].rearrange("(k p) n -> p k n", p=128), W=[tWo[3]])
            Wc = [WB[:, :, 0:512], WB[:, :, 512:1024], WB[:, :, 1024:1536], W4[:]]
            lnG = sb("lnGe", [128, 2048], F32, esE)
            lnB = sb("lnBe", [128, 2048], F32, esE)
            tLN = Tk()
            S.dma("sp", lnG[:], plgd[1], W=[tLN])
            S.dma("sp", lnB[:], plbd[1], W=[tLN])
            x32 = [sb("x32e_%d" % i, [128, 2048], F32, esE) for i in range(2)]
            z32 = [sb("z32e_%d" % i, [128, 2048], F32, esE) for i in range(2)]
            tx32 = [Tk(), Tk()]
            tz32 = [Tk(), Tk()]
            stt = sb("stte", [128, 4, 6], F32, esE)
            mv = sb("mve", [128, 2], F32, esE)
            sd = sb("sde", [128, 1], F32, esE)
            rstd = sb("rstde", [128, 1], F32, esE)
            tst = Tk()
            pC = [ps("pCe%d" % i, [128, 512], F32, esE) for i in range(4)]
            tpC = [Tk(True) for _ in range(4)]
            pU = ps("pU", [32, 2048], F32, esE)
            tpU = Tk(True)
            ut_sb = sb("ut_sb", [32, 2048], F32, esE)
            tut = Tk()
            for (c0, n, dst) in ((0, 30, nc_p[:, :]), (30, 16, nc_s[0, 14:30, :]), (46, 16, nc_s[1, 14:30, :])):
                def tru(t, c0=c0, n=n):
                    for ct in range(16):
                        ins = t.transpose(out=pU[0:n, ct * 128:(ct + 1) * 128], in_=utail[:, ct, c0:c0 + n],
                                          identity=ident32[:])
                    return ins
                S.op("pe", tru, R=[tUT, tC], W=[tpU])
                S.op("dve", lambda v, n=n: v.tensor_copy(out=ut_sb[0:n, :], in_=pU[0:n, :]), R=[tpU], W=[tut])
                S.dma("sp", dst, ut_sb[0:n, :], R=[tut], is_output=True)

            tiles = [(m, 64 + 128 * m, 128) for m in range(8)] + [(8, 1118, 62)]

            def xloadE(m, dst, tk):
                if m < 8:
                    S.dma("sp", dst[:], x1s[128 * m + 64:128 * m + 192, :], W=[tk])
                else:
                    S.dma("sp", dst[0:16, :], x1s[1088:1104, :], W=[tk])
                    S.dma("sp", dst[46:62, :], x1s[1120:1136, :], W=[tk])

            def postE(m, bi, when):
                if when == "late":
                    return
                if m < 8:
                    S.dma("sp", y_p[128 * m:128 * (m + 1), :], z32[bi][:], R=[tz32[bi]], is_output=True)
                else:
                    S.dma("sp", y_s[0], z32[bi][0:16, :], R=[tz32[bi]], is_output=True)
                    S.dma("sp", y_s[1], z32[bi][46:62, :], R=[tz32[bi]], is_output=True)
            for tk in tX1S:
                if tk.w is not None:
                    for t_ in tx32:
                        t_.r.append(tk.w)
            layer_tail(nc, S, tiles=tiles, actT=B1, tact=lambda m: list(tSGc), Wc=Wc, tW=tWo, xload=xloadE,
                       x32=x32, tx32=tx32, z32=z32, tz32=tz32, pC=pC, tpC=tpC, lnG=lnG, lnB=lnB, tLN=tLN,
                       stt=stt, mv=mv, sd=sd, rstd=rstd, tst=tst, epst=epst, tC=tC, post=postE)


class _AllTk:
    def __init__(self, tks):
        self.tks = tks

    @property
    def w(self):
        return None

    @property
    def r(self):
        return _Sink()


class _Sink:
    def append(self, x):
        pass


def _perm_q():
    perm = np.zeros(2048, dtype=np.int64)
    for j in range(4):
        for g in range(4):
            for half in range(2):
                for d in range(64):
                    perm[(j * 4 + g) * 128 + half * 64 + d] = ((2 * j + half) * 4 + g) * 64 + d
    return perm


_NC_CACHE = {}


def kernel(x_prompt, x_sample, cache_k, cache_v, state_conv, attn_w_in, attn_sink, attn_w_out,
           conv_w_in, conv_w_dw, conv_b_dw, conv_ln_g, conv_ln_b, conv_w_out, post_ln_g, post_ln_b):
    f = np.float32
    x_prompt = np.asarray(x_prompt, f)
    x_sample = np.asarray(x_sample, f)
    cache_k = np.asarray(cache_k, f)
    cache_v = np.asarray(cache_v, f)
    state_conv = np.asarray(state_conv, f)
    perm = _perm_q()
    w_in = np.asarray(attn_w_in, f)[0]
    wina = np.ascontiguousarray(np.concatenate(
        [w_in[:, perm], w_in[:, 2048:3072], w_in[:, 3072 + perm]], axis=1))
    waout = np.ascontiguousarray(np.asarray(attn_w_out, f)[0][perm, :])
    cw = np.asarray(conv_w_in, f)[0]
    cwin = np.ascontiguousarray(cw.reshape(2048, 3, 16, 128).transpose(0, 2, 1, 3).reshape(2048, 6144))
    cwout = np.ascontiguousarray(np.asarray(conv_w_out, f)[0])
    sink = np.asarray(attn_sink, f)[0]
    sinkl = np.zeros((128, 16), f)
    for p in range(128):
        for j in range(4):
            for g in range(4):
                sinkl[p, j * 4 + g] = sink[(2 * j + p // 64) * 4 + g]
    wdw = np.ascontiguousarray(np.asarray(conv_w_dw, f)[0].reshape(31, 16, 128).transpose(2, 1, 0).reshape(128, 16 * 31))
    lay = lambda v: np.ascontiguousarray(np.asarray(v, f)[0].reshape(16, 128).T)
    bdw, clg, clb = lay(conv_b_dw), lay(conv_ln_g), lay(conv_ln_b)
    plg = np.ascontiguousarray(np.broadcast_to(np.asarray(post_ln_g, f)[:, None, :], (2, 128, 2048)))
    plb = np.ascontiguousarray(np.broadcast_to(np.asarray(post_ln_b, f)[:, None, :], (2, 128, 2048)))
    ident = np.eye(128, dtype=f)
    half = 8
    inv = (500000.0 ** (-np.arange(half, dtype=np.float64) * (2.0 / 16))).astype(np.float32)

    in_maps = []
    for c in range(NCORES):
        s = 1024 * c
        xcat = np.zeros((NSLOT, 2048), f)
        pos = np.zeros(NSLOT, np.float64)
        lo = s - 192
        for slot in range(1216):
            tok = lo + slot
            if tok >= 0:
                pos[slot] = tok
        a0 = max(lo, 0)
        xcat[a0 - lo:1216, :] = x_prompt[0, a0:s + 1024, :]
        for b in range(2):
            r0 = 1216 + 32 * b
            xcat[r0:r0 + 16, :] = x_sample[2 * c + b]
            pos[r0:r0 + 16] = 1024 + np.arange(16)
        ang = pos.astype(np.float32)[:, None] * inv[None, :]
        cosd = np.cos(ang).astype(f).reshape(10, 128, 8).transpose(1, 0, 2).reshape(128, 80)
        sind = np.sin(ang).astype(f).reshape(10, 128, 8).transpose(1, 0, 2).reshape(128, 80)
        kmA = np.zeros((128, 24), f)
        kmB = np.zeros((128, 24), f)
        cvalid = lambda j: (lo + 64 * j) >= 0
        for j in range(2, 19):
            if j % 2 == 0:
                kmA[0:64, j] = 0.0 if cvalid(j - 2) else NEG
                kmA[64:128, j] = 0.0 if cvalid(j - 1) else NEG
                kmB[0:64, j] = 0.0 if cvalid(j) else NEG
                kmB[64:128, j] = NEG
            else:
                kmA[0:64, j] = 0.0 if cvalid(j - 1) else NEG
                kmA[64:128, j] = 0.0 if cvalid(j) else NEG
                kmB[0:64, j] = NEG
                kmB[64:128, j] = 0.0 if cvalid(j - 2) else NEG
        for b in range(2):
            kmB[:, 19 + b] = NEG
            kmB[64 + 32 * b:80 + 32 * b, 19 + b] = 0.0
        hv = np.full((128, 1), 1.0 if c > 0 else 0.0, f)
        in_maps.append({
            "xcat": xcat, "wina": wina, "waout": waout, "cwin": cwin, "cwout": cwout,
            "cosd": np.ascontiguousarray(cosd), "sind": np.ascontiguousarray(sind), "kmA": kmA, "kmB": kmB,
            "hv": hv, "sinkl": sinkl,
            "ck": np.ascontiguousarray(cache_k[0, 2 * c:2 * c + 2].reshape(2, 128, 512)),
            "cv": np.ascontiguousarray(cache_v[0, 2 * c:2 * c + 2].reshape(2, 128, 512)),
            "st": np.ascontiguousarray(state_conv[0, 2 * c:2 * c + 2]),
            "wdw": wdw, "bdw": bdw, "clg": clg, "clb": clb, "plg": plg, "plb": plb, "ident": ident,
        })
    if "nc" not in _NC_CACHE:
        _NC_CACHE["nc"] = build_nc()
    nc = _NC_CACHE["nc"]
    res = run_bass_kernel_spmd(nc, in_maps, core_ids=list(range(NCORES)))
    R = res.results
    y_prompt = np.concatenate([R[c]["y_p"] for c in range(NCORES)], axis=0)[None]
    y_sample = np.concatenate([R[c]["y_s"] for c in range(NCORES)], axis=0)
    nkp = R[7]["nk_p"].reshape(1, 1, 128, 8, 64)
    nvp = R[7]["nv_p"].reshape(1, 1, 128, 8, 64)
    nks = np.concatenate([R[c]["nk_s"] for c in range(NCORES)], axis=0).reshape(1, 16, 128, 8, 64)
    nvs = np.concatenate([R[c]["nv_s"] for c in range(NCORES)], axis=0).reshape(1, 16, 128, 8, 64)
    ncp = R[7]["nc_p"].reshape(1, 1, 30, 2048)
    ncs = np.concatenate([R[c]["nc_s"] for c in range(NCORES)], axis=0).reshape(1, 16, 30, 2048)
    return (y_prompt.astype(f), y_sample.astype(f), nkp.astype(f), nvp.astype(f), nks.astype(f),
            nvs.astype(f), ncp.astype(f), ncs.astype(f))
```

```python
import numpy as np
import concourse.bass as bass
import concourse.mybir as mybir
from concourse.bass_utils import run_bass_kernel_spmd

F32 = mybir.dt.float32
BF = mybir.dt.bfloat16
AF = mybir.ActivationFunctionType
OP = mybir.AluOpType

NCORES = 8
ALPHA = float((2.0 * 2) ** 0.25)
EPS = 1e-5
NEG = -30000.0
NSLOT = 1280
L1W = 1180
DEBUG_PHASES = 99
OP_LIMIT = 10 ** 9


class Tk:
    __slots__ = ("w", "r", "excl")

    def __init__(self, excl=False):
        self.w = None
        self.r = []
        self.excl = excl


class Sched:
    def __init__(self, nc):
        self.nc = nc
        self.eng = {}
        for name, h in (("pe", nc.tensor), ("act", nc.scalar), ("dve", nc.vector),
                        ("pool", nc.gpsimd), ("sp", nc.sync)):
            self.eng[name] = {"h": h, "sem": nc.alloc_semaphore(name="s_" + name), "cnt": 0,
                              "waited": {}, "dsems": [], "dnext": 0}
        for name, n in (("sp", 8), ("pool", 6), ("act", 4)):
            e = self.eng[name]
            for i in range(n):
                e["dsems"].append([nc.alloc_semaphore(name="d_%s%d" % (name, i)), 0])
        self.out_tokens = []
        self.nops = 0
        self.limit = OP_LIMIT

    def _wait(self, e, tok):
        if tok is None:
            return
        sem, val = tok
        key = id(sem)
        if e["waited"].get(key, 0) >= val:
            return
        e["h"].wait_ge(sem, val)
        e["waited"][key] = val

    def _deps(self, e, R, W):
        need = {}

        def add(tok):
            if tok is None:
                return
            k = id(tok[0])
            if k not in need or need[k][1] < tok[1]:
                need[k] = tok
        for t in R:
            add(t.w)
        for t in W:
            add(t.w)
            for rt in t.r:
                add(rt)
            if len(t.r) > 8:
                mx = {}
                for rt in t.r:
                    k = id(rt[0])
                    if k not in mx or mx[k][1] < rt[1]:
                        mx[k] = rt
                t.r = list(mx.values())
        for tok in need.values():
            self._wait(e, tok)

    def _commit(self, tok, R, W):
        for t in W:
            t.w = tok
            t.r = []
        for t in R:
            t.r.append(tok)

    def op(self, en, fn, R=(), W=()):
        self.nops += 1
        if self.nops > self.limit:
            return None
        e = self.eng[en]
        W = list(W) + [t for t in R if t.excl]
        R = [t for t in R if not t.excl]
        self._deps(e, R, W)
        ins = fn(e["h"])
        e["cnt"] += 1
        ins.then_inc(e["sem"], 1)
        tok = (e["sem"], e["cnt"])
        self._commit(tok, R, W)
        return tok

    def dma(self, en, out, in_, R=(), W=(), is_output=False):
        self.nops += 1
        if self.nops > self.limit:
            return None
        e = self.eng[en]
        self._deps(e, R, W)
        slot = e["dsems"][e["dnext"] % len(e["dsems"])]
        e["dnext"] += 1
        if slot[1] > 0:
            self._wait(e, (slot[0], slot[1]))
        e["h"].dma_start(out=out, in_=in_).then_inc(slot[0], 16)
        slot[1] += 16
        tok = (slot[0], slot[1])
        self._commit(tok, R, W)
        if is_output:
            self.out_tokens.append(tok)
        return tok

    def barrier(self):
        toks = []
        for e in self.eng.values():
            if e["cnt"] > 0:
                toks.append((e["sem"], e["cnt"]))
            for s in e["dsems"]:
                if s[1] > 0:
                    toks.append((s[0], s[1]))
        for e in self.eng.values():
            for t in toks:
                self._wait(e, t)

    def finish(self):
        e = self.eng["sp"]
        for tok in self.out_tokens:
            self._wait(e, tok)
        for name in ("pe", "act", "dve", "pool"):
            o = self.eng[name]
            if o["cnt"] > 0:
                self._wait(e, (o["sem"], o["cnt"]))


def build_nc():
    nc = bass.Bass("TRN2", target_bir_lowering=False)
    S = Sched(nc)

    def din(name, shape):
        return nc.dram_tensor(name, list(shape), F32, kind="ExternalInput").ap()

    def dout(name, shape):
        return nc.dram_tensor(name, list(shape), F32, kind="ExternalOutput").ap()

    xcat = din("xcat", [NSLOT, 2048])
    wina = din("wina", [2048, 5120])
    waout = din("waout", [2048, 2048])
    cwin = din("cwin", [2048, 6144])
    cwout = din("cwout", [2048, 2048])
    cosd = din("cosd", [128, 80])
    sind = din("sind", [128, 80])
    kmAd = din("kmA", [128, 24])
    kmBd = din("kmB", [128, 24])
    hvd = din("hv", [128, 1])
    sinkd = din("sinkl", [128, 16])
    ckd = din("ck", [2, 128, 512])
    cvd = din("cv", [2, 128, 512])
    std = din("st", [2, 30, 2048])
    wdwd = din("wdw", [128, 16 * 31])
    bdwd = din("bdw", [128, 16])
    clgd = din("clg", [128, 16])
    clbd = din("clb", [128, 16])
    plgd = din("plg", [2, 128, 2048])
    plbd = din("plb", [2, 128, 2048])
    identd = din("ident", [128, 128])

    y_p = dout("y_p", [1024, 2048])
    y_s = dout("y_s", [2, 16, 2048])
    nk_p = dout("nk_p", [128, 512])
    nv_p = dout("nv_p", [128, 512])
    nk_s = dout("nk_s", [2, 128, 512])
    nv_s = dout("nv_s", [2, 128, 512])
    nc_p = dout("nc_p", [30, 2048])
    nc_s = dout("nc_s", [2, 30, 2048])
    x1s = nc.dram_tensor("x1s", [1152, 2048], F32, kind="Internal").ap()

    from contextlib import ExitStack
    es_all = ExitStack()

    def sb(name, shape, dt, stack=None):
        return (stack or es_all).enter_context(nc.sbuf_tensor(name, list(shape), dt))

    def ps(name, shape, dt, stack):
        return stack.enter_context(nc.psum_tensor(name, list(shape), dt))

    with es_all:
        B1 = sb("B1", [128, 16, NSLOT], BF)
        B2 = sb("B2", [128, 16, 1184], BF)
        WB = sb("WB", [128, 16, 1536], BF)
        identb = sb("identb", [128, 128], BF)
        ident32 = sb("ident32", [128, 128], F32)
        onesb = sb("onesb", [128, 64], BF)
        ones32 = sb("ones32", [128, 128], F32)
        cosT = sb("cosT", [128, 10, 8], F32)
        sinT = sb("sinT", [128, 10, 8], F32)
        kmA = sb("kmAs", [128, 24], F32)
        kmB = sb("kmBs", [128, 24], F32)
        hv = sb("hvs", [128, 1], F32)
        esk = sb("esk", [128, 16], F32)
        epst = sb("epst", [128, 1], F32)
        wdw = sb("wdws", [128, 16, 31], F32)
        bdw = sb("bdws", [128, 16], F32)
        clg = sb("clgs", [128, 16], F32)
        clb = sb("clbs", [128, 16], F32)

        tC = Tk()
        ctk = []

        def cload(dst, srcap):
            tk = Tk()
            ctk.append(tk)
            S.dma("sp", dst, srcap, W=[tk])
        cload(ident32[:], identd)
        cload(cosT[:].rearrange("p a b -> p (a b)"), cosd)
        cload(sinT[:].rearrange("p a b -> p (a b)"), sind)
        cload(kmA[:], kmAd)
        cload(kmB[:], kmBd)
        cload(hv[:], hvd)
        cload(esk[:], sinkd)
        cload(wdw[:].rearrange("p a b -> p (a b)"), wdwd)
        cload(bdw[:], bdwd)
        cload(clg[:], clgd)
        cload(clb[:], clbd)
        S.op("dve", lambda v: v.tensor_copy(out=identb[:], in_=ident32[:]), R=ctk, W=[tC])
        S.op("dve", lambda v: v.memset(onesb[:], 1.0), W=[tC])
        S.op("dve", lambda v: v.memset(ones32[:], 1.0), W=[tC])
        S.op("dve", lambda v: v.memset(epst[:], EPS), W=[tC])
        S.op("act", lambda a: a.activation(out=esk[:], in_=esk[:], func=AF.Exp), R=[tC], W=[tC])

        tB1 = [Tk() for _ in range(10)]
        tB2 = [Tk() for _ in range(10)]
        tWB = [Tk() for _ in range(4)]
        tKT = [Tk() for _ in range(10)]
        tV = [Tk() for _ in range(10)]
        tSG = Tk()

        def wslot_ap(WB4, s):
            if s < 3:
                return WB[:, :, s * 512:(s + 1) * 512]
            return WB4[:]

        def load_wblock(slot, src_cols_ap):
            S.dma("pool", slot_ap_cur[slot], src_cols_ap.rearrange("(k p) n -> p k n", p=128), W=[tWB[slot]])

        with ExitStack() as esAB:
            KT = sb("KT", [128, 4, NSLOT], BF, esAB)
            V = sb("V", [128, 10, 512], BF, esAB)
            SG = sb("SG", [128, 16, 1152], BF, esAB)
            with ExitStack() as esA:
                slot_ap_cur = [wslot_ap(None, s) for s in range(3)]
                t32 = [sb("t32_%d" % i, [128, 512], F32, esA) for i in range(2)]
                t16 = [sb("t16_%d" % i, [128, 512], BF, esA) for i in range(6)]
                tt32 = [Tk(), Tk()]
                tt16 = [Tk() for _ in range(6)]
                rtmp = sb("rtmp", [128, 4, 64], F32, esA)
                trt = Tk()
                psA = [ps("psA%d" % i, [128, 512], F32, esA) for i in range(3)]
                tpsA = [Tk(True), Tk(True), Tk(True)]
                psQ = [ps("psQ%d" % i, [128, 512], BF, esA) for i in range(3)]
                tpsQ = [Tk(True), Tk(True), Tk(True)]
                esX = ExitStack()
                xb16 = [sb("xb16_%d" % i, [128, 2048], BF, esX) for i in range(2)]
                txb = [Tk(), Tk()]
                psT1 = ps("psT0", [128, 2048], BF, esX)
                psT = [psT1, psT1]
                tpsT1 = Tk(True)
                tpsT = [tpsT1, tpsT1]

                wb_order = list(range(10))
                def wb_src(wb):
                    return wina[:, wb * 512:(wb + 1) * 512]
                xorder = [1, 2, 3, 4, 5, 6, 7, 8, 9, 0]
                sgflat = SG[:].rearrange("p a b -> p (a b)")
                txs = [Tk() for _ in range(10)]

                def xsrc_ap(n_):
                    if n_ < 9:
                        return sgflat[:, n_ * 2048:(n_ + 1) * 2048]
                    return xb16[0][:]
                for n_, i in enumerate(xorder):
                    S.dma("pool", xsrc_ap(n_), xcat[i * 128:(i + 1) * 128, :], W=[txs[n_]])
                    if n_ == 1:
                        load_wblock(0, wb_src(0))
                    if n_ == 6:
                        load_wblock(1, wb_src(1))
                    if n_ == 9:
                        load_wblock(2, wb_src(2))

                def emit_X(n_):
                    i = xorder[n_]
                    xa = xsrc_ap(n_)

                    def tr_x(t):
                        for kk in range(16):
                            ins = t.transpose(out=psT1[:, kk * 128:(kk + 1) * 128],
                                              in_=xa[:, kk * 128:(kk + 1) * 128], identity=identb[:])
                        return ins
                    S.op("pe", tr_x, R=[txs[n_], tC], W=[tpsT1])
                    if n_ % 2 == 0:
                        S.op("act", lambda a: a.activation(
                            out=B1[:, :, i * 128:(i + 1) * 128],
                            in_=psT1[:].rearrange("p (k n) -> p k n", k=16), func=AF.Copy),
                            R=[tpsT1], W=[tB1[i]])
                    else:
                        S.op("dve", lambda v: v.tensor_copy(
                            out=B1[:, :, i * 128:(i + 1) * 128],
                            in_=psT1[:].rearrange("p (k n) -> p k n", k=16)),
                            R=[tpsT1], W=[tB1[i]])

                emit_X(0)
                emit_X(1)
                xnext = [2]
                grp = 0
                pending = []
                qcnt = [0]

                def flush_q(batch):
                    def trq(t):
                        for n_, (i, wb, q16) in enumerate(batch):
                            for c in range(4):
                                ins = t.transpose(out=psQ[n_][:, c * 128:(c + 1) * 128],
                                                  in_=t16[q16][:, c * 128:(c + 1) * 128], identity=identb[:])
                        return ins
                    S.op("pe", trq, R=[tt16[q16] for (_, _, q16) in batch] + [tC],
                         W=[tpsQ[n_] for n_ in range(len(batch))])
                    for n_, (i, wb, q16) in enumerate(batch):
                        if wb < 4:
                            S.op("dve", lambda v, n_=n_, i=i, wb=wb: v.tensor_copy(
                                out=B2[:, wb * 4:(wb + 1) * 4, (i - 1) * 128:i * 128],
                                in_=psQ[n_][:].rearrange("p (c n) -> p c n", c=4)),
                                R=[tpsQ[n_]], W=[tB2[i]])
                        else:
                            S.op("dve", lambda v, n_=n_, i=i: v.tensor_copy(
                                out=KT[:, :, i * 128:(i + 1) * 128],
                                in_=psQ[n_][:].rearrange("p (c n) -> p c n", c=4)),
                                R=[tpsQ[n_]], W=[tKT[i]])

                for wb in range(10):
                    slot = wb % 3
                    Wt = slot_ap_cur[slot]
                    if wb < 6:
                        tiles = list(range(1, 10)) if wb < 4 else list(range(10))
                        for i in tiles:
                            if xnext[0] < 10:
                                emit_X(xnext[0])
                                xnext[0] += 1
                            pa = grp % 3
                            grp += 1

                            def mm(t, i=i, pa=pa, Wt=Wt):
                                for kk in range(16):
                                    ins = t.matmul(psA[pa][:], lhsT=B1[:, kk, i * 128:(i + 1) * 128],
                                                   rhs=Wt[:, kk, :], start=(kk == 0), stop=(kk == 15))
                                return ins
                            S.op("pe", mm, R=[tB1[i], tWB[slot]], W=[tpsA[pa]])
                            if wb == 5:
                                S.op("act", lambda a, i=i, pa=pa: a.activation(out=V[:, i, :], in_=psA[pa][:], func=AF.Copy),
                                     R=[tpsA[pa]], W=[tV[i]])
                                if i >= 8:
                                    b3 = i % 2
                                    S.op("dve", lambda v, pa=pa, b3=b3: v.tensor_copy(out=t32[b3][:], in_=psA[pa][:]),
                                         R=[tpsA[pa]], W=[tt32[b3]])
                                    if i == 8:
                                        S.dma("sp", nv_p[0:64, :], t32[b3][64:128, :], R=[tt32[b3]], is_output=True)
                                    else:
                                        S.dma("sp", nv_p[64:128, :], t32[b3][0:64, :], R=[tt32[b3]], is_output=True)
                                        for b in range(2):
                                            S.dma("sp", nv_s[b, 112:128, :], t32[b3][64 + 32 * b:80 + 32 * b, :],
                                                  R=[tt32[b3]], is_output=True)
                                continue
                            b3 = grp % 2
                            S.op("act", lambda a, pa=pa, b3=b3: a.activation(out=t32[b3][:], in_=psA[pa][:], func=AF.Copy),
                                 R=[tpsA[pa]], W=[tt32[b3]])
                            xv = t32[b3][:].rearrange("p (h d) -> p h d", d=64)
                            x1 = xv[:, :, 0:8]
                            x2 = xv[:, :, 8:16]
                            cb = cosT[:, i, :].unsqueeze(1).broadcast_to([128, 8, 8])
                            sbb = sinT[:, i, :].unsqueeze(1).broadcast_to([128, 8, 8])
                            rv = [rtmp[:, j, :].rearrange("p (h d) -> p h d", d=8) for j in range(4)]

                            def rope1(v, x1=x1, x2=x2, cb=cb, sbb=sbb, rv=rv):
                                v.tensor_tensor(out=rv[0], in0=x1, in1=cb, op=OP.mult)
                                v.tensor_tensor(out=rv[1], in0=x2, in1=sbb, op=OP.mult)
                                v.tensor_tensor(out=rv[2], in0=x2, in1=cb, op=OP.mult)
                                return v.tensor_tensor(out=rv[3], in0=x1, in1=sbb, op=OP.mult)
                            S.op("dve", rope1, R=[tt32[b3], tC], W=[trt])

                            def rope2(v, x1=x1, x2=x2, rv=rv):
                                v.tensor_tensor(out=x1, in0=rv[0], in1=rv[1], op=OP.subtract)
                                return v.tensor_tensor(out=x2, in0=rv[2], in1=rv[3], op=OP.add)
                            S.op("dve", rope2, R=[trt], W=[tt32[b3]])
                            q16 = qcnt[0] % 6
                            qcnt[0] += 1
                            S.op("pool", lambda g, b3=b3, q16=q16: g.tensor_copy(out=t16[q16][:], in_=t32[b3][:]),
                                 R=[tt32[b3]], W=[tt16[q16]])
                            if wb == 4 and i >= 8:
                                if i == 8:
                                    S.dma("sp", nk_p[0:64, :], t32[b3][64:128, :], R=[tt32[b3]], is_output=True)
                                else:
                                    S.dma("sp", nk_p[64:128, :], t32[b3][0:64, :], R=[tt32[b3]], is_output=True)
                                    for b in range(2):
                                        S.dma("sp", nk_s[b, 112:128, :], t32[b3][64 + 32 * b:80 + 32 * b, :],
                                              R=[tt32[b3]], is_output=True)

                            pending.append((i, wb, q16))
                            if len(pending) == 5:
                                flush_q(pending[0:3])
                                del pending[0:3]
                        while pending:
                            flush_q(pending[0:3])
                            del pending[0:3]
                    else:
                        if wb == 6:
                            for tk in txs:
                                tSG.r.extend(tk.r)
                                if tk.w is not None:
                                    tSG.r.append(tk.w)
                        for c in range(4):
                            gt = (wb - 6) * 4 + c
                            for (c0, n) in ((128, 512), (640, 512), (1152, 128)):
                                pa = grp % 3
                                grp += 1
                                rt = [tB1[j] for j in range(c0 // 128, (c0 + n) // 128)]

                                def mmg(t, c=c, c0=c0, n=n, pa=pa, Wt=Wt):
                                    for kk in range(16):
                                        ins = t.matmul(psA[pa][:, 0:n], lhsT=Wt[:, kk, c * 128:(c + 1) * 128],
                                                       rhs=B1[:, kk, c0:c0 + n], start=(kk == 0), stop=(kk == 15))
                                    return ins
                                S.op("pe", mmg, R=rt + [tWB[slot]], W=[tpsA[pa]])
                                S.op("act", lambda a, gt=gt, c0=c0, n=n, pa=pa: a.activation(
                                    out=SG[:, gt, c0 - 128:c0 - 128 + n], in_=psA[pa][:, 0:n], func=AF.Silu),
                                    R=[tpsA[pa]], W=[tSG])
                    if wb + 3 < 10:
                        load_wblock(slot, wb_src(wb + 3))
                esX.close()
            S.barrier()

            if DEBUG_PHASES >= 2:
                with ExitStack() as esB:
                    P1 = [sb("P1_%d" % i, [128, 512], BF, esB) for i in range(2)]
                    P2 = [sb("P2_%d" % i, [128, 512], BF, esB) for i in range(2)]
                    tP1 = [Tk(), Tk()]
                    tP2 = [Tk(), Tk()]
                    Dsb2 = [sb("Dsb%d" % i, [128, 256], F32, esB) for i in range(2)]
                    rD2 = [sb("rD%d" % i, [128, 256], F32, esB) for i in range(2)]
                    og2 = [sb("og%d" % i, [128, 256], F32, esB) for i in range(2)]
                    tDsb2, trD2, tog2 = [Tk(), Tk()], [Tk(), Tk()], [Tk(), Tk()]
                    ck16 = [sb("ck16_%d" % i, [128, 512], BF, esB) for i in range(2)]
                    tck = [Tk(), Tk()]
                    KTc = sb("KTc", [128, 4, 2, 128], BF, esB)
                    Vc = sb("Vc", [128, 2, 512], BF, esB)
                    tKTc = [Tk(), Tk()]
                    tVc = [Tk(), Tk()]
                    pS = [ps("pS_%d" % i, [128, 1024], F32, esB) for i in range(2)]
                    pOD = [ps("pOD_%d" % i, [128, 512], F32, esB) for i in range(4)]
                    tS = [Tk(True), Tk(True)]
                    tOD = [Tk(True) for _ in range(4)]

                    for s in range(3):
                        S.dma("pool", WB[:, :, s * 512:(s + 1) * 512],
                              waout[:, s * 512:(s + 1) * 512].rearrange("(k p) n -> p k n", p=128), W=[tWB[s]])

                    for b in range(2):
                        S.dma("pool", ck16[b][:], ckd[b], W=[tck[b]])
                        S.dma("pool", Vc[:, b, :], cvd[b], W=[tVc[b]])
                        S.dma("sp", nk_s[b, 0:112, :], ckd[b, 16:128, :], is_output=True)
                        S.dma("sp", nv_s[b, 0:112, :], cvd[b, 16:128, :], is_output=True)

                    def prep_caches():
                        for b in range(2):
                            pSb = pS[0][:, 0:256].bitcast(BF)

                            def trc(t, pSb=pSb, b=b):
                                for c in range(4):
                                    ins = t.transpose(out=pSb[:, c * 128:(c + 1) * 128],
                                                      in_=ck16[b][:, c * 128:(c + 1) * 128], identity=identb[:])
                                return ins
                            S.op("pe", trc, R=[tck[b], tC], W=[tS[0]])
                            S.op("dve", lambda v, b=b, pSb=pSb: v.tensor_copy(
                                out=KTc[:, :, b, :], in_=pSb.rearrange("p (c n) -> p c n", c=4)),
                                R=[tS[0]], W=[tKTc[b]])

                    items = []
                    chunks = [("p", j) for j in range(2, 19)] + [("s", 0), ("s", 1)]
                    for (kind, j) in chunks:
                        if kind == "p":
                            nq = 64
                            qc0 = 64 * j - 128
                            qtile = j // 2
                            if j % 2 == 0:
                                i1, i2 = j // 2 - 1, j // 2
                            else:
                                i1, i2 = (j - 1) // 2, (j - 3) // 2
                            K1 = lambda jj, hs, i1=i1: KT[hs, jj, i1 * 128:(i1 + 1) * 128]
                            V1 = lambda cs, i1=i1: V[:, i1, cs]
                            K2 = lambda jj, hs, i2=i2: KT[hs, jj, i2 * 128:(i2 + 1) * 128]
                            V2 = lambda cs, i2=i2: V[:, i2, cs]
                            rdeps = [tKT[i1], tV[i1], tKT[i2], tV[i2], tB2[qtile], tC]
                            bcol = j
                        else:
                            b = j
                            nq = 16
                            qc0 = 1024 + 64 + 32 * b
                            qtile = 9
                            K1 = lambda jj, hs, b=b: KTc[hs, jj, b, :]
                            V1 = lambda cs, b=b: Vc[:, b, cs]
                            K2 = lambda jj, hs: KT[hs, jj, 1152:1280]
                            V2 = lambda cs: V[:, 9, cs]
                            rdeps = [tKTc[b], tVc[b], tKT[9], tV[9], tB2[9], tC]
                            bcol = 19 + b
                        pb = 0
                        for jj in range(4):
                            items.append(dict(nq=nq, qc0=qc0, qtile=qtile, pb=pb, K1=K1, V1=V1, K2=K2, V2=V2,
                                              rdeps=rdeps, bcol=bcol, jj=jj))

                    def emit_S(k):
                        I = items[k]
                        bi = k % 2
                        nq, n4, pb, jj = I["nq"], 4 * I["nq"], I["pb"], I["jj"]

                        def mmS(t):
                            for blk, Kf in ((0, I["K1"]), (1, I["K2"])):
                                for h in range(2):
                                    hs = slice(h * 64, (h + 1) * 64)
                                    q = B2[hs, jj * 4:(jj + 1) * 4, I["qc0"]:I["qc0"] + nq]
                                    c0 = 512 * h + 256 * blk
                                    ins = t.matmul(pS[bi][:, c0:c0 + n4].rearrange("p (g q) -> p g q", g=4),
                                                   lhsT=Kf(jj, hs), rhs=q, start=True, stop=True)
                            return ins
                        S.op("pe", mmS, R=I["rdeps"], W=[tS[bi]])
                        pv = pS[bi][:].rearrange("p (h c) -> p h c", h=2)
                        S.op("act", lambda a: a.activation(
                            out=P1[bi][:, 0:2 * n4].rearrange("p (h c) -> p h c", h=2), in_=pv[:, :, 0:n4], func=AF.Exp,
                            bias=kmA[:, I["bcol"]:I["bcol"] + 1], scale=0.125), R=[tS[bi], tC], W=[tP1[bi]])
                        S.op("act", lambda a: a.activation(
                            out=P2[bi][:, 0:2 * n4].rearrange("p (h c) -> p h c", h=2),
                            in_=pv[:, :, 256:256 + n4], func=AF.Exp,
                            bias=kmB[:, I["bcol"]:I["bcol"] + 1], scale=0.125), R=[tS[bi], tC], W=[tP2[bi]])

                    def emit_PV(k):
                        I = items[k]
                        bi = k % 2
                        b4 = k % 4
                        n4, jj = 4 * I["nq"], I["jj"]

                        def mmPV(t):
                            for h in range(2):
                                hs = slice(h * 64, (h + 1) * 64)
                                cs = slice(jj * 128 + h * 64, jj * 128 + (h + 1) * 64)
                                p1 = P1[bi][:, h * n4:(h + 1) * n4]
                                p2 = P2[bi][:, h * n4:(h + 1) * n4]
                                t.matmul(pOD[b4][hs, 0:n4], lhsT=I["V1"](cs), rhs=p1, start=True, stop=False)
                                t.matmul(pOD[b4][hs, 0:n4], lhsT=I["V2"](cs), rhs=p2, start=False, stop=True)
                                t.matmul(pOD[b4][hs, 256:256 + n4], lhsT=onesb[:, 0:64], rhs=p1, start=True, stop=False)
                                ins = t.matmul(pOD[b4][hs, 256:256 + n4], lhsT=onesb[:, 0:64], rhs=p2,
                                               start=False, stop=True)
                            return ins
                        S.op("pe", mmPV, R=I["rdeps"] + [tP1[bi], tP2[bi]], W=[tOD[b4]])

                    def emit_N(k):
                        I = items[k]
                        bi = k % 2
                        b4 = k % 4
                        nq, n4, jj, qc0 = I["nq"], 4 * I["nq"], I["jj"], I["qc0"]
                        Dsb, rD, og = Dsb2[bi], rD2[bi], og2[bi]
                        tDsb, trD, tog = tDsb2[bi], trD2[bi], tog2[bi]
                        esb = esk[:, jj * 4:(jj + 1) * 4].unsqueeze(2).broadcast_to([128, 4, nq])
                        S.op("dve", lambda v: v.tensor_tensor(
                            out=Dsb[:, 0:n4].rearrange("p (g q) -> p g q", g=4),
                            in0=pOD[b4][:, 256:256 + n4].rearrange("p (g q) -> p g q", g=4), in1=esb, op=OP.add),
                            R=[tOD[b4], tC], W=[tDsb])
                        if False:
                            S.op("dve", lambda v: v.reciprocal(out=rD[:, 0:n4], in_=Dsb[:, 0:n4]), R=[tDsb], W=[trD])
                        else:
                            S.op("act", lambda a: a.activation(out=rD[:, 0:n4], in_=Dsb[:, 0:n4], func=AF.Ln),
                                 R=[tDsb], W=[trD])
                            S.op("act", lambda a: a.activation(out=rD[:, 0:n4], in_=rD[:, 0:n4], func=AF.Exp,
                                                               scale=-1.0), W=[trD])
                        S.op("dve", lambda v: v.tensor_tensor(
                            out=og[:, 0:n4], in0=pOD[b4][:, 0:n4], in1=rD[:, 0:n4], op=OP.mult),
                            R=[tOD[b4], trD], W=[tog])
                        S.op("pool", lambda g: g.tensor_tensor(
                            out=B1[:, jj * 4:(jj + 1) * 4, 128 + qc0:128 + qc0 + nq],
                            in0=og[:, 0:n4].rearrange("p (g q) -> p g q", g=4),
                            in1=SG[:, jj * 4:(jj + 1) * 4, qc0:qc0 + nq], op=OP.mult),
                            R=[tog, tSG], W=[tB1[I["qtile"]]])

                    emit_S(0)
                    for k in range(len(items)):
                        if k == 40:
                            prep_caches()
                        if k + 1 < len(items):
                            emit_S(k + 1)
                        emit_PV(k)
                        if k >= 1:
                            emit_N(k - 1)
                    emit_N(len(items) - 1)
        S.barrier()

        tX1T = Tk()
        if DEBUG_PHASES >= 3:
            with ExitStack() as esC:
                W4 = sb("W4", [128, 16, 512], BF, esC)
                S.dma("pool", W4[:], waout[:, 1536:2048].rearrange("(k p) n -> p k n", p=128), W=[tWB[3]])
                Wc = [WB[:, :, 0:512], WB[:, :, 512:1024], WB[:, :, 1024:1536], W4[:]]
                lnG = sb("lnG", [128, 2048], F32, esC)
                lnB = sb("lnB", [128, 2048], F32, esC)
                tLN = Tk()
                S.dma("sp", lnG[:], plgd[0], W=[tLN])
                S.dma("sp", lnB[:], plbd[0], W=[tLN])
                x32 = [sb("x32_%d" % i, [128, 2048], F32, esC) for i in range(2)]
                z32 = [sb("z32_%d" % i, [128, 2048], F32, esC) for i in range(2)]
                x1b = [sb("x1b%d" % i, [128, 2048], BF, esC) for i in range(2)]
                tx32 = [Tk(), Tk()]
                tz32 = [Tk(), Tk()]
                tx1b = [Tk(), Tk()]
                stt = sb("stt", [128, 4, 6], F32, esC)
                mv = sb("mv", [128, 2], F32, esC)
                sd = sb("sd", [128, 1], F32, esC)
                rstd = sb("rstd", [128, 1], F32, esC)
                tst = Tk()
                pC = [ps("pC%d" % i, [128, 512], F32, esC) for i in range(4)]
                tpC = [Tk(True) for _ in range(4)]
                pT = ps("pTc", [128, 2048], BF, esC)
                tpT = Tk(True)
                def xloadC(i, dst, tk):
                    S.dma("sp", dst[:], xcat[i * 128:(i + 1) * 128, :], W=[tk])

                def postC(i, bi, when):
                    if when == "early":
                        S.dma("sp", x1s[(i - 1) * 128:i * 128, :], z32[bi][:], R=[tz32[bi]], W=[tX1S[i]])
                        S.op("act", lambda a: a.activation(out=x1b[i % 2][:], in_=z32[bi][:], func=AF.Copy),
                             R=[tz32[bi]], W=[tx1b[i % 2]])
                        return

                    def trx(t):
                        for kk in range(16):
                            ins = t.transpose(out=pT[:, kk * 128:(kk + 1) * 128],
                                              in_=x1b[i % 2][:, kk * 128:(kk + 1) * 128], identity=identb[:])
                        return ins
                    S.op("pe", trx, R=[tx1b[i % 2], tC], W=[tpT])
                    pv = pT[:].rearrange("p (k n) -> p k n", k=16)
                    if i < 9:
                        S.op("act", lambda a: a.activation(out=B2[:, :, (i - 1) * 128:i * 128], in_=pv, func=AF.Copy),
                             R=[tpT], W=[tX1T])
                    else:
                        def ev9(a):
                            a.activation(out=B2[:, :, 1024:1088], in_=pv[:, :, 0:64], func=AF.Copy)
                            a.activation(out=B2[:, :, 1118:1134], in_=pv[:, :, 64:80], func=AF.Copy)
                            return a.activation(out=B2[:, :, 1164:1180], in_=pv[:, :, 96:112], func=AF.Copy)
                        S.op("act", ev9, R=[tpT], W=[tX1T])
                for tk in tB2:
                    tX1T.r.extend(tk.r)
                    if tk.w is not None:
                        tX1T.r.append(tk.w)
                S.op("dve", lambda v: v.memset(B2[:, :, 1088:1184], 0.0), W=[tX1T])
                layer_tail(nc, S, tiles=[(i, i * 128, 128) for i in range(1, 10)], actT=B1,
                           tact=lambda i: [tB1[i]], Wc=Wc, tW=tWB, xload=xloadC, x32=x32, tx32=tx32,
                           z32=z32, tz32=tz32, pC=pC, tpC=tpC, lnG=lnG, lnB=lnB, tLN=tLN, stt=stt, mv=mv,
                           sd=sd, rstd=rstd, tst=tst, epst=epst, tC=tC, post=postC)
        S.barrier()
        if DEBUG_PHASES >= 4:
            phase_D(nc, S, sb, ps, locals())
        S.finish()
    print('total ops', S.nops)
    return nc


def layer_tail(nc, S, tiles, actT, tact, Wc, tW, xload, x32, tx32, z32, tz32, pC, tpC, lnG, lnB, tLN,
               stt, mv, sd, rstd, tst, epst, tC, post):
    pending = None
    for n_, (i, c0, m) in enumerate(tiles):
        bi = n_ % 2
        xload(i, x32[bi], tx32[bi])
        for cg in range(4):
            def mm(t, cg=cg, c0=c0, m=m):
                for kk in range(16):
                    ins = t.matmul(pC[cg][0:m, :], lhsT=actT[:, kk, c0:c0 + m], rhs=Wc[cg][:, kk, :],
                                   start=(kk == 0), stop=(kk == 15))
                return ins
            S.op("pe", mm, R=tact(i) + [tW[cg]], W=[tpC[cg]])
            S.op("dve", lambda v, cg=cg, bi=bi, m=m: v.scalar_tensor_tensor(
                out=z32[bi][0:m, cg * 512:(cg + 1) * 512], in0=x32[bi][0:m, cg * 512:(cg + 1) * 512],
                scalar=ALPHA, in1=pC[cg][0:m, :], op0=OP.mult, op1=OP.add),
                R=[tpC[cg], tx32[bi]], W=[tz32[bi]])
        if pending is not None:
            pending()
            pending = None

        def stats(v, bi=bi, m=m):
            for cg in range(4):
                ins = v.bn_stats(out=stt[0:m, cg, :], in_=z32[bi][0:m, cg * 512:(cg + 1) * 512])
            return ins
        S.op("dve", stats, R=[tz32[bi]], W=[tst])
        S.op("dve", lambda v, m=m: v.bn_aggr(out=mv[0:m, :], in_=stt[0:m, :, :].rearrange("p a b -> p (a b)")),
             W=[tst])
        S.op("act", lambda a, m=m: a.activation(out=sd[0:m, :], in_=mv[0:m, 1:2], func=AF.Sqrt,
                                                bias=epst[0:m, :], scale=1.0), R=[tC], W=[tst])
        S.op("dve", lambda v, m=m: v.reciprocal(out=rstd[0:m, :], in_=sd[0:m, :]), W=[tst])
        S.op("dve", lambda v, bi=bi, m=m: v.tensor_scalar(
            out=z32[bi][0:m, :], in0=z32[bi][0:m, :], scalar1=mv[0:m, 0:1], scalar2=rstd[0:m, 0:1],
            op0=OP.subtract, op1=OP.mult), R=[tst], W=[tz32[bi]])
        S.op("pool", lambda g, bi=bi, m=m: g.tensor_tensor(out=z32[bi][0:m, :], in0=z32[bi][0:m, :],
                                                          in1=lnG[0:m, :], op=OP.mult), R=[tLN], W=[tz32[bi]])
        S.op("pool", lambda g, bi=bi, m=m: g.tensor_tensor(out=z32[bi][0:m, :], in0=z32[bi][0:m, :],
                                                          in1=lnB[0:m, :], op=OP.add), R=[tLN], W=[tz32[bi]])
        pending = (lambda i=i, bi=bi: (post(i, bi, "early"), post(i, bi, "late")))
    if pending is not None:
        pending()


tX1S = [Tk() for _ in range(11)]


def phase_D(nc, S, sb, ps, L):
    from contextlib import ExitStack
    B1, B2, WB = L["B1"], L["B2"], L["WB"]
    tB1, tWB, tX1T, tC = L["tB1"], L["tWB"], L["tX1T"], L["tC"]
    identb, ident32, ones32 = L["identb"], L["ident32"], L["ones32"]
    wdw, bdw, clg, clb, hv, epst = L["wdw"], L["bdw"], L["clg"], L["clb"], L["hv"], L["epst"]
    cwin, cwout, std, nc_p, nc_s, y_p, y_s, x1s = (L["cwin"], L["cwout"], L["std"], L["nc_p"], L["nc_s"],
                                                   L["y_p"], L["y_s"], L["x1s"])
    plgd, plbd = L["plgd"], L["plbd"]
    tSGc = [Tk() for _ in range(16)]
    tU = [Tk() for _ in range(16)]
    tCc = [Tk() for _ in range(16)]
    tWs = [Tk() for _ in range(4)]
    tUT = Tk()
    blocks = ((34, 512), (546, 512), (1058, 122))
    oblocks = ((64, 512), (576, 512), (1088, 92))
    NOUT = 1116

    def wsl(s):
        return WB[:, :, s * 384:(s + 1) * 384]

    def load_cw(ct):
        s = ct % 4
        S.dma("pool", wsl(s)[:, :, 0:256], cwin[:, ct * 384:ct * 384 + 256].rearrange("(k p) n -> p k n", p=128),
              W=[tWs[s]])

    with ExitStack() as esD:
        utail = sb("utail", [128, 16, 62], F32, esD)
        esU = ExitStack()
        U16 = sb("U16", [128, 16, L1W], BF, esU)
        with ExitStack() as esD1:
            for s in range(4):
                tWs[s].w = None
            for s in range(3):
                for ws in tWs:
                    ws.r.extend(tWB[s].r)
                    if tWB[s].w is not None:
                        ws.r.append(tWB[s].w)
            for ct in range(16):
                for tk in tB1:
                    tSGc[ct].r.extend(tk.r)
                    if tk.w is not None:
                        tSGc[ct].r.append(tk.w)
            for ct in range(4):
                load_cw(ct)
            stT = sb("stT", [128, 16, 2, 30], F32, esD1)
            st32 = [sb("st32_%d" % i, [30, 2048], F32, esD1) for i in range(2)]
            tst32, tstT = [Tk(), Tk()], Tk()
            for b in range(2):
                S.dma("sp", st32[b][:], std[b], W=[tst32[b]])
                S.dma("sp", nc_s[b, 0:14, :], std[b, 16:30, :], is_output=True)
            sig32 = [sb("sig32_%d" % i, [128, 512], F32, esD1) for i in range(2)]
            tsig = [Tk(), Tk()]
            pa = [ps("pa%d" % i, [128, 512], F32, esD1) for i in range(2)]
            pb_ = [ps("pb%d" % i, [128, 512], F32, esD1) for i in range(2)]
            tpa, tpb = [Tk(True), Tk(True)], [Tk(True), Tk(True)]
            pst = ps("pst", [128, 512], F32, esD1)
            tpst = Tk(True)
            def prep_state():
                for b in range(2):
                    def trs(t, b=b):
                        for ct in range(16):
                            ins = t.transpose(out=pst[:, ct * 30:(ct + 1) * 30],
                                              in_=st32[b][:, ct * 128:(ct + 1) * 128], identity=ident32[0:30, 0:30])
                        return ins
                    S.op("pe", trs, R=[tst32[b], tC], W=[tpst])
                    S.op("dve", lambda v, b=b: v.tensor_copy(out=stT[:, :, b, :],
                                                             in_=pst[:, 0:480].rearrange("p (c n) -> p c n", c=16)),
                         R=[tpst], W=[tstT])
            it = 0
            for ct in range(16):
                s = ct % 4
                Wt = wsl(s)
                for (c0, n) in blocks:
                    bi = it % 2
                    it += 1
                    for part, (pp, tp) in enumerate(((pa, tpa), (pb_, tpb))):
                        def mm(t, part=part, pp=pp, bi=bi, c0=c0, n=n, Wt=Wt):
                            for kk in range(16):
                                ins = t.matmul(pp[bi][:, 0:n], lhsT=Wt[:, kk, part * 128:(part + 1) * 128],
                                               rhs=B2[:, kk, c0:c0 + n], start=(kk == 0), stop=(kk == 15))
                            return ins
                        S.op("pe", mm, R=[tX1T, tWs[s]], W=[tp[bi]])
                    S.op("act", lambda a, bi=bi, n=n: a.activation(out=sig32[bi][:, 0:n], in_=pb_[bi][:, 0:n],
                                                                   func=AF.Sigmoid), R=[tpb[bi]], W=[tsig[bi]])
                    S.op("dve", lambda v, bi=bi, n=n, c0=c0, ct=ct: v.tensor_tensor(
                        out=U16[:, ct, c0:c0 + n], in0=pa[bi][:, 0:n], in1=sig32[bi][:, 0:n], op=OP.mult),
                        R=[tpa[bi], tsig[bi]], W=[tU[ct]])
                    if c0 == 1058:
                        def tails(v, bi=bi, ct=ct):
                            v.tensor_tensor(out=utail[:, ct, 0:30], in0=pa[bi][:, 0:30], in1=sig32[bi][:, 0:30], op=OP.mult)
                            v.tensor_tensor(out=utail[:, ct, 30:46], in0=pa[bi][:, 60:76], in1=sig32[bi][:, 60:76], op=OP.mult)
                            return v.tensor_tensor(out=utail[:, ct, 46:62], in0=pa[bi][:, 106:122],
                                                   in1=sig32[bi][:, 106:122], op=OP.mult)
                        S.op("dve", tails, R=[tpa[bi], tsig[bi]], W=[tUT])
                if ct == 0:
                    prep_state()
                def fix(g, ct=ct):
                    g.tensor_copy(out=U16[:, ct, 1088:1118], in_=stT[:, ct, 0, :])
                    g.tensor_copy(out=U16[:, ct, 1134:1164], in_=stT[:, ct, 1, :])
                    return g.tensor_scalar(out=U16[:, ct, 34:64], in0=U16[:, ct, 34:64], scalar1=hv[:, 0:1],
                                           scalar2=1.0, op0=OP.mult, op1=OP.mult)
                S.op("pool", fix, R=[tstT, tC], W=[tU[ct]])
                if ct + 4 < 16:
                    load_cw(ct + 4)
        S.barrier()
        with ExitStack() as esD2:
            tWo = [Tk() for _ in range(4)]
            for s in range(3):
                for ws in tWs:
                    tWo[s].r.extend(ws.r)
                S.dma("pool", WB[:, :, s * 512:(s + 1) * 512],
                      cwout[:, s * 512:(s + 1) * 512].rearrange("(k p) n -> p k n", p=128), W=[tWo[s]])
            S1 = sb("S1", [128, NOUT], F32, esD2)
            S2 = sb("S2", [128, NOUT], F32, esD2)
            tS1, tS2 = Tk(), Tk()
            esD2i = ExitStack()
            diag = [sb("diag%d" % i, [128, 31, 128], BF, esD2i) for i in range(2)]
            tdg = [Tk(), Tk()]
            sq = [sb("sq%d" % i, [128, 512], F32, esD2i) for i in range(2)]
            tsq = [Tk(), Tk()]
            S.op("dve", lambda v: v.memset(S1[:], 0.0), W=[tS1])
            S.op("dve", lambda v: v.memset(S2[:], 0.0), W=[tS2])
            with ExitStack() as esD2p:
                pc = [ps("pc%d" % i, [128, 512], F32, esD2p) for i in range(6)]
                tpc = [Tk(True) for _ in range(6)]
                it = 0
                for ct in range(16):
                    d = ct % 2

                    def mkdiag(g, ct=ct, d=d):
                        for tap in range(31):
                            ins = g.tensor_scalar(out=diag[d][:, tap, :], in0=identb[:],
                                                  scalar1=wdw[:, ct, tap:tap + 1], scalar2=1.0, op0=OP.mult,
                                                  op1=OP.mult)
                        return ins
                    S.op("dve" if ct == 0 else "pool", mkdiag, R=[tC], W=[tdg[d]])
                    for (o0, n) in reversed(oblocks):
                        bi = it % 6
                        sqi = it % 2
                        it += 1

                        def mmc(t, ct=ct, d=d, o0=o0, n=n, bi=bi):
                            for tap in range(31):
                                ins = t.matmul(pc[bi][:, 0:n], lhsT=diag[d][:, tap, :],
                                               rhs=U16[:, ct, o0 + tap - 30:o0 + tap - 30 + n],
                                               start=(tap == 0), stop=(tap == 30))
                            return ins
                        S.op("pe", mmc, R=[tdg[d], tU[ct]], W=[tpc[bi]])
                        S.op("act", lambda a, ct=ct, o0=o0, n=n, bi=bi: a.activation(
                            out=U16[:, ct, o0:o0 + n], in_=pc[bi][:, 0:n], func=AF.Identity,
                            bias=bdw[:, ct:ct + 1], scale=1.0), R=[tpc[bi], tC], W=[tCc[ct]])
                        S.op("act", lambda a, ct=ct, n=n, bi=bi, sqi=sqi: a.activation(
                            out=sq[sqi][:, 0:n], in_=pc[bi][:, 0:n], func=AF.Square,
                            bias=bdw[:, ct:ct + 1], scale=1.0), R=[tpc[bi], tC], W=[tsq[sqi]])
                        S.op("dve", lambda v, ct=ct, o0=o0, n=n, bi=bi: v.scalar_tensor_tensor(
                            out=S1[:, o0 - 64:o0 - 64 + n], in0=pc[bi][:, 0:n], scalar=bdw[:, ct:ct + 1],
                            in1=S1[:, o0 - 64:o0 - 64 + n], op0=OP.add, op1=OP.add), R=[tpc[bi], tC], W=[tS1])
                        S.op("dve", lambda v, o0=o0, n=n, sqi=sqi: v.tensor_tensor(
                            out=S2[:, o0 - 64:o0 - 64 + n], in0=S2[:, o0 - 64:o0 - 64 + n], in1=sq[sqi][:, 0:n],
                            op=OP.add), R=[tsq[sqi]], W=[tS2])
            esD2i.close()
            S.barrier()
            with ExitStack() as esD3:
                tt = [sb("ttm%d" % i, [128, NOUT], F32, esD3) for i in range(2)]
                ttt = [Tk(), Tk()]
                pm = [ps("pm%d" % i, [128, 512], F32, esD3) for i in range(6)]
                tpm = Tk(True)
                GW = [sb("GW%d" % i, [128, 16, 128], BF, esD3) for i in range(2)]
                tGW = [Tk(), Tk()]
                pg = [ps("pg%d" % i, [128, 512], F32, esD3) for i in range(2)]
                tpg = [Tk(True), Tk(True)]

                def load_gw(ct):
                    S.dma("pool", GW[ct % 2][:],
                          cwin[:, ct * 384 + 256:ct * 384 + 384].rearrange("(k p) n -> p k n", p=128), W=[tGW[ct % 2]])

                gcnt = [0]

                def gate_proj(ct):
                    for (c0, n) in blocks:
                        bi = gcnt[0] % 2
                        gcnt[0] += 1

                        def mmg(t, bi=bi, c0=c0, n=n, ct=ct):
                            for kk in range(16):
                                ins = t.matmul(pg[bi][:, 0:n], lhsT=GW[ct % 2][:, kk, :], rhs=B2[:, kk, c0:c0 + n],
                                               start=(kk == 0), stop=(kk == 15))
                            return ins
                        S.op("pe", mmg, R=[tX1T, tGW[ct % 2]], W=[tpg[bi]])
                        S.op("act", lambda a, bi=bi, n=n, c0=c0, ct=ct: a.activation(
                            out=B1[:, ct, c0:c0 + n], in_=pg[bi][:, 0:n], func=AF.Silu),
                            R=[tpg[bi]], W=[tSGc[ct]])
                    if ct + 2 < 16:
                        load_gw(ct + 2)
                load_gw(0)
                load_gw(1)
                for ct in range(4):
                    gate_proj(ct)

                def mms(t):
                    for k_, (o0, n) in enumerate(oblocks):
                        t.matmul(pm[k_][:, 0:n], lhsT=ones32[:], rhs=S1[:, o0 - 64:o0 - 64 + n], start=True, stop=True)
                        ins = t.matmul(pm[3 + k_][:, 0:n], lhsT=ones32[:], rhs=S2[:, o0 - 64:o0 - 64 + n],
                                       start=True, stop=True)
                    return ins
                S.op("pe", mms, R=[tS1, tS2, tC], W=[tpm])

                def st1(v):
                    for k_, (o0, n) in enumerate(oblocks):
                        sl = slice(o0 - 64, o0 - 64 + n)
                        v.tensor_scalar(out=S1[:, sl], in0=pm[k_][:, 0:n], scalar1=1.0 / 2048, scalar2=None, op0=OP.mult)
                        ins = v.tensor_scalar(out=S2[:, sl], in0=pm[3 + k_][:, 0:n], scalar1=1.0 / 2048, scalar2=None,
                                              op0=OP.mult)
                    return ins
                S.op("dve", st1, R=[tpm], W=[tS1, tS2])
                S.op("dve", lambda v: v.tensor_tensor(out=tt[0][:], in0=S1[:], in1=S1[:], op=OP.mult),
                     R=[tS1], W=[ttt[0]])
                S.op("dve", lambda v: v.tensor_tensor(out=S2[:], in0=S2[:], in1=tt[0][:], op=OP.subtract),
                     R=[ttt[0]], W=[tS2])
                S.op("act", lambda a: a.activation(out=S2[:], in_=S2[:], func=AF.Ln, bias=epst[:, 0:1], scale=1.0),
                     R=[tC], W=[tS2])
                S.op("act", lambda a: a.activation(out=tt[0][:], in_=S2[:], func=AF.Exp, scale=-0.5),
                     R=[tS2], W=[ttt[0]])
                S.op("dve", lambda v: v.tensor_tensor(out=S2[:], in0=S1[:], in1=tt[0][:], op=OP.mult),
                     R=[tS1, ttt[0]], W=[tS2])
                tpr = Tk()

                def wr_ps(v):
                    for k_, (o0, n) in enumerate(oblocks):
                        sl = slice(o0 - 64, o0 - 64 + n)
                        v.tensor_copy(out=pm[k_][:, 0:n], in_=tt[0][:, sl])
                        ins = v.tensor_copy(out=pm[3 + k_][:, 0:n], in_=S2[:, sl])
                    return ins
                S.op("dve", wr_ps, R=[ttt[0], tS2], W=[tpm, tpr])
                def apply_ct(ct):
                    bi = ct % 2

                    def a1(v, ct=ct, bi=bi):
                        for k_, (o0, n) in enumerate(oblocks):
                            sl = slice(o0 - 64, o0 - 64 + n)
                            ins = v.tensor_tensor(out=tt[bi][:, sl], in0=U16[:, ct, o0:o0 + n], in1=pm[k_][:, 0:n],
                                                  op=OP.mult)
                        return ins
                    S.op("dve", a1, R=[tCc[ct], tpr], W=[ttt[bi]])

                    def a2(v, bi=bi):
                        for k_, (o0, n) in enumerate(oblocks):
                            sl = slice(o0 - 64, o0 - 64 + n)
                            ins = v.tensor_tensor(out=tt[bi][:, sl], in0=tt[bi][:, sl], in1=pm[3 + k_][:, 0:n],
                                                  op=OP.subtract)
                        return ins
                    S.op("dve", a2, R=[tpr], W=[ttt[bi]])
                    S.op("act", lambda a, ct=ct, bi=bi: a.activation(
                        out=tt[bi][:], in_=tt[bi][:], func=AF.Silu, bias=clb[:, ct:ct + 1], scale=clg[:, ct:ct + 1]),
                        R=[tC], W=[ttt[bi]])
                    S.op("pool", lambda g, ct=ct, bi=bi: g.tensor_tensor(
                        out=B1[:, ct, 64:64 + NOUT], in0=tt[bi][:], in1=B1[:, ct, 64:64 + NOUT], op=OP.mult),
                        R=[ttt[bi]], W=[tSGc[ct]])

                next_g = 4
                for ct in range(16):
                    apply_ct(ct)
                    if next_g < 16 and (next_g - ct <= 1 or ct % 2 == 0):
                        gate_proj(next_g)
                        next_g += 1
        esU.close()
        S.barrier()
        with ExitStack() as esE:
            W4 = sb("W4e", [128, 16, 512], BF, esE)
            S.dma("pool", W4[:], cwout[:, 1536:2048].rearrange("(k p) n -> p k n", p=128), W=[tWo[3]])
            Wc = [WB[:, :, 0:512], WB[:, :, 512:1024], WB[:, :, 1024:1536], W4[:]]
            lnG = sb("lnGe", [128, 2048], F32, esE)
            lnB = sb("lnBe", [128, 2048], F32, esE)
            tLN = Tk()
            S.dma("sp", lnG[:], plgd[1], W=[tLN])
            S.dma("sp", lnB[:], plbd[1], W=[tLN])
            x32 = [sb("x32e_%d" % i, [128, 2048], F32, esE) for i in range(2)]
            z32 = [sb("z32e_%d" % i, [128, 2048], F32, esE) for i in range(2)]
            tx32 = [Tk(), Tk()]
            tz32 = [Tk(), Tk()]
            stt = sb("stte", [128, 4, 6], F32, esE)
            mv = sb("mve", [128, 2], F32, esE)
            sd = sb("sde", [128, 1], F32, esE)
            rstd = sb("rstde", [128, 1], F32, esE)
            tst = Tk()
            pC = [ps("pCe%d" % i, [128, 512], F32, esE) for i in range(4)]
            tpC = [Tk(True) for _ in range(4)]
            pU = ps("pU", [32, 2048], F32, esE)
            tpU = Tk(True)
            ut_sb = sb("ut_sb", [32, 2048], F32, esE)
            tut = Tk()
            for (c0, n, dst) in ((0, 30, nc_p[:, :]), (30, 16, nc_s[0, 14:30, :]), (46, 16, nc_s[1, 14:30, :])):
                def tru(t, c0=c0, n=n):
                    for ct in range(16):
                        ins = t.transpose(out=pU[0:n, ct * 128:(ct + 1) * 128], in_=utail[:, ct, c0:c0 + n],
                                          identity=ident32[:])
                    return ins
                S.op("pe", tru, R=[tUT, tC], W=[tpU])
                S.op("dve", lambda v, n=n: v.tensor_copy(out=ut_sb[0:n, :], in_=pU[0:n, :]), R=[tpU], W=[tut])
                S.dma("sp", dst, ut_sb[0:n, :], R=[tut], is_output=True)

            tiles = [(m, 64 + 128 * m, 128) for m in range(8)] + [(8, 1118, 62)]

            def xloadE(m, dst, tk):
                if m < 8:
                    S.dma("sp", dst[:], x1s[128 * m + 64:128 * m + 192, :], W=[tk])
                else:
                    S.dma("sp", dst[0:16, :], x1s[1088:1104, :], W=[tk])
                    S.dma("sp", dst[46:62, :], x1s[1120:1136, :], W=[tk])

            def postE(m, bi, when):
                if when == "late":
                    return
                if m < 8:
                    S.dma("sp", y_p[128 * m:128 * (m + 1), :], z32[bi][:], R=[tz32[bi]], is_output=True)
                else:
                    S.dma("sp", y_s[0], z32[bi][0:16, :], R=[tz32[bi]], is_output=True)
                    S.dma("sp", y_s[1], z32[bi][46:62, :], R=[tz32[bi]], is_output=True)
            for tk in tX1S:
                if tk.w is not None:
                    for t_ in tx32:
                        t_.r.append(tk.w)
            layer_tail(nc, S, tiles=tiles, actT=B1, tact=lambda m: list(tSGc), Wc=Wc, tW=tWo, xload=xloadE,
                       x32=x32, tx32=tx32, z32=z32, tz32=tz32, pC=pC, tpC=tpC, lnG=lnG, lnB=lnB, tLN=tLN,
                       stt=stt, mv=mv, sd=sd, rstd=rstd, tst=tst, epst=epst, tC=tC, post=postE)


class _AllTk:
    def __init__(self, tks):
        self.tks = tks

    @property
    def w(self):
        return None

    @property
    def r(self):
        return _Sink()


class _Sink:
    def append(self, x):
        pass


def _perm_q():
    perm = np.zeros(2048, dtype=np.int64)
    for j in range(4):
        for g in range(4):
            for half in range(2):
                for d in range(64):
                    perm[(j * 4 + g) * 128 + half * 64 + d] = ((2 * j + half) * 4 + g) * 64 + d
    return perm


_NC_CACHE = {}


def kernel(x_prompt, x_sample, cache_k, cache_v, state_conv, attn_w_in, attn_sink, attn_w_out,
           conv_w_in, conv_w_dw, conv_b_dw, conv_ln_g, conv_ln_b, conv_w_out, post_ln_g, post_ln_b):
    f = np.float32
    x_prompt = np.asarray(x_prompt, f)
    x_sample = np.asarray(x_sample, f)
    cache_k = np.asarray(cache_k, f)
    cache_v = np.asarray(cache_v, f)
    state_conv = np.asarray(state_conv, f)
    perm = _perm_q()
    w_in = np.asarray(attn_w_in, f)[0]
    wina = np.ascontiguousarray(np.concatenate(
        [w_in[:, perm], w_in[:, 2048:3072], w_in[:, 3072 + perm]], axis=1))
    waout = np.ascontiguousarray(np.asarray(attn_w_out, f)[0][perm, :])
    cw = np.asarray(conv_w_in, f)[0]
    cwin = np.ascontiguousarray(cw.reshape(2048, 3, 16, 128).transpose(0, 2, 1, 3).reshape(2048, 6144))
    cwout = np.ascontiguousarray(np.asarray(conv_w_out, f)[0])
    sink = np.asarray(attn_sink, f)[0]
    sinkl = np.zeros((128, 16), f)
    for p in range(128):
        for j in range(4):
            for g in range(4):
                sinkl[p, j * 4 + g] = sink[(2 * j + p // 64) * 4 + g]
    wdw = np.ascontiguousarray(np.asarray(conv_w_dw, f)[0].reshape(31, 16, 128).transpose(2, 1, 0).reshape(128, 16 * 31))
    lay = lambda v: np.ascontiguousarray(np.asarray(v, f)[0].reshape(16, 128).T)
    bdw, clg, clb = lay(conv_b_dw), lay(conv_ln_g), lay(conv_ln_b)
    plg = np.ascontiguousarray(np.broadcast_to(np.asarray(post_ln_g, f)[:, None, :], (2, 128, 2048)))
    plb = np.ascontiguousarray(np.broadcast_to(np.asarray(post_ln_b, f)[:, None, :], (2, 128, 2048)))
    ident = np.eye(128, dtype=f)
    half = 8
    inv = (500000.0 ** (-np.arange(half, dtype=np.float64) * (2.0 / 16))).astype(np.float32)

    in_maps = []
    for c in range(NCORES):
        s = 1024 * c
        xcat = np.zeros((NSLOT, 2048), f)
        pos = np.zeros(NSLOT, np.float64)
        lo = s - 192
        for slot in range(1216):
            tok = lo + slot
            if tok >= 0:
                pos[slot] = tok
        a0 = max(lo, 0)
        xcat[a0 - lo:1216, :] = x_prompt[0, a0:s + 1024, :]
        for b in range(2):
            r0 = 1216 + 32 * b
            xcat[r0:r0 + 16, :] = x_sample[2 * c + b]
            pos[r0:r0 + 16] = 1024 + np.arange(16)
        ang = pos.astype(np.float32)[:, None] * inv[None, :]
        cosd = np.cos(ang).astype(f).reshape(10, 128, 8).transpose(1, 0, 2).reshape(128, 80)
        sind = np.sin(ang).astype(f).reshape(10, 128, 8).transpose(1, 0, 2).reshape(128, 80)
        kmA = np.zeros((128, 24), f)
        kmB = np.zeros((128, 24), f)
        cvalid = lambda j: (lo + 64 * j) >= 0
        for j in range(2, 19):
            if j % 2 == 0:
                kmA[0:64, j] = 0.0 if cvalid(j - 2) else NEG
                kmA[64:128, j] = 0.0 if cvalid(j - 1) else NEG
                kmB[0:64, j] = 0.0 if cvalid(j) else NEG
                kmB[64:128, j] = NEG
            else:
                kmA[0:64, j] = 0.0 if cvalid(j - 1) else NEG
                kmA[64:128, j] = 0.0 if cvalid(j) else NEG
                kmB[0:64, j] = NEG
                kmB[64:128, j] = 0.0 if cvalid(j - 2) else NEG
        for b in range(2):
            kmB[:, 19 + b] = NEG
            kmB[64 + 32 * b:80 + 32 * b, 19 + b] = 0.0
        hv = np.full((128, 1), 1.0 if c > 0 else 0.0, f)
        in_maps.append({
            "xcat": xcat, "wina": wina, "waout": waout, "cwin": cwin, "cwout": cwout,
            "cosd": np.ascontiguousarray(cosd), "sind": np.ascontiguousarray(sind), "kmA": kmA, "kmB": kmB,
            "hv": hv, "sinkl": sinkl,
            "ck": np.ascontiguousarray(cache_k[0, 2 * c:2 * c + 2].reshape(2, 128, 512)),
            "cv": np.ascontiguousarray(cache_v[0, 2 * c:2 * c + 2].reshape(2, 128, 512)),
            "st": np.ascontiguousarray(state_conv[0, 2 * c:2 * c + 2]),
            "wdw": wdw, "bdw": bdw, "clg": clg, "clb": clb, "plg": plg, "plb": plb, "ident": ident,
        })
    if "nc" not in _NC_CACHE:
        _NC_CACHE["nc"] = build_nc()
    nc = _NC_CACHE["nc"]
    res = run_bass_kernel_spmd(nc, in_maps, core_ids=list(range(NCORES)))
    R = res.results
    y_prompt = np.concatenate([R[c]["y_p"] for c in range(NCORES)], axis=0)[None]
    y_sample = np.concatenate([R[c]["y_s"] for c in range(NCORES)], axis=0)
    nkp = R[7]["nk_p"].reshape(1, 1, 128, 8, 64)
    nvp = R[7]["nv_p"].reshape(1, 1, 128, 8, 64)
    nks = np.concatenate([R[c]["nk_s"] for c in range(NCORES)], axis=0).reshape(1, 16, 128, 8, 64)
    nvs = np.concatenate([R[c]["nv_s"] for c in range(NCORES)], axis=0).reshape(1, 16, 128, 8, 64)
    ncp = R[7]["nc_p"].reshape(1, 1, 30, 2048)
    ncs = np.concatenate([R[c]["nc_s"] for c in range(NCORES)], axis=0).reshape(1, 16, 30, 2048)
    return (y_prompt.astype(f), y_sample.astype(f), nkp.astype(f), nvp.astype(f), nks.astype(f),
            nvs.astype(f), ncp.astype(f), ncs.astype(f))
```

```python
import numpy as np
import concourse.bass as bass
import concourse.mybir as mybir
from concourse.bass_utils import run_bass_kernel_spmd

F32 = mybir.dt.float32
BF = mybir.dt.bfloat16
AF = mybir.ActivationFunctionType
OP = mybir.AluOpType

NCORES = 8
ALPHA = float((2.0 * 2) ** 0.25)
EPS = 1e-5
NEG = -30000.0
NSLOT = 1280
L1W = 1180
DEBUG_PHASES = 99
OP_LIMIT = 10 ** 9


class Tk:
    __slots__ = ("w", "r", "excl")

    def __init__(self, excl=False):
        self.w = None
        self.r = []
        self.excl = excl


class Sched:
    def __init__(self, nc):
        self.nc = nc
        self.eng = {}
        for name, h in (("pe", nc.tensor), ("act", nc.scalar), ("dve", nc.vector),
                        ("pool", nc.gpsimd), ("sp", nc.sync)):
            self.eng[name] = {"h": h, "sem": nc.alloc_semaphore(name="s_" + name), "cnt": 0,
                              "waited": {}, "dsems": [], "dnext": 0}
        for name, n in (("sp", 8), ("pool", 6), ("act", 4)):
            e = self.eng[name]
            for i in range(n):
                e["dsems"].append([nc.alloc_semaphore(name="d_%s%d" % (name, i)), 0])
        self.out_tokens = []
        self.nops = 0
        self.limit = OP_LIMIT

    def _wait(self, e, tok):
        if tok is None:
            return
        sem, val = tok
        key = id(sem)
        if e["waited"].get(key, 0) >= val:
            return
        e["h"].wait_ge(sem, val)
        e["waited"][key] = val

    def _deps(self, e, R, W):
        need = {}

        def add(tok):
            if tok is None:
                return
            k = id(tok[0])
            if k not in need or need[k][1] < tok[1]:
                need[k] = tok
        for t in R:
            add(t.w)
        for t in W:
            add(t.w)
            for rt in t.r:
                add(rt)
            if len(t.r) > 8:
                mx = {}
                for rt in t.r:
                    k = id(rt[0])
                    if k not in mx or mx[k][1] < rt[1]:
                        mx[k] = rt
                t.r = list(mx.values())
        for tok in need.values():
            self._wait(e, tok)

    def _commit(self, tok, R, W):
        for t in W:
            t.w = tok
            t.r = []
        for t in R:
            t.r.append(tok)

    def op(self, en, fn, R=(), W=()):
        self.nops += 1
        if self.nops > self.limit:
            return None
        e = self.eng[en]
        W = list(W) + [t for t in R if t.excl]
        R = [t for t in R if not t.excl]
        self._deps(e, R, W)
        ins = fn(e["h"])
        e["cnt"] += 1
        ins.then_inc(e["sem"], 1)
        tok = (e["sem"], e["cnt"])
        self._commit(tok, R, W)
        return tok

    def dma(self, en, out, in_, R=(), W=(), is_output=False):
        self.nops += 1
        if self.nops > self.limit:
            return None
        e = self.eng[en]
        self._deps(e, R, W)
        slot = e["dsems"][e["dnext"] % len(e["dsems"])]
        e["dnext"] += 1
        if slot[1] > 0:
            self._wait(e, (slot[0], slot[1]))
        e["h"].dma_start(out=out, in_=in_).then_inc(slot[0], 16)
        slot[1] += 16
        tok = (slot[0], slot[1])
        self._commit(tok, R, W)
        if is_output:
            self.out_tokens.append(tok)
        return tok

    def barrier(self):
        toks = []
        for e in self.eng.values():
            if e["cnt"] > 0:
                toks.append((e["sem"], e["cnt"]))
            for s in e["dsems"]:
                if s[1] > 0:
                    toks.append((s[0], s[1]))
        for e in self.eng.values():
            for t in toks:
                self._wait(e, t)

    def finish(self):
        e = self.eng["sp"]
        for tok in self.out_tokens:
            self._wait(e, tok)
        for name in ("pe", "act", "dve", "pool"):
            o = self.eng[name]
            if o["cnt"] > 0:
                self._wait(e, (o["sem"], o["cnt"]))


def build_nc():
    nc = bass.Bass("TRN2", target_bir_lowering=False)
    S = Sched(nc)

    def din(name, shape):
        return nc.dram_tensor(name, list(shape), F32, kind="ExternalInput").ap()

    def dout(name, shape):
        return nc.dram_tensor(name, list(shape), F32, kind="ExternalOutput").ap()

    xcat = din("xcat", [NSLOT, 2048])
    wina = din("wina", [2048, 5120])
    waout = din("waout", [2048, 2048])
    cwin = din("cwin", [2048, 6144])
    cwout = din("cwout", [2048, 2048])
    cosd = din("cosd", [128, 80])
    sind = din("sind", [128, 80])
    kmAd = din("kmA", [128, 24])
    kmBd = din("kmB", [128, 24])
    hvd = din("hv", [128, 1])
    sinkd = din("sinkl", [128, 16])
    ckd = din("ck", [2, 128, 512])
    cvd = din("cv", [2, 128, 512])
    std = din("st", [2, 30, 2048])
    wdwd = din("wdw", [128, 16 * 31])
    bdwd = din("bdw", [128, 16])
    clgd = din("clg", [128, 16])
    clbd = din("clb", [128, 16])
    plgd = din("plg", [2, 128, 2048])
    plbd = din("plb", [2, 128, 2048])
    identd = din("ident", [128, 128])

    y_p = dout("y_p", [1024, 2048])
    y_s = dout("y_s", [2, 16, 2048])
    nk_p = dout("nk_p", [128, 512])
    nv_p = dout("nv_p", [128, 512])
    nk_s = dout("nk_s", [2, 128, 512])
    nv_s = dout("nv_s", [2, 128, 512])
    nc_p = dout("nc_p", [30, 2048])
    nc_s = dout("nc_s", [2, 30, 2048])
    x1s = nc.dram_tensor("x1s", [1152, 2048], F32, kind="Internal").ap()

    from contextlib import ExitStack
    es_all = ExitStack()

    def sb(name, shape, dt, stack=None):
        return (stack or es_all).enter_context(nc.sbuf_tensor(name, list(shape), dt))

    def ps(name, shape, dt, stack):
        return stack.enter_context(nc.psum_tensor(name, list(shape), dt))

    with es_all:
        B1 = sb("B1", [128, 16, NSLOT], BF)
        B2 = sb("B2", [128, 16, 1184], BF)
        WB = sb("WB", [128, 16, 1536], BF)
        identb = sb("identb", [128, 128], BF)
        ident32 = sb("ident32", [128, 128], F32)
        onesb = sb("onesb", [128, 64], BF)
        ones32 = sb("ones32", [128, 128], F32)
        cosT = sb("cosT", [128, 10, 8], F32)
        sinT = sb("sinT", [128, 10, 8], F32)
        kmA = sb("kmAs", [128, 24], F32)
        kmB = sb("kmBs", [128, 24], F32)
        hv = sb("hvs", [128, 1], F32)
        esk = sb("esk", [128, 16], F32)
        epst = sb("epst", [128, 1], F32)
        wdw = sb("wdws", [128, 16, 31], F32)
        bdw = sb("bdws", [128, 16], F32)
        clg = sb("clgs", [128, 16], F32)
        clb = sb("clbs", [128, 16], F32)

        tC = Tk()
        ctk = []

        def cload(dst, srcap):
            tk = Tk()
            ctk.append(tk)
            S.dma("sp", dst, srcap, W=[tk])
        cload(ident32[:], identd)
        cload(cosT[:].rearrange("p a b -> p (a b)"), cosd)
        cload(sinT[:].rearrange("p a b -> p (a b)"), sind)
        cload(kmA[:], kmAd)
        cload(kmB[:], kmBd)
        cload(hv[:], hvd)
        cload(esk[:], sinkd)
        cload(wdw[:].rearrange("p a b -> p (a b)"), wdwd)
        cload(bdw[:], bdwd)
        cload(clg[:], clgd)
        cload(clb[:], clbd)
        S.op("dve", lambda v: v.tensor_copy(out=identb[:], in_=ident32[:]), R=ctk, W=[tC])
        S.op("dve", lambda v: v.memset(onesb[:], 1.0), W=[tC])
        S.op("dve", lambda v: v.memset(ones32[:], 1.0), W=[tC])
        S.op("dve", lambda v: v.memset(epst[:], EPS), W=[tC])
        S.op("act", lambda a: a.activation(out=esk[:], in_=esk[:], func=AF.Exp), R=[tC], W=[tC])

        tB1 = [Tk() for _ in range(10)]
        tB2 = [Tk() for _ in range(10)]
        tWB = [Tk() for _ in range(4)]
        tKT = [Tk() for _ in range(10)]
        tV = [Tk() for _ in range(10)]
        tSG = Tk()

        def wslot_ap(WB4, s):
            if s < 3:
                return WB[:, :, s * 512:(s + 1) * 512]
            return WB4[:]

        def load_wblock(slot, src_cols_ap):
            S.dma("pool", slot_ap_cur[slot], src_cols_ap.rearrange("(k p) n -> p k n", p=128), W=[tWB[slot]])

        with ExitStack() as esAB:
            KT = sb("KT", [128, 4, NSLOT], BF, esAB)
            V = sb("V", [128, 10, 512], BF, esAB)
            SG = sb("SG", [128, 16, 1152], BF, esAB)
            with ExitStack() as esA:
                slot_ap_cur = [wslot_ap(None, s) for s in range(3)]
                t32 = [sb("t32_%d" % i, [128, 512], F32, esA) for i in range(2)]
                t16 = [sb("t16_%d" % i, [128, 512], BF, esA) for i in range(6)]
                tt32 = [Tk(), Tk()]
                tt16 = [Tk() for _ in range(6)]
                rtmp = sb("rtmp", [128, 4, 64], F32, esA)
                trt = Tk()
                psA = [ps("psA%d" % i, [128, 512], F32, esA) for i in range(3)]
                tpsA = [Tk(True), Tk(True), Tk(True)]
                psQ = [ps("psQ%d" % i, [128, 512], BF, esA) for i in range(3)]
                tpsQ = [Tk(True), Tk(True), Tk(True)]
                esX = ExitStack()
                xb16 = [sb("xb16_%d" % i, [128, 2048], BF, esX) for i in range(2)]
                txb = [Tk(), Tk()]
                psT1 = ps("psT0", [128, 2048], BF, esX)
                psT = [psT1, psT1]
                tpsT1 = Tk(True)
                tpsT = [tpsT1, tpsT1]

                wb_order = list(range(10))
                def wb_src(wb):
                    return wina[:, wb * 512:(wb + 1) * 512]
                xorder = [1, 2, 3, 4, 5, 6, 7, 8, 9, 0]
                sgflat = SG[:].rearrange("p a b -> p (a b)")
                txs = [Tk() for _ in range(10)]

                def xsrc_ap(n_):
                    if n_ < 9:
                        return sgflat[:, n_ * 2048:(n_ + 1) * 2048]
                    return xb16[0][:]
                for n_, i in enumerate(xorder):
                    S.dma("pool", xsrc_ap(n_), xcat[i * 128:(i + 1) * 128, :], W=[txs[n_]])
                    if n_ == 1:
                        load_wblock(0, wb_src(0))
                    if n_ == 6:
                        load_wblock(1, wb_src(1))
                    if n_ == 9:
                        load_wblock(2, wb_src(2))

                def emit_X(n_):
                    i = xorder[n_]
                    xa = xsrc_ap(n_)

                    def tr_x(t):
                        for kk in range(16):
                            ins = t.transpose(out=psT1[:, kk * 128:(kk + 1) * 128],
                                              in_=xa[:, kk * 128:(kk + 1) * 128], identity=identb[:])
                        return ins
                    S.op("pe", tr_x, R=[txs[n_], tC], W=[tpsT1])
                    if n_ % 2 == 0:
                        S.op("act", lambda a: a.activation(
                            out=B1[:, :, i * 128:(i + 1) * 128],
                            in_=psT1[:].rearrange("p (k n) -> p k n", k=16), func=AF.Copy),
                            R=[tpsT1], W=[tB1[i]])
                    else:
                        S.op("dve", lambda v: v.tensor_copy(
                            out=B1[:, :, i * 128:(i + 1) * 128],
                            in_=psT1[:].rearrange("p (k n) -> p k n", k=16)),
                            R=[tpsT1], W=[tB1[i]])

                emit_X(0)
                emit_X(1)
                xnext = [2]
                grp = 0
                pending = []
                qcnt = [0]

                def flush_q(batch):
                    def trq(t):
                        for n_, (i, wb, q16) in enumerate(batch):
                            for c in range(4):
                                ins = t.transpose(out=psQ[n_][:, c * 128:(c + 1) * 128],
                                                  in_=t16[q16][:, c * 128:(c + 1) * 128], identity=identb[:])
                        return ins
                    S.op("pe", trq, R=[tt16[q16] for (_, _, q16) in batch] + [tC],
                         W=[tpsQ[n_] for n_ in range(len(batch))])
                    for n_, (i, wb, q16) in enumerate(batch):
                        if wb < 4:
                            S.op("dve", lambda v, n_=n_, i=i, wb=wb: v.tensor_copy(
                                out=B2[:, wb * 4:(wb + 1) * 4, (i - 1) * 128:i * 128],
                                in_=psQ[n_][:].rearrange("p (c n) -> p c n", c=4)),
                                R=[tpsQ[n_]], W=[tB2[i]])
                        else:
                            S.op("dve", lambda v, n_=n_, i=i: v.tensor_copy(
                                out=KT[:, :, i * 128:(i + 1) * 128],
                                in_=psQ[n_][:].rearrange("p (c n) -> p c n", c=4)),
                                R=[tpsQ[n_]], W=[tKT[i]])

                for wb in range(10):
                    slot = wb % 3
                    Wt = slot_ap_cur[slot]
                    if wb < 6:
                        tiles = list(range(1, 10)) if wb < 4 else list(range(10))
                        for i in tiles:
                            if xnext[0] < 10:
                                emit_X(xnext[0])
                                xnext[0] += 1
                            pa = grp % 3
                            grp += 1

                            def mm(t, i=i, pa=pa, Wt=Wt):
                                for kk in range(16):
                                    ins = t.matmul(psA[pa][:], lhsT=B1[:, kk, i * 128:(i + 1) * 128],
                                                   rhs=Wt[:, kk, :], start=(kk == 0), stop=(kk == 15))
                                return ins
                            S.op("pe", mm, R=[tB1[i], tWB[slot]], W=[tpsA[pa]])
                            if wb == 5:
                                S.op("act", lambda a, i=i, pa=pa: a.activation(out=V[:, i, :], in_=psA[pa][:], func=AF.Copy),
                                     R=[tpsA[pa]], W=[tV[i]])
                                if i >= 8:
                                    b3 = i % 2
                                    S.op("dve", lambda v, pa=pa, b3=b3: v.tensor_copy(out=t32[b3][:], in_=psA[pa][:]),
                                         R=[tpsA[pa]], W=[tt32[b3]])
                                    if i == 8:
                                        S.dma("sp", nv_p[0:64, :], t32[b3][64:128, :], R=[tt32[b3]], is_output=True)
                                    else:
                                        S.dma("sp", nv_p[64:128, :], t32[b3][0:64, :], R=[tt32[b3]], is_output=True)
                                        for b in range(2):
                                            S.dma("sp", nv_s[b, 112:128, :], t32[b3][64 + 32 * b:80 + 32 * b, :],
                                                  R=[tt32[b3]], is_output=True)
                                continue
                            b3 = grp % 2
                            S.op("act", lambda a, pa=pa, b3=b3: a.activation(out=t32[b3][:], in_=psA[pa][:], func=AF.Copy),
                                 R=[tpsA[pa]], W=[tt32[b3]])
                            xv = t32[b3][:].rearrange("p (h d) -> p h d", d=64)
                            x1 = xv[:, :, 0:8]
                            x2 = xv[:, :, 8:16]
                            cb = cosT[:, i, :].unsqueeze(1).broadcast_to([128, 8, 8])
                            sbb = sinT[:, i, :].unsqueeze(1).broadcast_to([128, 8, 8])
                            rv = [rtmp[:, j, :].rearrange("p (h d) -> p h d", d=8) for j in range(4)]

                            def rope1(v, x1=x1, x2=x2, cb=cb, sbb=sbb, rv=rv):
                                v.tensor_tensor(out=rv[0], in0=x1, in1=cb, op=OP.mult)
                                v.tensor_tensor(out=rv[1], in0=x2, in1=sbb, op=OP.mult)
                                v.tensor_tensor(out=rv[2], in0=x2, in1=cb, op=OP.mult)
                                return v.tensor_tensor(out=rv[3], in0=x1, in1=sbb, op=OP.mult)
                            S.op("dve", rope1, R=[tt32[b3], tC], W=[trt])

                            def rope2(v, x1=x1, x2=x2, rv=rv):
                                v.tensor_tensor(out=x1, in0=rv[0], in1=rv[1], op=OP.subtract)
                                return v.tensor_tensor(out=x2, in0=rv[2], in1=rv[3], op=OP.add)
                            S.op("dve", rope2, R=[trt], W=[tt32[b3]])
                            q16 = qcnt[0] % 6
                            qcnt[0] += 1
                            S.op("pool", lambda g, b3=b3, q16=q16: g.tensor_copy(out=t16[q16][:], in_=t32[b3][:]),
                                 R=[tt32[b3]], W=[tt16[q16]])
                            if wb == 4 and i >= 8:
                                if i == 8:
                                    S.dma("sp", nk_p[0:64, :], t32[b3][64:128, :], R=[tt32[b3]], is_output=True)
                                else:
                                    S.dma("sp", nk_p[64:128, :], t32[b3][0:64, :], R=[tt32[b3]], is_output=True)
                                    for b in range(2):
                                        S.dma("sp", nk_s[b, 112:128, :], t32[b3][64 + 32 * b:80 + 32 * b, :],
                                              R=[tt32[b3]], is_output=True)

                            pending.append((i, wb, q16))
                            if len(pending) == 5:
                                flush_q(pending[0:3])
                                del pending[0:3]
                        while pending:
                            flush_q(pending[0:3])
                            del pending[0:3]
                    else:
                        if wb == 6:
                            for tk in txs:
                                tSG.r.extend(tk.r)
                                if tk.w is not None:
                                    tSG.r.append(tk.w)
                        for c in range(4):
                            gt = (wb - 6) * 4 + c
                            for (c0, n) in ((128, 512), (640, 512), (1152, 128)):
                                pa = grp % 3
                                grp += 1
                                rt = [tB1[j] for j in range(c0 // 128, (c0 + n) // 128)]

                                def mmg(t, c=c, c0=c0, n=n, pa=pa, Wt=Wt):
                                    for kk in range(16):
                                        ins = t.matmul(psA[pa][:, 0:n], lhsT=Wt[:, kk, c * 128:(c + 1) * 128],
                                                       rhs=B1[:, kk, c0:c0 + n], start=(kk == 0), stop=(kk == 15))
                                    return ins
                                S.op("pe", mmg, R=rt + [tWB[slot]], W=[tpsA[pa]])
                                S.op("act", lambda a, gt=gt, c0=c0, n=n, pa=pa: a.activation(
                                    out=SG[:, gt, c0 - 128:c0 - 128 + n], in_=psA[pa][:, 0:n], func=AF.Silu),
                                    R=[tpsA[pa]], W=[tSG])
                    if wb + 3 < 10:
                        load_wblock(slot, wb_src(wb + 3))
                esX.close()
            S.barrier()

            if DEBUG_PHASES >= 2:
                with ExitStack() as esB:
                    P1 = [sb("P1_%d" % i, [128, 512], BF, esB) for i in range(2)]
                    P2 = [sb("P2_%d" % i, [128, 512], BF, esB) for i in range(2)]
                    tP1 = [Tk(), Tk()]
                    tP2 = [Tk(), Tk()]
                    Dsb2 = [sb("Dsb%d" % i, [128, 512], F32, esB) for i in range(2)]
                    og2 = [sb("og%d" % i, [128, 256], F32, esB) for i in range(2)]
                    tDsb2, tog2 = [Tk(), Tk()], [Tk(), Tk()]
                    ck16 = [sb("ck16_%d" % i, [128, 512], BF, esB) for i in range(2)]
                    tck = [Tk(), Tk()]
                    KTc = sb("KTc", [128, 4, 2, 128], BF, esB)
                    Vc = sb("Vc", [128, 2, 512], BF, esB)
                    tKTc = [Tk(), Tk()]
                    tVc = [Tk(), Tk()]
                    pS = [ps("pS_%d" % i, [128, 1024], F32, esB) for i in range(2)]
                    pOD = [ps("pOD_%d" % i, [128, 512], F32, esB) for i in range(4)]
                    tS = [Tk(True), Tk(True)]
                    tOD = [Tk(True) for _ in range(4)]

                    for s in range(3):
                        S.dma("pool", WB[:, :, s * 512:(s + 1) * 512],
                              waout[:, s * 512:(s + 1) * 512].rearrange("(k p) n -> p k n", p=128), W=[tWB[s]])

                    for b in range(2):
                        S.dma("pool", ck16[b][:], ckd[b], W=[tck[b]])
                        S.dma("pool", Vc[:, b, :], cvd[b], W=[tVc[b]])
                        S.dma("sp", nk_s[b, 0:112, :], ckd[b, 16:128, :], is_output=True)
                        S.dma("sp", nv_s[b, 0:112, :], cvd[b, 16:128, :], is_output=True)

                    def prep_caches():
                        for b in range(2):
                            pSb = pS[0][:, 0:256].bitcast(BF)

                            def trc(t, pSb=pSb, b=b):
                                for c in range(4):
                                    ins = t.transpose(out=pSb[:, c * 128:(c + 1) * 128],
                                                      in_=ck16[b][:, c * 128:(c + 1) * 128], identity=identb[:])
                                return ins
                            S.op("pe", trc, R=[tck[b], tC], W=[tS[0]])
                            S.op("dve", lambda v, b=b, pSb=pSb: v.tensor_copy(
                                out=KTc[:, :, b, :], in_=pSb.rearrange("p (c n) -> p c n", c=4)),
                                R=[tS[0]], W=[tKTc[b]])

                    items = []
                    chunks = [("p", j) for j in range(2, 19)] + [("s", 0), ("s", 1)]
                    for (kind, j) in chunks:
                        if kind == "p":
                            nq = 64
                            qc0 = 64 * j - 128
                            qtile = j // 2
                            if j % 2 == 0:
                                i1, i2 = j // 2 - 1, j // 2
                            else:
                                i1, i2 = (j - 1) // 2, (j - 3) // 2
                            K1 = lambda jj, hs, i1=i1: KT[hs, jj, i1 * 128:(i1 + 1) * 128]
                            V1 = lambda cs, i1=i1: V[:, i1, cs]
                            K2 = lambda jj, hs, i2=i2: KT[hs, jj, i2 * 128:(i2 + 1) * 128]
                            V2 = lambda cs, i2=i2: V[:, i2, cs]
                            rdeps = [tKT[i1], tV[i1], tKT[i2], tV[i2], tB2[qtile], tC]
                            bcol = j
                        else:
                            b = j
                            nq = 16
                            qc0 = 1024 + 64 + 32 * b
                            qtile = 9
                            K1 = lambda jj, hs, b=b: KTc[hs, jj, b, :]
                            V1 = lambda cs, b=b: Vc[:, b, cs]
                            K2 = lambda jj, hs: KT[hs, jj, 1152:1280]
                            V2 = lambda cs: V[:, 9, cs]
                            rdeps = [tKTc[b], tVc[b], tKT[9], tV[9], tB2[9], tC]
                            bcol = 19 + b
                        pb = 0
                        for jj in range(4):
                            items.append(dict(nq=nq, qc0=qc0, qtile=qtile, pb=pb, K1=K1, V1=V1, K2=K2, V2=V2,
                                              rdeps=rdeps, bcol=bcol, jj=jj))

                    def emit_S(k):
                        I = items[k]
                        bi = k % 2
                        nq, n4, pb, jj = I["nq"], 4 * I["nq"], I["pb"], I["jj"]

                        def mmS(t):
                            for blk, Kf in ((0, I["K1"]), (1, I["K2"])):
                                for h in range(2):
                                    hs = slice(h * 64, (h + 1) * 64)
                                    q = B2[hs, jj * 4:(jj + 1) * 4, I["qc0"]:I["qc0"] + nq]
                                    c0 = 512 * h + 256 * blk
                                    ins = t.matmul(pS[bi][:, c0:c0 + n4].rearrange("p (g q) -> p g q", g=4),
                                                   lhsT=Kf(jj, hs), rhs=q, start=True, stop=True)
                            return ins
                        S.op("pe", mmS, R=I["rdeps"], W=[tS[bi]])
                        pv = pS[bi][:].rearrange("p (h c) -> p h c", h=2)
                        if I["bcol"] >= 5:
                            S.op("act", lambda a: a.activation(
                                out=P1[bi][:, 0:2 * n4].rearrange("p (h c) -> p h c", h=2), in_=pv[:, :, 0:n4],
                                func=AF.Exp, scale=0.125), R=[tS[bi]], W=[tP1[bi]])
                        else:
                            S.op("act", lambda a: a.activation(
                                out=P1[bi][:, 0:2 * n4].rearrange("p (h c) -> p h c", h=2), in_=pv[:, :, 0:n4],
                                func=AF.Exp, bias=kmA[:, I["bcol"]:I["bcol"] + 1], scale=0.125),
                                R=[tS[bi], tC], W=[tP1[bi]])
                        S.op("act", lambda a: a.activation(
                            out=P2[bi][:, 0:2 * n4].rearrange("p (h c) -> p h c", h=2),
                            in_=pv[:, :, 256:256 + n4], func=AF.Exp,
                            bias=kmB[:, I["bcol"]:I["bcol"] + 1], scale=0.125), R=[tS[bi], tC], W=[tP2[bi]])

                    def emit_PV(k):
                        I = items[k]
                        bi = k % 2
                        b4 = k % 4
                        n4, jj = 4 * I["nq"], I["jj"]

                        def mmPV(t):
                            for h in range(2):
                                hs = slice(h * 64, (h + 1) * 64)
                                cs = slice(jj * 128 + h * 64, jj * 128 + (h + 1) * 64)
                                p1 = P1[bi][:, h * n4:(h + 1) * n4]
                                p2 = P2[bi][:, h * n4:(h + 1) * n4]
                                t.matmul(pOD[b4][hs, 0:n4], lhsT=I["V1"](cs), rhs=p1, start=True, stop=False)
                                t.matmul(pOD[b4][hs, 0:n4], lhsT=I["V2"](cs), rhs=p2, start=False, stop=True)
                                t.matmul(pOD[b4][hs, 256:256 + n4], lhsT=onesb[:, 0:64], rhs=p1, start=True, stop=False)
                                ins = t.matmul(pOD[b4][hs, 256:256 + n4], lhsT=onesb[:, 0:64], rhs=p2,
                                               start=False, stop=True)
                            return ins
                        S.op("pe", mmPV, R=I["rdeps"] + [tP1[bi], tP2[bi]], W=[tOD[b4]])

                    def emit_Npair(m):
                        pi = m % 2
                        Dsb, tDsb = Dsb2[pi], tDsb2[pi]
                        ks = (2 * m, 2 * m + 1)
                        n4 = 4 * items[ks[0]]["nq"]
                        for e_, k in enumerate(ks):
                            I = items[k]
                            nq, jj = I["nq"], I["jj"]
                            esb = esk[:, jj * 4:(jj + 1) * 4].unsqueeze(2).broadcast_to([128, 4, nq])
                            S.op("dve", lambda v, e_=e_, k=k, esb=esb: v.tensor_tensor(
                                out=Dsb[:, e_ * n4:(e_ + 1) * n4].rearrange("p (g q) -> p g q", g=4),
                                in0=pOD[k % 4][:, 256:256 + n4].rearrange("p (g q) -> p g q", g=4), in1=esb, op=OP.add),
                                R=[tOD[k % 4], tC], W=[tDsb])
                        S.op("act", lambda a: a.activation(out=Dsb[:, 0:2 * n4], in_=Dsb[:, 0:2 * n4], func=AF.Ln),
                             W=[tDsb])
                        S.op("act", lambda a: a.activation(out=Dsb[:, 0:2 * n4], in_=Dsb[:, 0:2 * n4], func=AF.Exp,
                                                           scale=-1.0), W=[tDsb])
                        for e_, k in enumerate(ks):
                            I = items[k]
                            nq, jj, qc0 = I["nq"], I["jj"], I["qc0"]
                            og, tog = og2[e_], tog2[e_]
                            S.op("dve", lambda v, e_=e_, k=k, og=og: v.tensor_tensor(
                                out=og[:, 0:n4], in0=pOD[k % 4][:, 0:n4], in1=Dsb[:, e_ * n4:(e_ + 1) * n4], op=OP.mult),
                                R=[tOD[k % 4], tDsb], W=[tog])
                            S.op("pool", lambda g, og=og, jj=jj, nq=nq, qc0=qc0: g.tensor_tensor(
                                out=B1[:, jj * 4:(jj + 1) * 4, 128 + qc0:128 + qc0 + nq],
                                in0=og[:, 0:n4].rearrange("p (g q) -> p g q", g=4),
                                in1=SG[:, jj * 4:(jj + 1) * 4, qc0:qc0 + nq], op=OP.mult),
                                R=[tog, tSG], W=[tB1[I["qtile"]]])

                    npair = [0]
                    emit_S(0)
                    for k in range(len(items)):
                        if k == 40:
                            prep_caches()
                        if k + 1 < len(items):
                            emit_S(k + 1)
                        emit_PV(k)
                        while 2 * npair[0] + 1 <= k - 1:
                            emit_Npair(npair[0])
                            npair[0] += 1
                    while npair[0] < len(items) // 2:
                        emit_Npair(npair[0])
                        npair[0] += 1
        S.barrier()

        tX1T = Tk()
        if DEBUG_PHASES >= 3:
            with ExitStack() as esC:
                W4 = sb("W4", [128, 16, 512], BF, esC)
                S.dma("pool", W4[:], waout[:, 1536:2048].rearrange("(k p) n -> p k n", p=128), W=[tWB[3]])
                Wc = [WB[:, :, 0:512], WB[:, :, 512:1024], WB[:, :, 1024:1536], W4[:]]
                lnG = sb("lnG", [128, 2048], F32, esC)
                lnB = sb("lnB", [128, 2048], F32, esC)
                tLN = Tk()
                S.dma("sp", lnG[:], plgd[0], W=[tLN])
                S.dma("sp", lnB[:], plbd[0], W=[tLN])
                x32 = [sb("x32_%d" % i, [128, 2048], F32, esC) for i in range(2)]
                z32 = [sb("z32_%d" % i, [128, 2048], F32, esC) for i in range(2)]
                x1b = [sb("x1b%d" % i, [128, 2048], BF, esC) for i in range(2)]
                tx32 = [Tk(), Tk()]
                tz32 = [Tk(), Tk()]
                tx1b = [Tk(), Tk()]
                stt = sb("stt", [128, 4, 6], F32, esC)
                mv = sb("mv", [128, 2], F32, esC)
                sd = sb("sd", [128, 1], F32, esC)
                rstd = sb("rstd", [128, 1], F32, esC)
                tst = Tk()
                pC = [ps("pC%d" % i, [128, 512], F32, esC) for i in range(4)]
                tpC = [Tk(True) for _ in range(4)]
                pT = ps("pTc", [128, 2048], BF, esC)
                tpT = Tk(True)
                def xloadC(i, dst, tk):
                    S.dma("sp", dst[:], xcat[i * 128:(i + 1) * 128, :], W=[tk])

                def postC(i, bi, when):
                    if when == "early":
                        S.dma("sp", x1s[(i - 1) * 128:i * 128, :], z32[bi][:], R=[tz32[bi]], W=[tX1S[i]])
                        S.op("act", lambda a: a.activation(out=x1b[i % 2][:], in_=z32[bi][:], func=AF.Copy),
                             R=[tz32[bi]], W=[tx1b[i % 2]])
                        return

                    def trx(t):
                        for kk in range(16):
                            ins = t.transpose(out=pT[:, kk * 128:(kk + 1) * 128],
                                              in_=x1b[i % 2][:, kk * 128:(kk + 1) * 128], identity=identb[:])
                        return ins
                    S.op("pe", trx, R=[tx1b[i % 2], tC], W=[tpT])
                    pv = pT[:].rearrange("p (k n) -> p k n", k=16)
                    if i < 9:
                        S.op("act", lambda a: a.activation(out=B2[:, :, (i - 1) * 128:i * 128], in_=pv, func=AF.Copy),
                             R=[tpT], W=[tX1T])
                    else:
                        def ev9(a):
                            a.activation(out=B2[:, :, 1024:1088], in_=pv[:, :, 0:64], func=AF.Copy)
                            a.activation(out=B2[:, :, 1118:1134], in_=pv[:, :, 64:80], func=AF.Copy)
                            return a.activation(out=B2[:, :, 1164:1180], in_=pv[:, :, 96:112], func=AF.Copy)
                        S.op("act", ev9, R=[tpT], W=[tX1T])
                for tk in tB2:
                    tX1T.r.extend(tk.r)
                    if tk.w is not None:
                        tX1T.r.append(tk.w)
                S.op("dve", lambda v: v.memset(B2[:, :, 1088:1184], 0.0), W=[tX1T])
                layer_tail(nc, S, tiles=[(i, i * 128, 128) for i in range(1, 10)], actT=B1,
                           tact=lambda i: [tB1[i]], Wc=Wc, tW=tWB, xload=xloadC, x32=x32, tx32=tx32,
                           z32=z32, tz32=tz32, pC=pC, tpC=tpC, lnG=lnG, lnB=lnB, tLN=tLN, stt=stt, mv=mv,
                           sd=sd, rstd=rstd, tst=tst, epst=epst, tC=tC, post=postC)
        S.barrier()
        if DEBUG_PHASES >= 4:
            phase_D(nc, S, sb, ps, locals())
        S.finish()
    print('total ops', S.nops)
    return nc


def layer_tail(nc, S, tiles, actT, tact, Wc, tW, xload, x32, tx32, z32, tz32, pC, tpC, lnG, lnB, tLN,
               stt, mv, sd, rstd, tst, epst, tC, post):
    pending = None
    for n_, (i, c0, m) in enumerate(tiles):
        bi = n_ % 2
        xload(i, x32[bi], tx32[bi])
        for cg in range(4):
            def mm(t, cg=cg, c0=c0, m=m):
                for kk in range(16):
                    ins = t.matmul(pC[cg][0:m, :], lhsT=actT[:, kk, c0:c0 + m], rhs=Wc[cg][:, kk, :],
                                   start=(kk == 0), stop=(kk == 15))
                return ins
            S.op("pe", mm, R=tact(i) + [tW[cg]], W=[tpC[cg]])
            S.op("dve", lambda v, cg=cg, bi=bi, m=m: v.scalar_tensor_tensor(
                out=z32[bi][0:m, cg * 512:(cg + 1) * 512], in0=x32[bi][0:m, cg * 512:(cg + 1) * 512],
                scalar=ALPHA, in1=pC[cg][0:m, :], op0=OP.mult, op1=OP.add),
                R=[tpC[cg], tx32[bi]], W=[tz32[bi]])
        if pending is not None:
            pending()
            pending = None

        def stats(v, bi=bi, m=m):
            for cg in range(4):
                ins = v.bn_stats(out=stt[0:m, cg, :], in_=z32[bi][0:m, cg * 512:(cg + 1) * 512])
            return ins
        S.op("dve", stats, R=[tz32[bi]], W=[tst])
        S.op("dve", lambda v, m=m: v.bn_aggr(out=mv[0:m, :], in_=stt[0:m, :, :].rearrange("p a b -> p (a b)")),
             W=[tst])
        S.op("act", lambda a, m=m: a.activation(out=sd[0:m, :], in_=mv[0:m, 1:2], func=AF.Sqrt,
                                                bias=epst[0:m, :], scale=1.0), R=[tC], W=[tst])
        S.op("dve", lambda v, m=m: v.reciprocal(out=rstd[0:m, :], in_=sd[0:m, :]), W=[tst])
        S.op("dve", lambda v, bi=bi, m=m: v.tensor_scalar(
            out=z32[bi][0:m, :], in0=z32[bi][0:m, :], scalar1=mv[0:m, 0:1], scalar2=rstd[0:m, 0:1],
            op0=OP.subtract, op1=OP.mult), R=[tst], W=[tz32[bi]])
        S.op("pool", lambda g, bi=bi, m=m: g.tensor_tensor(out=z32[bi][0:m, :], in0=z32[bi][0:m, :],
                                                          in1=lnG[0:m, :], op=OP.mult), R=[tLN], W=[tz32[bi]])
        S.op("pool", lambda g, bi=bi, m=m: g.tensor_tensor(out=z32[bi][0:m, :], in0=z32[bi][0:m, :],
                                                          in1=lnB[0:m, :], op=OP.add), R=[tLN], W=[tz32[bi]])
        pending = (lambda i=i, bi=bi: (post(i, bi, "early"), post(i, bi, "late")))
    if pending is not None:
        pending()


tX1S = [Tk() for _ in range(11)]


def phase_D(nc, S, sb, ps, L):
    from contextlib import ExitStack
    B1, B2, WB = L["B1"], L["B2"], L["WB"]
    tB1, tWB, tX1T, tC = L["tB1"], L["tWB"], L["tX1T"], L["tC"]
    identb, ident32, ones32 = L["identb"], L["ident32"], L["ones32"]
    wdw, bdw, clg, clb, hv, epst = L["wdw"], L["bdw"], L["clg"], L["clb"], L["hv"], L["epst"]
    cwin, cwout, std, nc_p, nc_s, y_p, y_s, x1s = (L["cwin"], L["cwout"], L["std"], L["nc_p"], L["nc_s"],
                                                   L["y_p"], L["y_s"], L["x1s"])
    plgd, plbd = L["plgd"], L["plbd"]
    tSGc = [Tk() for _ in range(16)]
    tU = [Tk() for _ in range(16)]
    tCc = [Tk() for _ in range(16)]
    tWs = [Tk() for _ in range(4)]
    tUT = Tk()
    blocks = ((34, 512), (546, 512), (1058, 122))
    oblocks = ((64, 512), (576, 512), (1088, 92))
    NOUT = 1116

    def wsl(s):
        return WB[:, :, s * 384:(s + 1) * 384]

    def load_cw(ct):
        s = ct % 4
        S.dma("pool", wsl(s)[:, :, 0:256], cwin[:, ct * 384:ct * 384 + 256].rearrange("(k p) n -> p k n", p=128),
              W=[tWs[s]])

    with ExitStack() as esD:
        utail = sb("utail", [128, 16, 62], F32, esD)
        esU = ExitStack()
        U16 = sb("U16", [128, 16, L1W], BF, esU)
        with ExitStack() as esD1:
            for s in range(4):
                tWs[s].w = None
            for s in range(3):
                for ws in tWs:
                    ws.r.extend(tWB[s].r)
                    if tWB[s].w is not None:
                        ws.r.append(tWB[s].w)
            for ct in range(16):
                for tk in tB1:
                    tSGc[ct].r.extend(tk.r)
                    if tk.w is not None:
                        tSGc[ct].r.append(tk.w)
            for ct in range(4):
                load_cw(ct)
            stT = sb("stT", [128, 16, 2, 30], F32, esD1)
            st32 = [sb("st32_%d" % i, [30, 2048], F32, esD1) for i in range(2)]
            tst32, tstT = [Tk(), Tk()], Tk()
            for b in range(2):
                S.dma("sp", st32[b][:], std[b], W=[tst32[b]])
                S.dma("sp", nc_s[b, 0:14, :], std[b, 16:30, :], is_output=True)
            sig32 = [sb("sig32_%d" % i, [128, 512], F32, esD1) for i in range(2)]
            tsig = [Tk(), Tk()]
            pa = [ps("pa%d" % i, [128, 512], F32, esD1) for i in range(2)]
            pb_ = [ps("pb%d" % i, [128, 512], F32, esD1) for i in range(2)]
            tpa, tpb = [Tk(True), Tk(True)], [Tk(True), Tk(True)]
            pst = ps("pst", [128, 512], F32, esD1)
            tpst = Tk(True)
            def prep_state():
                for b in range(2):
                    def trs(t, b=b):
                        for ct in range(16):
                            ins = t.transpose(out=pst[:, ct * 30:(ct + 1) * 30],
                                              in_=st32[b][:, ct * 128:(ct + 1) * 128], identity=ident32[0:30, 0:30])
                        return ins
                    S.op("pe", trs, R=[tst32[b], tC], W=[tpst])
                    S.op("dve", lambda v, b=b: v.tensor_copy(out=stT[:, :, b, :],
                                                             in_=pst[:, 0:480].rearrange("p (c n) -> p c n", c=16)),
                         R=[tpst], W=[tstT])
            it = 0
            for ct in range(16):
                s = ct % 4
                Wt = wsl(s)
                for (c0, n) in blocks:
                    bi = it % 2
                    it += 1
                    for part, (pp, tp) in enumerate(((pa, tpa), (pb_, tpb))):
                        def mm(t, part=part, pp=pp, bi=bi, c0=c0, n=n, Wt=Wt):
                            for kk in range(16):
                                ins = t.matmul(pp[bi][:, 0:n], lhsT=Wt[:, kk, part * 128:(part + 1) * 128],
                                               rhs=B2[:, kk, c0:c0 + n], start=(kk == 0), stop=(kk == 15))
                            return ins
                        S.op("pe", mm, R=[tX1T, tWs[s]], W=[tp[bi]])
                    S.op("act", lambda a, bi=bi, n=n: a.activation(out=sig32[bi][:, 0:n], in_=pb_[bi][:, 0:n],
                                                                   func=AF.Sigmoid), R=[tpb[bi]], W=[tsig[bi]])
                    S.op("dve", lambda v, bi=bi, n=n, c0=c0, ct=ct: v.tensor_tensor(
                        out=U16[:, ct, c0:c0 + n], in0=pa[bi][:, 0:n], in1=sig32[bi][:, 0:n], op=OP.mult),
                        R=[tpa[bi], tsig[bi]], W=[tU[ct]])
                    if c0 == 1058:
                        def tails(v, bi=bi, ct=ct):
                            v.tensor_tensor(out=utail[:, ct, 0:30], in0=pa[bi][:, 0:30], in1=sig32[bi][:, 0:30], op=OP.mult)
                            v.tensor_tensor(out=utail[:, ct, 30:46], in0=pa[bi][:, 60:76], in1=sig32[bi][:, 60:76], op=OP.mult)
                            return v.tensor_tensor(out=utail[:, ct, 46:62], in0=pa[bi][:, 106:122],
                                                   in1=sig32[bi][:, 106:122], op=OP.mult)
                        S.op("dve", tails, R=[tpa[bi], tsig[bi]], W=[tUT])
                if ct == 0:
                    prep_state()
                def fix(g, ct=ct):
                    g.tensor_copy(out=U16[:, ct, 1088:1118], in_=stT[:, ct, 0, :])
                    g.tensor_copy(out=U16[:, ct, 1134:1164], in_=stT[:, ct, 1, :])
                    return g.tensor_scalar(out=U16[:, ct, 34:64], in0=U16[:, ct, 34:64], scalar1=hv[:, 0:1],
                                           scalar2=1.0, op0=OP.mult, op1=OP.mult)
                S.op("pool", fix, R=[tstT, tC], W=[tU[ct]])
                if ct + 4 < 16:
                    load_cw(ct + 4)
        S.barrier()
        with ExitStack() as esD2:
            tWo = [Tk() for _ in range(4)]
            for s in range(3):
                for ws in tWs:
                    tWo[s].r.extend(ws.r)
                S.dma("pool", WB[:, :, s * 512:(s + 1) * 512],
                      cwout[:, s * 512:(s + 1) * 512].rearrange("(k p) n -> p k n", p=128), W=[tWo[s]])
            S1 = sb("S1", [128, NOUT], F32, esD2)
            S2 = sb("S2", [128, NOUT], F32, esD2)
            tS1, tS2 = Tk(), Tk()
            esD2i = ExitStack()
            diag = [sb("diag%d" % i, [128, 31, 128], BF, esD2i) for i in range(2)]
            tdg = [Tk(), Tk()]
            sq = [sb("sq%d" % i, [128, 512], F32, esD2i) for i in range(2)]
            tsq = [Tk(), Tk()]
            S.op("dve", lambda v: v.memset(S1[:], 0.0), W=[tS1])
            S.op("dve", lambda v: v.memset(S2[:], 0.0), W=[tS2])
            with ExitStack() as esD2p:
                pc = [ps("pc%d" % i, [128, 512], F32, esD2p) for i in range(6)]
                tpc = [Tk(True) for _ in range(6)]
                it = 0
                for ct in range(16):
                    d = ct % 2

                    def mkdiag(g, ct=ct, d=d):
                        for tap in range(31):
                            ins = g.tensor_scalar(out=diag[d][:, tap, :], in0=identb[:],
                                                  scalar1=wdw[:, ct, tap:tap + 1], scalar2=1.0, op0=OP.mult,
                                                  op1=OP.mult)
                        return ins
                    S.op("dve" if ct == 0 else "pool", mkdiag, R=[tC], W=[tdg[d]])
                    for (o0, n) in reversed(oblocks):
                        bi = it % 6
                        sqi = it % 2
                        it += 1

                        def mmc(t, ct=ct, d=d, o0=o0, n=n, bi=bi):
                            for tap in range(31):
                                ins = t.matmul(pc[bi][:, 0:n], lhsT=diag[d][:, tap, :],
                                               rhs=U16[:, ct, o0 + tap - 30:o0 + tap - 30 + n],
                                               start=(tap == 0), stop=(tap == 30))
                            return ins
                        S.op("pe", mmc, R=[tdg[d], tU[ct]], W=[tpc[bi]])
                        S.op("act", lambda a, ct=ct, o0=o0, n=n, bi=bi: a.activation(
                            out=U16[:, ct, o0:o0 + n], in_=pc[bi][:, 0:n], func=AF.Identity,
                            bias=bdw[:, ct:ct + 1], scale=1.0), R=[tpc[bi], tC], W=[tCc[ct]])
                        S.op("act", lambda a, ct=ct, n=n, bi=bi, sqi=sqi: a.activation(
                            out=sq[sqi][:, 0:n], in_=pc[bi][:, 0:n], func=AF.Square,
                            bias=bdw[:, ct:ct + 1], scale=1.0), R=[tpc[bi], tC], W=[tsq[sqi]])
                        S.op("dve", lambda v, ct=ct, o0=o0, n=n, bi=bi: v.scalar_tensor_tensor(
                            out=S1[:, o0 - 64:o0 - 64 + n], in0=pc[bi][:, 0:n], scalar=bdw[:, ct:ct + 1],
                            in1=S1[:, o0 - 64:o0 - 64 + n], op0=OP.add, op1=OP.add), R=[tpc[bi], tC], W=[tS1])
                        S.op("dve", lambda v, o0=o0, n=n, sqi=sqi: v.tensor_tensor(
                            out=S2[:, o0 - 64:o0 - 64 + n], in0=S2[:, o0 - 64:o0 - 64 + n], in1=sq[sqi][:, 0:n],
                            op=OP.add), R=[tsq[sqi]], W=[tS2])
            esD2i.close()
            S.barrier()
            with ExitStack() as esD3:
                tt = [sb("ttm%d" % i, [128, NOUT], F32, esD3) for i in range(2)]
                ttt = [Tk(), Tk()]
                pm = [ps("pm%d" % i, [128, 512], F32, esD3) for i in range(6)]
                tpm = Tk(True)
                GW = [sb("GW%d" % i, [128, 16, 128], BF, esD3) for i in range(2)]
                tGW = [Tk(), Tk()]
                pg = [ps("pg%d" % i, [128, 512], F32, esD3) for i in range(2)]
                tpg = [Tk(True), Tk(True)]

                def load_gw(ct):
                    S.dma("pool", GW[ct % 2][:],
                          cwin[:, ct * 384 + 256:ct * 384 + 384].rearrange("(k p) n -> p k n", p=128), W=[tGW[ct % 2]])

                gcnt = [0]

                def gate_proj(ct):
                    for (c0, n) in blocks:
                        bi = gcnt[0] % 2
                        gcnt[0] += 1

                        def mmg(t, bi=bi, c0=c0, n=n, ct=ct):
                            for kk in range(16):
                                ins = t.matmul(pg[bi][:, 0:n], lhsT=GW[ct % 2][:, kk, :], rhs=B2[:, kk, c0:c0 + n],
                                               start=(kk == 0), stop=(kk == 15))
                            return ins
                        S.op("pe", mmg, R=[tX1T, tGW[ct % 2]], W=[tpg[bi]])
                        S.op("act", lambda a, bi=bi, n=n, c0=c0, ct=ct: a.activation(
                            out=B1[:, ct, c0:c0 + n], in_=pg[bi][:, 0:n], func=AF.Silu),
                            R=[tpg[bi]], W=[tSGc[ct]])
                    if ct + 2 < 16:
                        load_gw(ct + 2)
                load_gw(0)
                load_gw(1)
                for ct in range(4):
                    gate_proj(ct)

                def mms(t):
                    for k_, (o0, n) in enumerate(oblocks):
                        t.matmul(pm[k_][:, 0:n], lhsT=ones32[:], rhs=S1[:, o0 - 64:o0 - 64 + n], start=True, stop=True)
                        ins = t.matmul(pm[3 + k_][:, 0:n], lhsT=ones32[:], rhs=S2[:, o0 - 64:o0 - 64 + n],
                                       start=True, stop=True)
                    return ins
                S.op("pe", mms, R=[tS1, tS2, tC], W=[tpm])

                def st1(v):
                    for k_, (o0, n) in enumerate(oblocks):
                        sl = slice(o0 - 64, o0 - 64 + n)
                        v.tensor_scalar(out=S1[:, sl], in0=pm[k_][:, 0:n], scalar1=1.0 / 2048, scalar2=None, op0=OP.mult)
                        ins = v.tensor_scalar(out=S2[:, sl], in0=pm[3 + k_][:, 0:n], scalar1=1.0 / 2048, scalar2=None,
                                              op0=OP.mult)
                    return ins
                S.op("dve", st1, R=[tpm], W=[tS1, tS2])
                S.op("dve", lambda v: v.tensor_tensor(out=tt[0][:], in0=S1[:], in1=S1[:], op=OP.mult),
                     R=[tS1], W=[ttt[0]])
                S.op("dve", lambda v: v.tensor_tensor(out=S2[:], in0=S2[:], in1=tt[0][:], op=OP.subtract),
                     R=[ttt[0]], W=[tS2])
                S.op("act", lambda a: a.activation(out=S2[:], in_=S2[:], func=AF.Ln, bias=epst[:, 0:1], scale=1.0),
                     R=[tC], W=[tS2])
                S.op("act", lambda a: a.activation(out=tt[0][:], in_=S2[:], func=AF.Exp, scale=-0.5),
                     R=[tS2], W=[ttt[0]])
                S.op("dve", lambda v: v.tensor_tensor(out=S2[:], in0=S1[:], in1=tt[0][:], op=OP.mult),
                     R=[tS1, ttt[0]], W=[tS2])
                tpr = Tk()

                def wr_ps(v):
                    for k_, (o0, n) in enumerate(oblocks):
                        sl = slice(o0 - 64, o0 - 64 + n)
                        v.tensor_copy(out=pm[k_][:, 0:n], in_=tt[0][:, sl])
                        ins = v.tensor_copy(out=pm[3 + k_][:, 0:n], in_=S2[:, sl])
                    return ins
                S.op("dve", wr_ps, R=[ttt[0], tS2], W=[tpm, tpr])
                def apply_ct(ct):
                    bi = ct % 2

                    def a1(v, ct=ct, bi=bi):
                        for k_, (o0, n) in enumerate(oblocks):
                            sl = slice(o0 - 64, o0 - 64 + n)
                            ins = v.tensor_tensor(out=tt[bi][:, sl], in0=U16[:, ct, o0:o0 + n], in1=pm[k_][:, 0:n],
                                                  op=OP.mult)
                        return ins
                    S.op("dve", a1, R=[tCc[ct], tpr], W=[ttt[bi]])

                    def a2(v, bi=bi):
                        for k_, (o0, n) in enumerate(oblocks):
                            sl = slice(o0 - 64, o0 - 64 + n)
                            ins = v.tensor_tensor(out=tt[bi][:, sl], in0=tt[bi][:, sl], in1=pm[3 + k_][:, 0:n],
                                                  op=OP.subtract)
                        return ins
                    S.op("dve", a2, R=[tpr], W=[ttt[bi]])
                    S.op("act", lambda a, ct=ct, bi=bi: a.activation(
                        out=tt[bi][:], in_=tt[bi][:], func=AF.Silu, bias=clb[:, ct:ct + 1], scale=clg[:, ct:ct + 1]),
                        R=[tC], W=[ttt[bi]])
                    S.op("pool", lambda g, ct=ct, bi=bi: g.tensor_tensor(
                        out=B1[:, ct, 64:64 + NOUT], in0=tt[bi][:], in1=B1[:, ct, 64:64 + NOUT], op=OP.mult),
                        R=[ttt[bi]], W=[tSGc[ct]])

                next_g = 4
                for ct in range(16):
                    apply_ct(ct)
                    if next_g < 16 and (next_g - ct <= 1 or ct % 2 == 0):
                        gate_proj(next_g)
                        next_g += 1
        esU.close()
        S.barrier()
        with ExitStack() as esE:
            W4 = sb("W4e", [128, 16, 512], BF, esE)
            S.dma("pool", W4[:], cwout[:, 1536:2048].rearrange("(k p) n -> p k n", p=128), W=[tWo[3]])
            Wc = [WB[:, :, 0:512], WB[:, :, 512:1024], WB[:, :, 1024:1536], W4[:]]
            lnG = sb("lnGe", [128, 2048], F32, esE)
            lnB = sb("lnBe", [128, 2048], F32, esE)
            tLN = Tk()
            S.dma("sp", lnG[:], plgd[1], W=[tLN])
            S.dma("sp", lnB[:], plbd[1], W=[tLN])
            x32 = [sb("x32e_%d" % i, [128, 2048], F32, esE) for i in range(2)]
            z32 = [sb("z32e_%d" % i, [128, 2048], F32, esE) for i in range(2)]
            tx32 = [Tk(), Tk()]
            tz32 = [Tk(), Tk()]
            stt = sb("stte", [128, 4, 6], F32, esE)
            mv = sb("mve", [128, 2], F32, esE)
            sd = sb("sde", [128, 1], F32, esE)
            rstd = sb("rstde", [128, 1], F32, esE)
            tst = Tk()
            pC = [ps("pCe%d" % i, [128, 512], F32, esE) for i in range(4)]
            tpC = [Tk(True) for _ in range(4)]
            pU = ps("pU", [32, 2048], F32, esE)
            tpU = Tk(True)
            ut_sb = sb("ut_sb", [32, 2048], F32, esE)
            tut = Tk()
            for (c0, n, dst) in ((0, 30, nc_p[:, :]), (30, 16, nc_s[0, 14:30, :]), (46, 16, nc_s[1, 14:30, :])):
                def tru(t, c0=c0, n=n):
                    for ct in range(16):
                        ins = t.transpose(out=pU[0:n, ct * 128:(ct + 1) * 128], in_=utail[:, ct, c0:c0 + n],
                                          identity=ident32[:])
                    return ins
                S.op("pe", tru, R=[tUT, tC], W=[tpU])
                S.op("dve", lambda v, n=n: v.tensor_copy(out=ut_sb[0:n, :], in_=pU[0:n, :]), R=[tpU], W=[tut])
                S.dma("sp", dst, ut_sb[0:n, :], R=[tut], is_output=True)

            tiles = [(m, 64 + 128 * m, 128) for m in range(8)] + [(8, 1118, 62)]

            def xloadE(m, dst, tk):
                if m < 8:
                    S.dma("sp", dst[:], x1s[128 * m + 64:128 * m + 192, :], W=[tk])
                else:
                    S.dma("sp", dst[0:16, :], x1s[1088:1104, :], W=[tk])
                    S.dma("sp", dst[46:62, :], x1s[1120:1136, :], W=[tk])

            def postE(m, bi, when):
                if when == "late":
                    return
                if m < 8:
                    S.dma("sp", y_p[128 * m:128 * (m + 1), :], z32[bi][:], R=[tz32[bi]], is_output=True)
                else:
                    S.dma("sp", y_s[0], z32[bi][0:16, :], R=[tz32[bi]], is_output=True)
                    S.dma("sp", y_s[1], z32[bi][46:62, :], R=[tz32[bi]], is_output=True)
            for tk in tX1S:
                if tk.w is not None:
                    for t_ in tx32:
                        t_.r.append(tk.w)
            layer_tail(nc, S, tiles=tiles, actT=B1, tact=lambda m: list(tSGc), Wc=Wc, tW=tWo, xload=xloadE,
                       x32=x32, tx32=tx32, z32=z32, tz32=tz32, pC=pC, tpC=tpC, lnG=lnG, lnB=lnB, tLN=tLN,
                       stt=stt, mv=mv, sd=sd, rstd=rstd, tst=tst, epst=epst, tC=tC, post=postE)


class _AllTk:
    def __init__(self, tks):
        self.tks = tks

    @property
    def w(self):
        return None

    @property
    def r(self):
        return _Sink()


class _Sink:
    def append(self, x):
        pass


def _perm_q():
    perm = np.zeros(2048, dtype=np.int64)
    for j in range(4):
        for g in range(4):
            for half in range(2):
                for d in range(64):
                    perm[(j * 4 + g) * 128 + half * 64 + d] = ((2 * j + half) * 4 + g) * 64 + d
    return perm


_NC_CACHE = {}


def kernel(x_prompt, x_sample, cache_k, cache_v, state_conv, attn_w_in, attn_sink, attn_w_out,
           conv_w_in, conv_w_dw, conv_b_dw, conv_ln_g, conv_ln_b, conv_w_out, post_ln_g, post_ln_b):
    f = np.float32
    x_prompt = np.asarray(x_prompt, f)
    x_sample = np.asarray(x_sample, f)
    cache_k = np.asarray(cache_k, f)
    cache_v = np.asarray(cache_v, f)
    state_conv = np.asarray(state_conv, f)
    perm = _perm_q()
    w_in = np.asarray(attn_w_in, f)[0]
    wina = np.ascontiguousarray(np.concatenate(
        [w_in[:, perm], w_in[:, 2048:3072], w_in[:, 3072 + perm]], axis=1))
    waout = np.ascontiguousarray(np.asarray(attn_w_out, f)[0][perm, :])
    cw = np.asarray(conv_w_in, f)[0]
    cwin = np.ascontiguousarray(cw.reshape(2048, 3, 16, 128).transpose(0, 2, 1, 3).reshape(2048, 6144))
    cwout = np.ascontiguousarray(np.asarray(conv_w_out, f)[0])
    sink = np.asarray(attn_sink, f)[0]
    sinkl = np.zeros((128, 16), f)
    for p in range(128):
        for j in range(4):
            for g in range(4):
                sinkl[p, j * 4 + g] = sink[(2 * j + p // 64) * 4 + g]
    wdw = np.ascontiguousarray(np.asarray(conv_w_dw, f)[0].reshape(31, 16, 128).transpose(2, 1, 0).reshape(128, 16 * 31))
    lay = lambda v: np.ascontiguousarray(np.asarray(v, f)[0].reshape(16, 128).T)
    bdw, clg, clb = lay(conv_b_dw), lay(conv_ln_g), lay(conv_ln_b)
    plg = np.ascontiguousarray(np.broadcast_to(np.asarray(post_ln_g, f)[:, None, :], (2, 128, 2048)))
    plb = np.ascontiguousarray(np.broadcast_to(np.asarray(post_ln_b, f)[:, None, :], (2, 128, 2048)))
    ident = np.eye(128, dtype=f)
    half = 8
    inv = (500000.0 ** (-np.arange(half, dtype=np.float64) * (2.0 / 16))).astype(np.float32)

    in_maps = []
    for c in range(NCORES):
        s = 1024 * c
        xcat = np.zeros((NSLOT, 2048), f)
        pos = np.zeros(NSLOT, np.float64)
        lo = s - 192
        for slot in range(1216):
            tok = lo + slot
            if tok >= 0:
                pos[slot] = tok
        a0 = max(lo, 0)
        xcat[a0 - lo:1216, :] = x_prompt[0, a0:s + 1024, :]
        for b in range(2):
            r0 = 1216 + 32 * b
            xcat[r0:r0 + 16, :] = x_sample[2 * c + b]
            pos[r0:r0 + 16] = 1024 + np.arange(16)
        ang = pos.astype(np.float32)[:, None] * inv[None, :]
        cosd = np.cos(ang).astype(f).reshape(10, 128, 8).transpose(1, 0, 2).reshape(128, 80)
        sind = np.sin(ang).astype(f).reshape(10, 128, 8).transpose(1, 0, 2).reshape(128, 80)
        kmA = np.zeros((128, 24), f)
        kmB = np.zeros((128, 24), f)
        cvalid = lambda j: (lo + 64 * j) >= 0
        for j in range(2, 19):
            if j % 2 == 0:
                kmA[0:64, j] = 0.0 if cvalid(j - 2) else NEG
                kmA[64:128, j] = 0.0 if cvalid(j - 1) else NEG
                kmB[0:64, j] = 0.0 if cvalid(j) else NEG
                kmB[64:128, j] = NEG
            else:
                kmA[0:64, j] = 0.0 if cvalid(j - 1) else NEG
                kmA[64:128, j] = 0.0 if cvalid(j) else NEG
                kmB[0:64, j] = NEG
                kmB[64:128, j] = 0.0 if cvalid(j - 2) else NEG
        for b in range(2):
            kmB[:, 19 + b] = NEG
            kmB[64 + 32 * b:80 + 32 * b, 19 + b] = 0.0
        hv = np.full((128, 1), 1.0 if c > 0 else 0.0, f)
        in_maps.append({
            "xcat": xcat, "wina": wina, "waout": waout, "cwin": cwin, "cwout": cwout,
            "cosd": np.ascontiguousarray(cosd), "sind": np.ascontiguousarray(sind), "kmA": kmA, "kmB": kmB,
            "hv": hv, "sinkl": sinkl,
            "ck": np.ascontiguousarray(cache_k[0, 2 * c:2 * c + 2].reshape(2, 128, 512)),
            "cv": np.ascontiguousarray(cache_v[0, 2 * c:2 * c + 2].reshape(2, 128, 512)),
            "st": np.ascontiguousarray(state_conv[0, 2 * c:2 * c + 2]),
            "wdw": wdw, "bdw": bdw, "clg": clg, "clb": clb, "plg": plg, "plb": plb, "ident": ident,
        })
    if "nc" not in _NC_CACHE:
        _NC_CACHE["nc"] = build_nc()
    nc = _NC_CACHE["nc"]
    res = run_bass_kernel_spmd(nc, in_maps, core_ids=list(range(NCORES)))
    R = res.results
    y_prompt = np.concatenate([R[c]["y_p"] for c in range(NCORES)], axis=0)[None]
    y_sample = np.concatenate([R[c]["y_s"] for c in range(NCORES)], axis=0)
    nkp = R[7]["nk_p"].reshape(1, 1, 128, 8, 64)
    nvp = R[7]["nv_p"].reshape(1, 1, 128, 8, 64)
    nks = np.concatenate([R[c]["nk_s"] for c in range(NCORES)], axis=0).reshape(1, 16, 128, 8, 64)
    nvs = np.concatenate([R[c]["nv_s"] for c in range(NCORES)], axis=0).reshape(1, 16, 128, 8, 64)
    ncp = R[7]["nc_p"].reshape(1, 1, 30, 2048)
    ncs = np.concatenate([R[c]["nc_s"] for c in range(NCORES)], axis=0).reshape(1, 16, 30, 2048)
    return (y_prompt.astype(f), y_sample.astype(f), nkp.astype(f), nvp.astype(f), nks.astype(f),
            nvs.astype(f), ncp.astype(f), ncs.astype(f))
```

```python
import numpy as np
import concourse.bass as bass
import concourse.mybir as mybir
from concourse.bass_utils import run_bass_kernel_spmd

F32 = mybir.dt.float32
BF = mybir.dt.bfloat16
AF = mybir.ActivationFunctionType
OP = mybir.AluOpType

NCORES = 8
ALPHA = float((2.0 * 2) ** 0.25)
EPS = 1e-5
NEG = -30000.0
NSLOT = 1280
L1W = 1180
DEBUG_PHASES = 99
OP_LIMIT = 10 ** 9


class Tk:
    __slots__ = ("w", "r", "excl")

    def __init__(self, excl=False):
        self.w = None
        self.r = []
        self.excl = excl


class Sched:
    def __init__(self, nc):
        self.nc = nc
        self.eng = {}
        for name, h in (("pe", nc.tensor), ("act", nc.scalar), ("dve", nc.vector),
                        ("pool", nc.gpsimd), ("sp", nc.sync)):
            self.eng[name] = {"h": h, "sem": nc.alloc_semaphore(name="s_" + name), "cnt": 0,
                              "waited": {}, "dsems": [], "dnext": 0}
        for name, n in (("sp", 8), ("pool", 6), ("act", 4)):
            e = self.eng[name]
            for i in range(n):
                e["dsems"].append([nc.alloc_semaphore(name="d_%s%d" % (name, i)), 0])
        self.out_tokens = []
        self.nops = 0
        self.limit = OP_LIMIT

    def _wait(self, e, tok):
        if tok is None:
            return
        sem, val = tok
        key = id(sem)
        if e["waited"].get(key, 0) >= val:
            return
        e["h"].wait_ge(sem, val)
        e["waited"][key] = val

    def _deps(self, e, R, W):
        need = {}

        def add(tok):
            if tok is None:
                return
            k = id(tok[0])
            if k not in need or need[k][1] < tok[1]:
                need[k] = tok
        for t in R:
            add(t.w)
        for t in W:
            add(t.w)
            for rt in t.r:
                add(rt)
            if len(t.r) > 8:
                mx = {}
                for rt in t.r:
                    k = id(rt[0])
                    if k not in mx or mx[k][1] < rt[1]:
                        mx[k] = rt
                t.r = list(mx.values())
        for tok in need.values():
            self._wait(e, tok)

    def _commit(self, tok, R, W):
        for t in W:
            t.w = tok
            t.r = []
        for t in R:
            t.r.append(tok)

    def op(self, en, fn, R=(), W=()):
        self.nops += 1
        if self.nops > self.limit:
            return None
        e = self.eng[en]
        W = list(W) + [t for t in R if t.excl]
        R = [t for t in R if not t.excl]
        self._deps(e, R, W)
        ins = fn(e["h"])
        e["cnt"] += 1
        ins.then_inc(e["sem"], 1)
        tok = (e["sem"], e["cnt"])
        self._commit(tok, R, W)
        return tok

    def dma(self, en, out, in_, R=(), W=(), is_output=False):
        self.nops += 1
        if self.nops > self.limit:
            return None
        e = self.eng[en]
        self._deps(e, R, W)
        slot = e["dsems"][e["dnext"] % len(e["dsems"])]
        e["dnext"] += 1
        if slot[1] > 0:
            self._wait(e, (slot[0], slot[1]))
        e["h"].dma_start(out=out, in_=in_).then_inc(slot[0], 16)
        slot[1] += 16
        tok = (slot[0], slot[1])
        self._commit(tok, R, W)
        if is_output:
            self.out_tokens.append(tok)
        return tok

    def barrier(self):
        toks = []
        for e in self.eng.values():
            if e["cnt"] > 0:
                toks.append((e["sem"], e["cnt"]))
            for s in e["dsems"]:
                if s[1] > 0:
                    toks.append((s[0], s[1]))
        for e in self.eng.values():
            for t in toks:
                self._wait(e, t)

    def finish(self):
        e = self.eng["sp"]
        for tok in self.out_tokens:
            self._wait(e, tok)
        for name in ("pe", "act", "dve", "pool"):
            o = self.eng[name]
            if o["cnt"] > 0:
                self._wait(e, (o["sem"], o["cnt"]))


def build_nc():
    nc = bass.Bass("TRN2", target_bir_lowering=False)
    S = Sched(nc)

    def din(name, shape):
        return nc.dram_tensor(name, list(shape), F32, kind="ExternalInput").ap()

    def dout(name, shape):
        return nc.dram_tensor(name, list(shape), F32, kind="ExternalOutput").ap()

    xcat = din("xcat", [NSLOT, 2048])
    wina = din("wina", [2048, 5120])
    waout = din("waout", [2048, 2048])
    cwin = din("cwin", [2048, 6144])
    cwout = din("cwout", [2048, 2048])
    cosd = din("cosd", [128, 80])
    sind = din("sind", [128, 80])
    kmAd = din("kmA", [128, 24])
    kmBd = din("kmB", [128, 24])
    hvd = din("hv", [128, 1])
    sinkd = din("sinkl", [128, 16])
    ckd = din("ck", [2, 128, 512])
    cvd = din("cv", [2, 128, 512])
    std = din("st", [2, 30, 2048])
    wdwd = din("wdw", [128, 16 * 31])
    bdwd = din("bdw", [128, 16])
    clgd = din("clg", [128, 16])
    clbd = din("clb", [128, 16])
    plgd = din("plg", [2, 128, 2048])
    plbd = din("plb", [2, 128, 2048])
    identd = din("ident", [128, 128])

    y_p = dout("y_p", [1024, 2048])
    y_s = dout("y_s", [2, 16, 2048])
    nk_p = dout("nk_p", [128, 512])
    nv_p = dout("nv_p", [128, 512])
    nk_s = dout("nk_s", [2, 128, 512])
    nv_s = dout("nv_s", [2, 128, 512])
    nc_p = dout("nc_p", [30, 2048])
    nc_s = dout("nc_s", [2, 30, 2048])
    x1s = nc.dram_tensor("x1s", [1152, 2048], F32, kind="Internal").ap()

    from contextlib import ExitStack
    es_all = ExitStack()

    def sb(name, shape, dt, stack=None):
        return (stack or es_all).enter_context(nc.sbuf_tensor(name, list(shape), dt))

    def ps(name, shape, dt, stack):
        return stack.enter_context(nc.psum_tensor(name, list(shape), dt))

    with es_all:
        B1 = sb("B1", [128, 16, NSLOT], BF)
        B2 = sb("B2", [128, 16, 1184], BF)
        WB = sb("WB", [128, 16, 1536], BF)
        identb = sb("identb", [128, 128], BF)
        ident32 = sb("ident32", [128, 128], F32)
        onesb = sb("onesb", [128, 64], BF)
        ones32 = sb("ones32", [128, 128], F32)
        cosT = sb("cosT", [128, 10, 8], F32)
        sinT = sb("sinT", [128, 10, 8], F32)
        kmA = sb("kmAs", [128, 24], F32)
        kmB = sb("kmBs", [128, 24], F32)
        hv = sb("hvs", [128, 1], F32)
        esk = sb("esk", [128, 16], F32)
        epst = sb("epst", [128, 1], F32)
        wdw = sb("wdws", [128, 16, 31], F32)
        bdw = sb("bdws", [128, 16], F32)
        clg = sb("clgs", [128, 16], F32)
        clb = sb("clbs", [128, 16], F32)

        tC = Tk()
        ctk = []

        def cload(dst, srcap):
            tk = Tk()
            ctk.append(tk)
            S.dma("sp", dst, srcap, W=[tk])
        cload(ident32[:], identd)
        cload(cosT[:].rearrange("p a b -> p (a b)"), cosd)
        cload(sinT[:].rearrange("p a b -> p (a b)"), sind)
        cload(kmA[:], kmAd)
        cload(kmB[:], kmBd)
        cload(hv[:], hvd)
        cload(esk[:], sinkd)
        cload(wdw[:].rearrange("p a b -> p (a b)"), wdwd)
        cload(bdw[:], bdwd)
        cload(clg[:], clgd)
        cload(clb[:], clbd)
        S.op("dve", lambda v: v.tensor_copy(out=identb[:], in_=ident32[:]), R=ctk, W=[tC])
        S.op("dve", lambda v: v.memset(onesb[:], 1.0), W=[tC])
        S.op("dve", lambda v: v.memset(ones32[:], 1.0), W=[tC])
        S.op("dve", lambda v: v.memset(epst[:], EPS), W=[tC])
        S.op("act", lambda a: a.activation(out=esk[:], in_=esk[:], func=AF.Exp), R=[tC], W=[tC])

        tB1 = [Tk() for _ in range(10)]
        tB2 = [Tk() for _ in range(10)]
        tWB = [Tk() for _ in range(4)]
        tKT = [Tk() for _ in range(10)]
        tV = [Tk() for _ in range(10)]
        tSG = Tk()

        def wslot_ap(WB4, s):
            if s < 3:
                return WB[:, :, s * 512:(s + 1) * 512]
            return WB4[:]

        def load_wblock(slot, src_cols_ap):
            S.dma("pool", slot_ap_cur[slot], src_cols_ap.rearrange("(k p) n -> p k n", p=128), W=[tWB[slot]])

        with ExitStack() as esAB:
            KT = sb("KT", [128, 4, NSLOT], BF, esAB)
            V = sb("V", [128, 10, 512], BF, esAB)
            SG = sb("SG", [128, 16, 1152], BF, esAB)
            with ExitStack() as esA:
                slot_ap_cur = [wslot_ap(None, s) for s in range(3)]
                t32 = [sb("t32_%d" % i, [128, 512], F32, esA) for i in range(2)]
                t16 = [sb("t16_%d" % i, [128, 512], BF, esA) for i in range(6)]
                tt32 = [Tk(), Tk()]
                tt16 = [Tk() for _ in range(6)]
                rtmp = sb("rtmp", [128, 4, 64], F32, esA)
                trt = Tk()
                psA = [ps("psA%d" % i, [128, 512], F32, esA) for i in range(3)]
                tpsA = [Tk(True), Tk(True), Tk(True)]
                psQ = [ps("psQ%d" % i, [128, 512], BF, esA) for i in range(3)]
                tpsQ = [Tk(True), Tk(True), Tk(True)]
                esX = ExitStack()
                xb16 = [sb("xb16_%d" % i, [128, 2048], BF, esX) for i in range(2)]
                txb = [Tk(), Tk()]
                psT1 = ps("psT0", [128, 2048], BF, esX)
                psT = [psT1, psT1]
                tpsT1 = Tk(True)
                tpsT = [tpsT1, tpsT1]

                wb_order = list(range(10))
                def wb_src(wb):
                    return wina[:, wb * 512:(wb + 1) * 512]
                xorder = [1, 2, 3, 4, 5, 6, 7, 8, 9, 0]
                sgflat = SG[:].rearrange("p a b -> p (a b)")
                txs = [Tk() for _ in range(10)]

                def xsrc_ap(n_):
                    if n_ < 9:
                        return sgflat[:, n_ * 2048:(n_ + 1) * 2048]
                    return xb16[0][:]
                for n_, i in enumerate(xorder):
                    S.dma("pool", xsrc_ap(n_), xcat[i * 128:(i + 1) * 128, :], W=[txs[n_]])
                    if n_ == 1:
                        load_wblock(0, wb_src(0))
                    if n_ == 6:
                        load_wblock(1, wb_src(1))
                    if n_ == 9:
                        load_wblock(2, wb_src(2))

                def emit_X(n_):
                    i = xorder[n_]
                    xa = xsrc_ap(n_)

                    def tr_x(t):
                        for kk in range(16):
                            ins = t.transpose(out=psT1[:, kk * 128:(kk + 1) * 128],
                                              in_=xa[:, kk * 128:(kk + 1) * 128], identity=identb[:])
                        return ins
                    S.op("pe", tr_x, R=[txs[n_], tC], W=[tpsT1])
                    if n_ % 2 == 0:
                        S.op("act", lambda a: a.activation(
                            out=B1[:, :, i * 128:(i + 1) * 128],
                            in_=psT1[:].rearrange("p (k n) -> p k n", k=16), func=AF.Copy),
                            R=[tpsT1], W=[tB1[i]])
                    else:
                        S.op("dve", lambda v: v.tensor_copy(
                            out=B1[:, :, i * 128:(i + 1) * 128],
                            in_=psT1[:].rearrange("p (k n) -> p k n", k=16)),
                            R=[tpsT1], W=[tB1[i]])

                emit_X(0)
                emit_X(1)
                xnext = [2]
                grp = 0
                pending = []
                qcnt = [0]

                def flush_q(batch):
                    def trq(t):
                        for n_, (i, wb, q16) in enumerate(batch):
                            for c in range(4):
                                ins = t.transpose(out=psQ[n_][:, c * 128:(c + 1) * 128],
                                                  in_=t16[q16][:, c * 128:(c + 1) * 128], identity=identb[:])
                        return ins
                    S.op("pe", trq, R=[tt16[q16] for (_, _, q16) in batch] + [tC],
                         W=[tpsQ[n_] for n_ in range(len(batch))])
                    for n_, (i, wb, q16) in enumerate(batch):
                        if wb < 4:
                            S.op("dve", lambda v, n_=n_, i=i, wb=wb: v.tensor_copy(
                                out=B2[:, wb * 4:(wb + 1) * 4, (i - 1) * 128:i * 128],
                                in_=psQ[n_][:].rearrange("p (c n) -> p c n", c=4)),
                                R=[tpsQ[n_]], W=[tB2[i]])
                        else:
                            S.op("dve", lambda v, n_=n_, i=i: v.tensor_copy(
                                out=KT[:, :, i * 128:(i + 1) * 128],
                                in_=psQ[n_][:].rearrange("p (c n) -> p c n", c=4)),
                                R=[tpsQ[n_]], W=[tKT[i]])

                for wb in range(10):
                    slot = wb % 3
                    Wt = slot_ap_cur[slot]
                    if wb < 6:
                        tiles = list(range(1, 10)) if wb < 4 else list(range(10))
                        for i in tiles:
                            if xnext[0] < 10:
                                emit_X(xnext[0])
                                xnext[0] += 1
                            pa = grp % 3
                            grp += 1

                            def mm(t, i=i, pa=pa, Wt=Wt):
                                for kk in range(16):
                                    ins = t.matmul(psA[pa][:], lhsT=B1[:, kk, i * 128:(i + 1) * 128],
                                                   rhs=Wt[:, kk, :], start=(kk == 0), stop=(kk == 15))
                                return ins
                            S.op("pe", mm, R=[tB1[i], tWB[slot]], W=[tpsA[pa]])
                            if wb == 5:
                                S.op("act", lambda a, i=i, pa=pa: a.activation(out=V[:, i, :], in_=psA[pa][:], func=AF.Copy),
                                     R=[tpsA[pa]], W=[tV[i]])
                                if i >= 8:
                                    b3 = i % 2
                                    S.op("dve", lambda v, pa=pa, b3=b3: v.tensor_copy(out=t32[b3][:], in_=psA[pa][:]),
                                         R=[tpsA[pa]], W=[tt32[b3]])
                                    if i == 8:
                                        S.dma("sp", nv_p[0:64, :], t32[b3][64:128, :], R=[tt32[b3]], is_output=True)
                                    else:
                                        S.dma("sp", nv_p[64:128, :], t32[b3][0:64, :], R=[tt32[b3]], is_output=True)
                                        for b in range(2):
                                            S.dma("sp", nv_s[b, 112:128, :], t32[b3][64 + 32 * b:80 + 32 * b, :],
                                                  R=[tt32[b3]], is_output=True)
                                continue
                            b3 = grp % 2
                            S.op("act", lambda a, pa=pa, b3=b3: a.activation(out=t32[b3][:], in_=psA[pa][:], func=AF.Copy),
                                 R=[tpsA[pa]], W=[tt32[b3]])
                            xv = t32[b3][:].rearrange("p (h d) -> p h d", d=64)
                            x1 = xv[:, :, 0:8]
                            x2 = xv[:, :, 8:16]
                            cb = cosT[:, i, :].unsqueeze(1).broadcast_to([128, 8, 8])
                            sbb = sinT[:, i, :].unsqueeze(1).broadcast_to([128, 8, 8])
                            rv = [rtmp[:, j, :].rearrange("p (h d) -> p h d", d=8) for j in range(4)]

                            def rope1(v, x1=x1, x2=x2, cb=cb, sbb=sbb, rv=rv):
                                v.tensor_tensor(out=rv[0], in0=x1, in1=cb, op=OP.mult)
                                v.tensor_tensor(out=rv[1], in0=x2, in1=sbb, op=OP.mult)
                                v.tensor_tensor(out=rv[2], in0=x2, in1=cb, op=OP.mult)
                                return v.tensor_tensor(out=rv[3], in0=x1, in1=sbb, op=OP.mult)
                            S.op("dve", rope1, R=[tt32[b3], tC], W=[trt])

                            def rope2(v, x1=x1, x2=x2, rv=rv):
                                v.tensor_tensor(out=x1, in0=rv[0], in1=rv[1], op=OP.subtract)
                                return v.tensor_tensor(out=x2, in0=rv[2], in1=rv[3], op=OP.add)
                            S.op("dve", rope2, R=[trt], W=[tt32[b3]])
                            q16 = qcnt[0] % 6
                            qcnt[0] += 1
                            S.op("pool", lambda g, b3=b3, q16=q16: g.tensor_copy(out=t16[q16][:], in_=t32[b3][:]),
                                 R=[tt32[b3]], W=[tt16[q16]])
                            if wb == 4 and i >= 8:
                                if i == 8:
                                    S.dma("sp", nk_p[0:64, :], t32[b3][64:128, :], R=[tt32[b3]], is_output=True)
                                else:
                                    S.dma("sp", nk_p[64:128, :], t32[b3][0:64, :], R=[tt32[b3]], is_output=True)
                                    for b in range(2):
                                        S.dma("sp", nk_s[b, 112:128, :], t32[b3][64 + 32 * b:80 + 32 * b, :],
                                              R=[tt32[b3]], is_output=True)

                            pending.append((i, wb, q16))
                            if len(pending) == 5:
                                flush_q(pending[0:3])
                                del pending[0:3]
                        while pending:
                            flush_q(pending[0:3])
                            del pending[0:3]
                    else:
                        if wb == 6:
                            for tk in txs:
                                tSG.r.extend(tk.r)
                                if tk.w is not None:
                                    tSG.r.append(tk.w)
                        for c in range(4):
                            gt = (wb - 6) * 4 + c
                            for (c0, n) in ((128, 512), (640, 512), (1152, 128)):
                                pa = grp % 3
                                grp += 1
                                rt = [tB1[j] for j in range(c0 // 128, (c0 + n) // 128)]

                                def mmg(t, c=c, c0=c0, n=n, pa=pa, Wt=Wt):
                                    for kk in range(16):
                                        ins = t.matmul(psA[pa][:, 0:n], lhsT=Wt[:, kk, c * 128:(c + 1) * 128],
                                                       rhs=B1[:, kk, c0:c0 + n], start=(kk == 0), stop=(kk == 15))
                                    return ins
                                S.op("pe", mmg, R=rt + [tWB[slot]], W=[tpsA[pa]])
                                S.op("act", lambda a, gt=gt, c0=c0, n=n, pa=pa: a.activation(
                                    out=SG[:, gt, c0 - 128:c0 - 128 + n], in_=psA[pa][:, 0:n], func=AF.Silu),
                                    R=[tpsA[pa]], W=[tSG])
                    if wb + 3 < 10:
                        load_wblock(slot, wb_src(wb + 3))
                esX.close()
            S.barrier()

            if DEBUG_PHASES >= 2:
                with ExitStack() as esB:
                    P1p = [sb("P1p_%d" % i, [128, 2, 512], BF, esB) for i in range(2)]
                    P2p = [sb("P2p_%d" % i, [128, 2, 512], BF, esB) for i in range(2)]
                    tP1 = [Tk(), Tk()]
                    tP2 = [Tk(), Tk()]
                    Dsb2 = [sb("Dsb%d" % i, [128, 512], F32, esB) for i in range(2)]
                    og2 = [sb("og%d" % i, [128, 256], F32, esB) for i in range(2)]
                    tDsb2, tog2 = [Tk(), Tk()], [Tk(), Tk()]
                    ck16 = [sb("ck16_%d" % i, [128, 512], BF, esB) for i in range(2)]
                    tck = [Tk(), Tk()]
                    KTc = sb("KTc", [128, 4, 2, 128], BF, esB)
                    Vc = sb("Vc", [128, 2, 512], BF, esB)
                    tKTc = [Tk(), Tk()]
                    tVc = [Tk(), Tk()]
                    pSall = ps("pS_all", [128, 2048], F32, esB)
                    pS = [pSall[:, 0:1024], pSall[:, 1024:2048]]
                    tSall = Tk(True)
                    pOD = [ps("pOD_%d" % i, [128, 512], F32, esB) for i in range(4)]
                    tS = [tSall, tSall]
                    tOD = [Tk(True) for _ in range(4)]

                    for s in range(3):
                        S.dma("pool", WB[:, :, s * 512:(s + 1) * 512],
                              waout[:, s * 512:(s + 1) * 512].rearrange("(k p) n -> p k n", p=128), W=[tWB[s]])

                    for b in range(2):
                        S.dma("pool", ck16[b][:], ckd[b], W=[tck[b]])
                        S.dma("pool", Vc[:, b, :], cvd[b], W=[tVc[b]])
                        S.dma("sp", nk_s[b, 0:112, :], ckd[b, 16:128, :], is_output=True)
                        S.dma("sp", nv_s[b, 0:112, :], cvd[b, 16:128, :], is_output=True)

                    def prep_caches():
                        for b in range(2):
                            pSb = pSall[:, 0:256].bitcast(BF)

                            def trc(t, pSb=pSb, b=b):
                                for c in range(4):
                                    ins = t.transpose(out=pSb[:, c * 128:(c + 1) * 128],
                                                      in_=ck16[b][:, c * 128:(c + 1) * 128], identity=identb[:])
                                return ins
                            S.op("pe", trc, R=[tck[b], tC], W=[tS[0]])
                            S.op("dve", lambda v, b=b, pSb=pSb: v.tensor_copy(
                                out=KTc[:, :, b, :], in_=pSb.rearrange("p (c n) -> p c n", c=4)),
                                R=[tS[0]], W=[tKTc[b]])

                    items = []
                    chunks = [("p", j) for j in range(2, 19)] + [("s", 0), ("s", 1)]
                    for (kind, j) in chunks:
                        if kind == "p":
                            nq = 64
                            qc0 = 64 * j - 128
                            qtile = j // 2
                            if j % 2 == 0:
                                i1, i2 = j // 2 - 1, j // 2
                            else:
                                i1, i2 = (j - 1) // 2, (j - 3) // 2
                            K1 = lambda jj, hs, i1=i1: KT[hs, jj, i1 * 128:(i1 + 1) * 128]
                            V1 = lambda cs, i1=i1: V[:, i1, cs]
                            K2 = lambda jj, hs, i2=i2: KT[hs, jj, i2 * 128:(i2 + 1) * 128]
                            V2 = lambda cs, i2=i2: V[:, i2, cs]
                            rdeps = [tKT[i1], tV[i1], tKT[i2], tV[i2], tB2[qtile], tC]
                            bcol = j
                        else:
                            b = j
                            nq = 16
                            qc0 = 1024 + 64 + 32 * b
                            qtile = 9
                            K1 = lambda jj, hs, b=b: KTc[hs, jj, b, :]
                            V1 = lambda cs, b=b: Vc[:, b, cs]
                            K2 = lambda jj, hs: KT[hs, jj, 1152:1280]
                            V2 = lambda cs: V[:, 9, cs]
                            rdeps = [tKTc[b], tVc[b], tKT[9], tV[9], tB2[9], tC]
                            bcol = 19 + b
                        pb = 0
                        for jj in range(4):
                            items.append(dict(nq=nq, qc0=qc0, qtile=qtile, pb=pb, K1=K1, V1=V1, K2=K2, V2=V2,
                                              rdeps=rdeps, bcol=bcol, jj=jj))

                    def emit_Spair(m):
                        pi = m % 2
                        ks = (2 * m, 2 * m + 1)
                        I0 = items[ks[0]]
                        nq, n4 = I0["nq"], 4 * I0["nq"]

                        def mmS(t):
                            for e_, k in enumerate(ks):
                                I = items[k]
                                jj = I["jj"]
                                for blk, Kf in ((0, I["K1"]), (1, I["K2"])):
                                    for h in range(2):
                                        hs = slice(h * 64, (h + 1) * 64)
                                        q = B2[hs, jj * 4:(jj + 1) * 4, I["qc0"]:I["qc0"] + nq]
                                        c0 = 1024 * e_ + 512 * h + 256 * blk
                                        ins = t.matmul(pSall[:, c0:c0 + n4].rearrange("p (g q) -> p g q", g=4),
                                                       lhsT=Kf(jj, hs), rhs=q, start=True, stop=True)
                            return ins
                        S.op("pe", mmS, R=I0["rdeps"], W=[tSall])
                        pv = pSall[:].rearrange("p (e h c) -> p e h c", e=2, h=2)
                        o1 = P1p[pi][:, :, 0:2 * n4].rearrange("p e (h c) -> p e h c", h=2)
                        o2 = P2p[pi][:, :, 0:2 * n4].rearrange("p e (h c) -> p e h c", h=2)
                        bc = I0["bcol"]
                        if bc >= 5:
                            S.op("act", lambda a: a.activation(out=o1, in_=pv[:, :, :, 0:n4], func=AF.Exp, scale=0.125),
                                 R=[tSall], W=[tP1[pi]])
                        else:
                            S.op("act", lambda a: a.activation(out=o1, in_=pv[:, :, :, 0:n4], func=AF.Exp,
                                                               bias=kmA[:, bc:bc + 1], scale=0.125),
                                 R=[tSall, tC], W=[tP1[pi]])
                        S.op("act", lambda a: a.activation(out=o2, in_=pv[:, :, :, 256:256 + n4], func=AF.Exp,
                                                           bias=kmB[:, bc:bc + 1], scale=0.125),
                             R=[tSall, tC], W=[tP2[pi]])

                    def emit_PV(k):
                        I = items[k]
                        bi = k % 2
                        b4 = k % 4
                        n4, jj = 4 * I["nq"], I["jj"]

                        def mmPV(t):
                            for h in range(2):
                                hs = slice(h * 64, (h + 1) * 64)
                                cs = slice(jj * 128 + h * 64, jj * 128 + (h + 1) * 64)
                                p1 = P1p[(k // 2) % 2][:, k % 2, h * n4:(h + 1) * n4]
                                p2 = P2p[(k // 2) % 2][:, k % 2, h * n4:(h + 1) * n4]
                                t.matmul(pOD[b4][hs, 0:n4], lhsT=I["V1"](cs), rhs=p1, start=True, stop=False)
                                t.matmul(pOD[b4][hs, 0:n4], lhsT=I["V2"](cs), rhs=p2, start=False, stop=True)
                                t.matmul(pOD[b4][hs, 256:256 + n4], lhsT=onesb[:, 0:64], rhs=p1, start=True, stop=False)
                                ins = t.matmul(pOD[b4][hs, 256:256 + n4], lhsT=onesb[:, 0:64], rhs=p2,
                                               start=False, stop=True)
                            return ins
                        S.op("pe", mmPV, R=I["rdeps"] + [tP1[(k // 2) % 2], tP2[(k // 2) % 2]], W=[tOD[b4]])

                    def emit_Npair(m):
                        pi = m % 2
                        Dsb, tDsb = Dsb2[pi], tDsb2[pi]
                        ks = (2 * m, 2 * m + 1)
                        n4 = 4 * items[ks[0]]["nq"]
                        for e_, k in enumerate(ks):
                            I = items[k]
                            nq, jj = I["nq"], I["jj"]
                            esb = esk[:, jj * 4:(jj + 1) * 4].unsqueeze(2).broadcast_to([128, 4, nq])
                            S.op("dve", lambda v, e_=e_, k=k, esb=esb: v.tensor_tensor(
                                out=Dsb[:, e_ * n4:(e_ + 1) * n4].rearrange("p (g q) -> p g q", g=4),
                                in0=pOD[k % 4][:, 256:256 + n4].rearrange("p (g q) -> p g q", g=4), in1=esb, op=OP.add),
                                R=[tOD[k % 4], tC], W=[tDsb])
                        S.op("act", lambda a: a.activation(out=Dsb[:, 0:2 * n4], in_=Dsb[:, 0:2 * n4], func=AF.Ln),
                             W=[tDsb])
                        S.op("act", lambda a: a.activation(out=Dsb[:, 0:2 * n4], in_=Dsb[:, 0:2 * n4], func=AF.Exp,
                                                           scale=-1.0), W=[tDsb])
                        for e_, k in enumerate(ks):
                            I = items[k]
                            nq, jj, qc0 = I["nq"], I["jj"], I["qc0"]
                            og, tog = og2[e_], tog2[e_]
                            S.op("dve", lambda v, e_=e_, k=k, og=og: v.tensor_tensor(
                                out=og[:, 0:n4], in0=pOD[k % 4][:, 0:n4], in1=Dsb[:, e_ * n4:(e_ + 1) * n4], op=OP.mult),
                                R=[tOD[k % 4], tDsb], W=[tog])
                            S.op("pool", lambda g, og=og, jj=jj, nq=nq, qc0=qc0: g.tensor_tensor(
                                out=B1[:, jj * 4:(jj + 1) * 4, 128 + qc0:128 + qc0 + nq],
                                in0=og[:, 0:n4].rearrange("p (g q) -> p g q", g=4),
                                in1=SG[:, jj * 4:(jj + 1) * 4, qc0:qc0 + nq], op=OP.mult),
                                R=[tog, tSG], W=[tB1[I["qtile"]]])

                    npair = [0]
                    NP = len(items) // 2
                    emit_Spair(0)
                    emit_Spair(1)
                    for m in range(NP):
                        if m == 20:
                            prep_caches()
                        emit_PV(2 * m)
                        emit_PV(2 * m + 1)
                        while npair[0] <= m - 1:
                            emit_Npair(npair[0])
                            npair[0] += 1
                        if m + 2 < NP:
                            emit_Spair(m + 2)
                    while npair[0] < NP:
                        emit_Npair(npair[0])
                        npair[0] += 1
        S.barrier()

        tX1T = Tk()
        if DEBUG_PHASES >= 3:
            with ExitStack() as esC:
                W4 = sb("W4", [128, 16, 512], BF, esC)
                S.dma("pool", W4[:], waout[:, 1536:2048].rearrange("(k p) n -> p k n", p=128), W=[tWB[3]])
                Wc = [WB[:, :, 0:512], WB[:, :, 512:1024], WB[:, :, 1024:1536], W4[:]]
                lnG = sb("lnG", [128, 2048], F32, esC)
                lnB = sb("lnB", [128, 2048], F32, esC)
                tLN = Tk()
                S.dma("sp", lnG[:], plgd[0], W=[tLN])
                S.dma("sp", lnB[:], plbd[0], W=[tLN])
                x32 = [sb("x32_%d" % i, [128, 2048], F32, esC) for i in range(2)]
                z32 = [sb("z32_%d" % i, [128, 2048], F32, esC) for i in range(2)]
                x1b = [sb("x1b%d" % i, [128, 2048], BF, esC) for i in range(2)]
                tx32 = [Tk(), Tk()]
                tz32 = [Tk(), Tk()]
                tx1b = [Tk(), Tk()]
                stt = sb("stt", [128, 4, 6], F32, esC)
                mv = sb("mv", [128, 2], F32, esC)
                sd = sb("sd", [128, 1], F32, esC)
                rstd = sb("rstd", [128, 1], F32, esC)
                tst = Tk()
                pC = [ps("pC%d" % i, [128, 512], F32, esC) for i in range(4)]
                tpC = [Tk(True) for _ in range(4)]
                pT = ps("pTc", [128, 2048], BF, esC)
                tpT = Tk(True)
                def xloadC(i, dst, tk):
                    S.dma("sp", dst[:], xcat[i * 128:(i + 1) * 128, :], W=[tk])

                def postC(i, bi, when):
                    if when == "early":
                        S.dma("sp", x1s[(i - 1) * 128:i * 128, :], z32[bi][:], R=[tz32[bi]], W=[tX1S[i]])
                        S.op("act", lambda a: a.activation(out=x1b[i % 2][:], in_=z32[bi][:], func=AF.Copy),
                             R=[tz32[bi]], W=[tx1b[i % 2]])
                        return

                    def trx(t):
                        for kk in range(16):
                            ins = t.transpose(out=pT[:, kk * 128:(kk + 1) * 128],
                                              in_=x1b[i % 2][:, kk * 128:(kk + 1) * 128], identity=identb[:])
                        return ins
                    S.op("pe", trx, R=[tx1b[i % 2], tC], W=[tpT])
                    pv = pT[:].rearrange("p (k n) -> p k n", k=16)
                    if i < 9:
                        S.op("act", lambda a: a.activation(out=B2[:, :, (i - 1) * 128:i * 128], in_=pv, func=AF.Copy),
                             R=[tpT], W=[tX1T])
                    else:
                        def ev9(a):
                            a.activation(out=B2[:, :, 1024:1088], in_=pv[:, :, 0:64], func=AF.Copy)
                            a.activation(out=B2[:, :, 1118:1134], in_=pv[:, :, 64:80], func=AF.Copy)
                            return a.activation(out=B2[:, :, 1164:1180], in_=pv[:, :, 96:112], func=AF.Copy)
                        S.op("act", ev9, R=[tpT], W=[tX1T])
                for tk in tB2:
                    tX1T.r.extend(tk.r)
                    if tk.w is not None:
                        tX1T.r.append(tk.w)
                S.op("dve", lambda v: v.memset(B2[:, :, 1088:1184], 0.0), W=[tX1T])
                layer_tail(nc, S, tiles=[(i, i * 128, 128) for i in range(1, 10)], actT=B1,
                           tact=lambda i: [tB1[i]], Wc=Wc, tW=tWB, xload=xloadC, x32=x32, tx32=tx32,
                           z32=z32, tz32=tz32, pC=pC, tpC=tpC, lnG=lnG, lnB=lnB, tLN=tLN, stt=stt, mv=mv,
                           sd=sd, rstd=rstd, tst=tst, epst=epst, tC=tC, post=postC)
        S.barrier()
        if DEBUG_PHASES >= 4:
            phase_D(nc, S, sb, ps, locals())
        S.finish()
    print('total ops', S.nops)
    return nc


def layer_tail(nc, S, tiles, actT, tact, Wc, tW, xload, x32, tx32, z32, tz32, pC, tpC, lnG, lnB, tLN,
               stt, mv, sd, rstd, tst, epst, tC, post):
    pending = None
    for n_, (i, c0, m) in enumerate(tiles):
        bi = n_ % 2
        xload(i, x32[bi], tx32[bi])
        for cg in range(4):
            def mm(t, cg=cg, c0=c0, m=m):
                for kk in range(16):
                    ins = t.matmul(pC[cg][0:m, :], lhsT=actT[:, kk, c0:c0 + m], rhs=Wc[cg][:, kk, :],
                                   start=(kk == 0), stop=(kk == 15))
                return ins
            S.op("pe", mm, R=tact(i) + [tW[cg]], W=[tpC[cg]])
            S.op("dve", lambda v, cg=cg, bi=bi, m=m: v.scalar_tensor_tensor(
                out=z32[bi][0:m, cg * 512:(cg + 1) * 512], in0=x32[bi][0:m, cg * 512:(cg + 1) * 512],
                scalar=ALPHA, in1=pC[cg][0:m, :], op0=OP.mult, op1=OP.add),
                R=[tpC[cg], tx32[bi]], W=[tz32[bi]])
        if pending is not None:
            pending()
            pending = None

        def stats(v, bi=bi, m=m):
            for cg in range(4):
                ins = v.bn_stats(out=stt[0:m, cg, :], in_=z32[bi][0:m, cg * 512:(cg + 1) * 512])
            return ins
        S.op("dve", stats, R=[tz32[bi]], W=[tst])
        S.op("dve", lambda v, m=m: v.bn_aggr(out=mv[0:m, :], in_=stt[0:m, :, :].rearrange("p a b -> p (a b)")),
             W=[tst])
        S.op("act", lambda a, m=m: a.activation(out=sd[0:m, :], in_=mv[0:m, 1:2], func=AF.Sqrt,
                                                bias=epst[0:m, :], scale=1.0), R=[tC], W=[tst])
        S.op("dve", lambda v, m=m: v.reciprocal(out=rstd[0:m, :], in_=sd[0:m, :]), W=[tst])
        S.op("dve", lambda v, bi=bi, m=m: v.tensor_scalar(
            out=z32[bi][0:m, :], in0=z32[bi][0:m, :], scalar1=mv[0:m, 0:1], scalar2=rstd[0:m, 0:1],
            op0=OP.subtract, op1=OP.mult), R=[tst], W=[tz32[bi]])
        S.op("pool", lambda g, bi=bi, m=m: g.tensor_tensor(out=z32[bi][0:m, :], in0=z32[bi][0:m, :],
                                                          in1=lnG[0:m, :], op=OP.mult), R=[tLN], W=[tz32[bi]])
        S.op("pool", lambda g, bi=bi, m=m: g.tensor_tensor(out=z32[bi][0:m, :], in0=z32[bi][0:m, :],
                                                          in1=lnB[0:m, :], op=OP.add), R=[tLN], W=[tz32[bi]])
        pending = (lambda i=i, bi=bi: (post(i, bi, "early"), post(i, bi, "late")))
    if pending is not None:
        pending()


tX1S = [Tk() for _ in range(11)]


def phase_D(nc, S, sb, ps, L):
    from contextlib import ExitStack
    B1, B2, WB = L["B1"], L["B2"], L["WB"]
    tB1, tWB, tX1T, tC = L["tB1"], L["tWB"], L["tX1T"], L["tC"]
    identb, ident32, ones32 = L["identb"], L["ident32"], L["ones32"]
    wdw, bdw, clg, clb, hv, epst = L["wdw"], L["bdw"], L["clg"], L["clb"], L["hv"], L["epst"]
    cwin, cwout, std, nc_p, nc_s, y_p, y_s, x1s = (L["cwin"], L["cwout"], L["std"], L["nc_p"], L["nc_s"],
                                                   L["y_p"], L["y_s"], L["x1s"])
    plgd, plbd = L["plgd"], L["plbd"]
    tSGc = [Tk() for _ in range(16)]
    tU = [Tk() for _ in range(16)]
    tCc = [Tk() for _ in range(16)]
    tWs = [Tk() for _ in range(4)]
    tUT = Tk()
    blocks = ((34, 512), (546, 512), (1058, 122))
    oblocks = ((64, 512), (576, 512), (1088, 92))
    NOUT = 1116

    def wsl(s):
        return WB[:, :, s * 384:(s + 1) * 384]

    def load_cw(ct):
        s = ct % 4
        S.dma("pool", wsl(s)[:, :, 0:256], cwin[:, ct * 384:ct * 384 + 256].rearrange("(k p) n -> p k n", p=128),
              W=[tWs[s]])

    with ExitStack() as esD:
        utail = sb("utail", [128, 16, 62], F32, esD)
        esU = ExitStack()
        U16 = sb("U16", [128, 16, L1W], BF, esU)
        with ExitStack() as esD1:
            for s in range(4):
                tWs[s].w = None
            for s in range(3):
                for ws in tWs:
                    ws.r.extend(tWB[s].r)
                    if tWB[s].w is not None:
                        ws.r.append(tWB[s].w)
            for ct in range(16):
                for tk in tB1:
                    tSGc[ct].r.extend(tk.r)
                    if tk.w is not None:
                        tSGc[ct].r.append(tk.w)
            for ct in range(4):
                load_cw(ct)
            stT = sb("stT", [128, 16, 2, 30], F32, esD1)
            st32 = [sb("st32_%d" % i, [30, 2048], F32, esD1) for i in range(2)]
            tst32, tstT = [Tk(), Tk()], Tk()
            for b in range(2):
                S.dma("sp", st32[b][:], std[b], W=[tst32[b]])
                S.dma("sp", nc_s[b, 0:14, :], std[b, 16:30, :], is_output=True)
            sig32 = [sb("sig32_%d" % i, [128, 512], F32, esD1) for i in range(2)]
            tsig = [Tk(), Tk()]
            pa = [ps("pa%d" % i, [128, 512], F32, esD1) for i in range(2)]
            pb_ = [ps("pb%d" % i, [128, 512], F32, esD1) for i in range(2)]
            tpa, tpb = [Tk(True), Tk(True)], [Tk(True), Tk(True)]
            pst = ps("pst", [128, 512], F32, esD1)
            tpst = Tk(True)
            def prep_state():
                for b in range(2):
                    def trs(t, b=b):
                        for ct in range(16):
                            ins = t.transpose(out=pst[:, ct * 30:(ct + 1) * 30],
                                              in_=st32[b][:, ct * 128:(ct + 1) * 128], identity=ident32[0:30, 0:30])
                        return ins
                    S.op("pe", trs, R=[tst32[b], tC], W=[tpst])
                    S.op("dve", lambda v, b=b: v.tensor_copy(out=stT[:, :, b, :],
                                                             in_=pst[:, 0:480].rearrange("p (c n) -> p c n", c=16)),
                         R=[tpst], W=[tstT])
            it = 0
            for ct in range(16):
                s = ct % 4
                Wt = wsl(s)
                for (c0, n) in blocks:
                    bi = it % 2
                    it += 1
                    for part, (pp, tp) in enumerate(((pa, tpa), (pb_, tpb))):
                        def mm(t, part=part, pp=pp, bi=bi, c0=c0, n=n, Wt=Wt):
                            for kk in range(16):
                                ins = t.matmul(pp[bi][:, 0:n], lhsT=Wt[:, kk, part * 128:(part + 1) * 128],
                                               rhs=B2[:, kk, c0:c0 + n], start=(kk == 0), stop=(kk == 15))
                            return ins
                        S.op("pe", mm, R=[tX1T, tWs[s]], W=[tp[bi]])
                    S.op("act", lambda a, bi=bi, n=n: a.activation(out=sig32[bi][:, 0:n], in_=pb_[bi][:, 0:n],
                                                                   func=AF.Sigmoid), R=[tpb[bi]], W=[tsig[bi]])
                    S.op("dve", lambda v, bi=bi, n=n, c0=c0, ct=ct: v.tensor_tensor(
                        out=U16[:, ct, c0:c0 + n], in0=pa[bi][:, 0:n], in1=sig32[bi][:, 0:n], op=OP.mult),
                        R=[tpa[bi], tsig[bi]], W=[tU[ct]])
                    if c0 == 1058:
                        def tails(v, bi=bi, ct=ct):
                            v.tensor_tensor(out=utail[:, ct, 0:30], in0=pa[bi][:, 0:30], in1=sig32[bi][:, 0:30], op=OP.mult)
                            v.tensor_tensor(out=utail[:, ct, 30:46], in0=pa[bi][:, 60:76], in1=sig32[bi][:, 60:76], op=OP.mult)
                            return v.tensor_tensor(out=utail[:, ct, 46:62], in0=pa[bi][:, 106:122],
                                                   in1=sig32[bi][:, 106:122], op=OP.mult)
                        S.op("dve", tails, R=[tpa[bi], tsig[bi]], W=[tUT])
                if ct == 0:
                    prep_state()
                def fix(g, ct=ct):
                    g.tensor_copy(out=U16[:, ct, 1088:1118], in_=stT[:, ct, 0, :])
                    g.tensor_copy(out=U16[:, ct, 1134:1164], in_=stT[:, ct, 1, :])
                    return g.tensor_scalar(out=U16[:, ct, 34:64], in0=U16[:, ct, 34:64], scalar1=hv[:, 0:1],
                                           scalar2=1.0, op0=OP.mult, op1=OP.mult)
                S.op("pool", fix, R=[tstT, tC], W=[tU[ct]])
                if ct + 4 < 16:
                    load_cw(ct + 4)
        S.barrier()
        with ExitStack() as esD2:
            tWo = [Tk() for _ in range(4)]
            for s in range(3):
                for ws in tWs:
                    tWo[s].r.extend(ws.r)
                S.dma("pool", WB[:, :, s * 512:(s + 1) * 512],
                      cwout[:, s * 512:(s + 1) * 512].rearrange("(k p) n -> p k n", p=128), W=[tWo[s]])
            S1 = sb("S1", [128, NOUT], F32, esD2)
            S2 = sb("S2", [128, NOUT], F32, esD2)
            tS1, tS2 = Tk(), Tk()
            esD2i = ExitStack()
            diag = [sb("diag%d" % i, [128, 31, 128], BF, esD2i) for i in range(2)]
            tdg = [Tk(), Tk()]
            sq = [sb("sq%d" % i, [128, 512], F32, esD2i) for i in range(2)]
            tsq = [Tk(), Tk()]
            S.op("dve", lambda v: v.memset(S1[:], 0.0), W=[tS1])
            S.op("dve", lambda v: v.memset(S2[:], 0.0), W=[tS2])
            with ExitStack() as esD2p:
                pc = [ps("pc%d" % i, [128, 512], F32, esD2p) for i in range(6)]
                tpc = [Tk(True) for _ in range(6)]
                it = 0
                for ct in range(16):
                    d = ct % 2

                    def mkdiag(g, ct=ct, d=d):
                        for tap in range(31):
                            ins = g.tensor_scalar(out=diag[d][:, tap, :], in0=identb[:],
                                                  scalar1=wdw[:, ct, tap:tap + 1], scalar2=1.0, op0=OP.mult,
                                                  op1=OP.mult)
                        return ins
                    S.op("dve" if ct == 0 else "pool", mkdiag, R=[tC], W=[tdg[d]])
                    for (o0, n) in reversed(oblocks):
                        bi = it % 6
                        sqi = it % 2
                        it += 1

                        def mmc(t, ct=ct, d=d, o0=o0, n=n, bi=bi):
                            for tap in range(31):
                                ins = t.matmul(pc[bi][:, 0:n], lhsT=diag[d][:, tap, :],
                                               rhs=U16[:, ct, o0 + tap - 30:o0 + tap - 30 + n],
                                               start=(tap == 0), stop=(tap == 30))
                            return ins
                        S.op("pe", mmc, R=[tdg[d], tU[ct]], W=[tpc[bi]])
                        S.op("act", lambda a, ct=ct, o0=o0, n=n, bi=bi: a.activation(
                            out=U16[:, ct, o0:o0 + n], in_=pc[bi][:, 0:n], func=AF.Identity,
                            bias=bdw[:, ct:ct + 1], scale=1.0), R=[tpc[bi], tC], W=[tCc[ct]])
                        S.op("act", lambda a, ct=ct, n=n, bi=bi, sqi=sqi: a.activation(
                            out=sq[sqi][:, 0:n], in_=pc[bi][:, 0:n], func=AF.Square,
                            bias=bdw[:, ct:ct + 1], scale=1.0), R=[tpc[bi], tC], W=[tsq[sqi]])
                        S.op("dve", lambda v, ct=ct, o0=o0, n=n, bi=bi: v.scalar_tensor_tensor(
                            out=S1[:, o0 - 64:o0 - 64 + n], in0=pc[bi][:, 0:n], scalar=bdw[:, ct:ct + 1],
                            in1=S1[:, o0 - 64:o0 - 64 + n], op0=OP.add, op1=OP.add), R=[tpc[bi], tC], W=[tS1])
                        S.op("dve", lambda v, o0=o0, n=n, sqi=sqi: v.tensor_tensor(
                            out=S2[:, o0 - 64:o0 - 64 + n], in0=S2[:, o0 - 64:o0 - 64 + n], in1=sq[sqi][:, 0:n],
                            op=OP.add), R=[tsq[sqi]], W=[tS2])
            esD2i.close()
            S.barrier()
            with ExitStack() as esD3:
                tt = [sb("ttm%d" % i, [128, NOUT], F32, esD3) for i in range(2)]
                ttt = [Tk(), Tk()]
                pm = [ps("pm%d" % i, [128, 512], F32, esD3) for i in range(6)]
                tpm = Tk(True)
                GW = [sb("GW%d" % i, [128, 16, 128], BF, esD3) for i in range(2)]
                tGW = [Tk(), Tk()]
                pg = [ps("pg%d" % i, [128, 512], F32, esD3) for i in range(2)]
                tpg = [Tk(True), Tk(True)]

                def load_gw(ct):
                    S.dma("pool", GW[ct % 2][:],
                          cwin[:, ct * 384 + 256:ct * 384 + 384].rearrange("(k p) n -> p k n", p=128), W=[tGW[ct % 2]])

                gcnt = [0]

                def gate_proj(ct):
                    for (c0, n) in blocks:
                        bi = gcnt[0] % 2
                        gcnt[0] += 1

                        def mmg(t, bi=bi, c0=c0, n=n, ct=ct):
                            for kk in range(16):
                                ins = t.matmul(pg[bi][:, 0:n], lhsT=GW[ct % 2][:, kk, :], rhs=B2[:, kk, c0:c0 + n],
                                               start=(kk == 0), stop=(kk == 15))
                            return ins
                        S.op("pe", mmg, R=[tX1T, tGW[ct % 2]], W=[tpg[bi]])
                        S.op("act", lambda a, bi=bi, n=n, c0=c0, ct=ct: a.activation(
                            out=B1[:, ct, c0:c0 + n], in_=pg[bi][:, 0:n], func=AF.Silu),
                            R=[tpg[bi]], W=[tSGc[ct]])
                    if ct + 2 < 16:
                        load_gw(ct + 2)
                load_gw(0)
                load_gw(1)
                for ct in range(4):
                    gate_proj(ct)

                def mms(t):
                    for k_, (o0, n) in enumerate(oblocks):
                        t.matmul(pm[k_][:, 0:n], lhsT=ones32[:], rhs=S1[:, o0 - 64:o0 - 64 + n], start=True, stop=True)
                        ins = t.matmul(pm[3 + k_][:, 0:n], lhsT=ones32[:], rhs=S2[:, o0 - 64:o0 - 64 + n],
                                       start=True, stop=True)
                    return ins
                S.op("pe", mms, R=[tS1, tS2, tC], W=[tpm])

                def st1(v):
                    for k_, (o0, n) in enumerate(oblocks):
                        sl = slice(o0 - 64, o0 - 64 + n)
                        v.tensor_scalar(out=S1[:, sl], in0=pm[k_][:, 0:n], scalar1=1.0 / 2048, scalar2=None, op0=OP.mult)
                        ins = v.tensor_scalar(out=S2[:, sl], in0=pm[3 + k_][:, 0:n], scalar1=1.0 / 2048, scalar2=None,
                                              op0=OP.mult)
                    return ins
                S.op("dve", st1, R=[tpm], W=[tS1, tS2])
                S.op("dve", lambda v: v.tensor_tensor(out=tt[0][:], in0=S1[:], in1=S1[:], op=OP.mult),
                     R=[tS1], W=[ttt[0]])
                S.op("dve", lambda v: v.tensor_tensor(out=S2[:], in0=S2[:], in1=tt[0][:], op=OP.subtract),
                     R=[ttt[0]], W=[tS2])
                S.op("act", lambda a: a.activation(out=S2[:], in_=S2[:], func=AF.Ln, bias=epst[:, 0:1], scale=1.0),
                     R=[tC], W=[tS2])
                S.op("act", lambda a: a.activation(out=tt[0][:], in_=S2[:], func=AF.Exp, scale=-0.5),
                     R=[tS2], W=[ttt[0]])
                S.op("dve", lambda v: v.tensor_tensor(out=S2[:], in0=S1[:], in1=tt[0][:], op=OP.mult),
                     R=[tS1, ttt[0]], W=[tS2])
                tpr = Tk()

                def wr_ps(v):
                    for k_, (o0, n) in enumerate(oblocks):
                        sl = slice(o0 - 64, o0 - 64 + n)
                        v.tensor_copy(out=pm[k_][:, 0:n], in_=tt[0][:, sl])
                        ins = v.tensor_copy(out=pm[3 + k_][:, 0:n], in_=S2[:, sl])
                    return ins
                S.op("dve", wr_ps, R=[ttt[0], tS2], W=[tpm, tpr])
                def apply_ct(ct):
                    bi = ct % 2

                    def a1(v, ct=ct, bi=bi):
                        for k_, (o0, n) in enumerate(oblocks):
                            sl = slice(o0 - 64, o0 - 64 + n)
                            ins = v.tensor_tensor(out=tt[bi][:, sl], in0=U16[:, ct, o0:o0 + n], in1=pm[k_][:, 0:n],
                                                  op=OP.mult)
                        return ins
                    S.op("dve", a1, R=[tCc[ct], tpr], W=[ttt[bi]])

                    def a2(v, bi=bi):
                        for k_, (o0, n) in enumerate(oblocks):
                            sl = slice(o0 - 64, o0 - 64 + n)
                            ins = v.tensor_tensor(out=tt[bi][:, sl], in0=tt[bi][:, sl], in1=pm[3 + k_][:, 0:n],
                                                  op=OP.subtract)
                        return ins
                    S.op("dve", a2, R=[tpr], W=[ttt[bi]])
                    S.op("act", lambda a, ct=ct, bi=bi: a.activation(
                        out=tt[bi][:], in_=tt[bi][:], func=AF.Silu, bias=clb[:, ct:ct + 1], scale=clg[:, ct:ct + 1]),
                        R=[tC], W=[ttt[bi]])
                    S.op("pool", lambda g, ct=ct, bi=bi: g.tensor_tensor(
                        out=B1[:, ct, 64:64 + NOUT], in0=tt[bi][:], in1=B1[:, ct, 64:64 + NOUT], op=OP.mult),
                        R=[ttt[bi]], W=[tSGc[ct]])

                next_g = 4
                for ct in range(16):
                    apply_ct(ct)
                    if next_g < 16 and (next_g - ct <= 1 or ct % 2 == 0):
                        gate_proj(next_g)
                        next_g += 1
        esU.close()
        S.barrier()
        with ExitStack() as esE:
            W4 = sb("W4e", [128, 16, 512], BF, esE)
            S.dma("pool", W4[:], cwout[:, 1536:2048].rearrange("(k p) n -> p k n", p=128), W=[tWo[3]])
            Wc = [WB[:, :, 0:512], WB[:, :, 512:1024], WB[:, :, 1024:1536], W4[:]]
            lnG = sb("lnGe", [128, 2048], F32, esE)
            lnB = sb("lnBe", [128, 2048], F32, esE)
            tLN = Tk()
            S.dma("sp", lnG[:], plgd[1], W=[tLN])
            S.dma("sp", lnB[:], plbd[1], W=[tLN])
            x32 = [sb("x32e_%d" % i, [128, 2048], F32, esE) for i in range(2)]
            z32 = [sb("z32e_%d" % i, [128, 2048], F32, esE) for i in range(2)]
            tx32 = [Tk(), Tk()]
            tz32 = [Tk(), Tk()]
            stt = sb("stte", [128, 4, 6], F32, esE)
            mv = sb("mve", [128, 2], F32, esE)
            sd = sb("sde", [128, 1], F32, esE)
            rstd = sb("rstde", [128, 1], F32, esE)
            tst = Tk()
            pC = [ps("pCe%d" % i, [128, 512], F32, esE) for i in range(4)]
            tpC = [Tk(True) for _ in range(4)]
            pU = ps("pU", [32, 2048], F32, esE)
            tpU = Tk(True)
            ut_sb = sb("ut_sb", [32, 2048], F32, esE)
            tut = Tk()
            for (c0, n, dst) in ((0, 30, nc_p[:, :]), (30, 16, nc_s[0, 14:30, :]), (46, 16, nc_s[1, 14:30, :])):
                def tru(t, c0=c0, n=n):
                    for ct in range(16):
                        ins = t.transpose(out=pU[0:n, ct * 128:(ct + 1) * 128], in_=utail[:, ct, c0:c0 + n],
                                          identity=ident32[:])
                    return ins
                S.op("pe", tru, R=[tUT, tC], W=[tpU])
                S.op("dve", lambda v, n=n: v.tensor_copy(out=ut_sb[0:n, :], in_=pU[0:n, :]), R=[tpU], W=[tut])
                S.dma("sp", dst, ut_sb[0:n, :], R=[tut], is_output=True)

            tiles = [(m, 64 + 128 * m, 128) for m in range(8)] + [(8, 1118, 62)]

            def xloadE(m, dst, tk):
                if m < 8:
                    S.dma("sp", dst[:], x1s[128 * m + 64:128 * m + 192, :], W=[tk])
                else:
                    S.dma("sp", dst[0:16, :], x1s[1088:1104, :], W=[tk])
                    S.dma("sp", dst[46:62, :], x1s[1120:1136, :], W=[tk])

            def postE(m, bi, when):
                if when == "late":
                    return
                if m < 8:
                    S.dma("sp", y_p[128 * m:128 * (m + 1), :], z32[bi][:], R=[tz32[bi]], is_output=True)
                else:
                    S.dma("sp", y_s[0], z32[bi][0:16, :], R=[tz32[bi]], is_output=True)
                    S.dma("sp", y_s[1], z32[bi][46:62, :], R=[tz32[bi]], is_output=True)
            for tk in tX1S:
                if tk.w is not None:
                    for t_ in tx32:
                        t_.r.append(tk.w)
            layer_tail(nc, S, tiles=tiles, actT=B1, tact=lambda m: list(tSGc), Wc=Wc, tW=tWo, xload=xloadE,
                       x32=x32, tx32=tx32, z32=z32, tz32=tz32, pC=pC, tpC=tpC, lnG=lnG, lnB=lnB, tLN=tLN,
                       stt=stt, mv=mv, sd=sd, rstd=rstd, tst=tst, epst=epst, tC=tC, post=postE)


class _AllTk:
    def __init__(self, tks):
        self.tks = tks

    @property
    def w(self):
        return None

    @property
    def r(self):
        return _Sink()


class _Sink:
    def append(self, x):
        pass


def _perm_q():
    perm = np.zeros(2048, dtype=np.int64)
    for j in range(4):
        for g in range(4):
            for half in range(2):
                for d in range(64):
                    perm[(j * 4 + g) * 128 + half * 64 + d] = ((2 * j + half) * 4 + g) * 64 + d
    return perm


_NC_CACHE = {}


def kernel(x_prompt, x_sample, cache_k, cache_v, state_conv, attn_w_in, attn_sink, attn_w_out,
           conv_w_in, conv_w_dw, conv_b_dw, conv_ln_g, conv_ln_b, conv_w_out, post_ln_g, post_ln_b):
    f = np.float32
    x_prompt = np.asarray(x_prompt, f)
    x_sample = np.asarray(x_sample, f)
    cache_k = np.asarray(cache_k, f)
    cache_v = np.asarray(cache_v, f)
    state_conv = np.asarray(state_conv, f)
    perm = _perm_q()
    w_in = np.asarray(attn_w_in, f)[0]
    wina = np.ascontiguousarray(np.concatenate(
        [w_in[:, perm], w_in[:, 2048:3072], w_in[:, 3072 + perm]], axis=1))
    waout = np.ascontiguousarray(np.asarray(attn_w_out, f)[0][perm, :])
    cw = np.asarray(conv_w_in, f)[0]
    cwin = np.ascontiguousarray(cw.reshape(2048, 3, 16, 128).transpose(0, 2, 1, 3).reshape(2048, 6144))
    cwout = np.ascontiguousarray(np.asarray(conv_w_out, f)[0])
    sink = np.asarray(attn_sink, f)[0]
    sinkl = np.zeros((128, 16), f)
    for p in range(128):
        for j in range(4):
            for g in range(4):
                sinkl[p, j * 4 + g] = sink[(2 * j + p // 64) * 4 + g]
    wdw = np.ascontiguousarray(np.asarray(conv_w_dw, f)[0].reshape(31, 16, 128).transpose(2, 1, 0).reshape(128, 16 * 31))
    lay = lambda v: np.ascontiguousarray(np.asarray(v, f)[0].reshape(16, 128).T)
    bdw, clg, clb = lay(conv_b_dw), lay(conv_ln_g), lay(conv_ln_b)
    plg = np.ascontiguousarray(np.broadcast_to(np.asarray(post_ln_g, f)[:, None, :], (2, 128, 2048)))
    plb = np.ascontiguousarray(np.broadcast_to(np.asarray(post_ln_b, f)[:, None, :], (2, 128, 2048)))
    ident = np.eye(128, dtype=f)
    half = 8
    inv = (500000.0 ** (-np.arange(half, dtype=np.float64) * (2.0 / 16))).astype(np.float32)

    in_maps = []
    for c in range(NCORES):
        s = 1024 * c
        xcat = np.zeros((NSLOT, 2048), f)
        pos = np.zeros(NSLOT, np.float64)
        lo = s - 192
        for slot in range(1216):
            tok = lo + slot
            if tok >= 0:
                pos[slot] = tok
        a0 = max(lo, 0)
        xcat[a0 - lo:1216, :] = x_prompt[0, a0:s + 1024, :]
        for b in range(2):
            r0 = 1216 + 32 * b
            xcat[r0:r0 + 16, :] = x_sample[2 * c + b]
            pos[r0:r0 + 16] = 1024 + np.arange(16)
        ang = pos.astype(np.float32)[:, None] * inv[None, :]
        cosd = np.cos(ang).astype(f).reshape(10, 128, 8).transpose(1, 0, 2).reshape(128, 80)
        sind = np.sin(ang).astype(f).reshape(10, 128, 8).transpose(1, 0, 2).reshape(128, 80)
        kmA = np.zeros((128, 24), f)
        kmB = np.zeros((128, 24), f)
        cvalid = lambda j: (lo + 64 * j) >= 0
        for j in range(2, 19):
            if j % 2 == 0:
                kmA[0:64, j] = 0.0 if cvalid(j - 2) else NEG
                kmA[64:128, j] = 0.0 if cvalid(j - 1) else NEG
                kmB[0:64, j] = 0.0 if cvalid(j) else NEG
                kmB[64:128, j] = NEG
            else:
                kmA[0:64, j] = 0.0 if cvalid(j - 1) else NEG
                kmA[64:128, j] = 0.0 if cvalid(j) else NEG
                kmB[0:64, j] = NEG
                kmB[64:128, j] = 0.0 if cvalid(j - 2) else NEG
        for b in range(2):
            kmB[:, 19 + b] = NEG
            kmB[64 + 32 * b:80 + 32 * b, 19 + b] = 0.0
        hv = np.full((128, 1), 1.0 if c > 0 else 0.0, f)
        in_maps.append({
            "xcat": xcat, "wina": wina, "waout": waout, "cwin": cwin, "cwout": cwout,
            "cosd": np.ascontiguousarray(cosd), "sind": np.ascontiguousarray(sind), "kmA": kmA, "kmB": kmB,
            "hv": hv, "sinkl": sinkl,
            "ck": np.ascontiguousarray(cache_k[0, 2 * c:2 * c + 2].reshape(2, 128, 512)),
            "cv": np.ascontiguousarray(cache_v[0, 2 * c:2 * c + 2].reshape(2, 128, 512)),
            "st": np.ascontiguousarray(state_conv[0, 2 * c:2 * c + 2]),
            "wdw": wdw, "bdw": bdw, "clg": clg, "clb": clb, "plg": plg, "plb": plb, "ident": ident,
        })
    if "nc" not in _NC_CACHE:
        _NC_CACHE["nc"] = build_nc()
    nc = _NC_CACHE["nc"]
    res = run_bass_kernel_spmd(nc, in_maps, core_ids=list(range(NCORES)))
    R = res.results
    y_prompt = np.concatenate([R[c]["y_p"] for c in range(NCORES)], axis=0)[None]
    y_sample = np.concatenate([R[c]["y_s"] for c in range(NCORES)], axis=0)
    nkp = R[7]["nk_p"].reshape(1, 1, 128, 8, 64)
    nvp = R[7]["nv_p"].reshape(1, 1, 128, 8, 64)
    nks = np.concatenate([R[c]["nk_s"] for c in range(NCORES)], axis=0).reshape(1, 16, 128, 8, 64)
    nvs = np.concatenate([R[c]["nv_s"] for c in range(NCORES)], axis=0).reshape(1, 16, 128, 8, 64)
    ncp = R[7]["nc_p"].reshape(1, 1, 30, 2048)
    ncs = np.concatenate([R[c]["nc_s"] for c in range(NCORES)], axis=0).reshape(1, 16, 30, 2048)
    return (y_prompt.astype(f), y_sample.astype(f), nkp.astype(f), nvp.astype(f), nks.astype(f),
            nvs.astype(f), ncp.astype(f), ncs.astype(f))
```
